# Optimizing a Trainium2 kernel written in Bass

```python
import jax, jax.numpy as jnp
from jax import lax
import numpy as np

D_MODEL = 1024
BATCH = 32
SEQ = 256
DEPTH = 2
DEC_BATCH = 2
DEC_SEQ = 1024
PAST_LEN = 256

GRID_W = 64
RET_HEADS = 8
RET_DK = 64
RET_DV = 64
RET_QK = RET_HEADS * RET_DK
RET_WIDTH = RET_HEADS * RET_DV
RET_CHUNK = 128
RET_DECAY_OFFSET = 5.0
MLA_HEADS = 8
MLA_NOPE = 64
MLA_ROPE = 32
MLA_V = 64
MLA_WIDTH = MLA_HEADS * MLA_V
Q_RANK = 256
KV_RANK = 128
D_MIX = RET_WIDTH + MLA_WIDTH
IN_COLS = 2 * RET_QK + 2 * RET_WIDTH + Q_RANK + KV_RANK + MLA_ROPE
D_FF = ((8 * D_MODEL // 3 + 255) // 256) * 256
N_MOD = 6
ROPE_BASE = 10000.0
ATTN_BLOCK = 128
EPS = 1e-6

kernel_name = 'hybrid_retention_mla_diffusion_step'


def rms_norm(x, g):
    xf = x.astype(jnp.float32)
    y = xf * lax.rsqrt(jnp.mean(xf * xf, axis=-1, keepdims=True) + EPS)
    return (y * g.astype(jnp.float32)).astype(x.dtype)


def heads(t, n_heads):
    b, n, _ = t.shape
    return t.reshape(b, n, n_heads, -1).transpose(0, 2, 1, 3)


def axial_rope(x, row, col):
    d = x.shape[-1]
    half = d // 2
    nf = half // 2
    inv = ROPE_BASE ** (-jnp.arange(nf, dtype=jnp.float32) / nf)

    def rot(xa, pos):
        ang = pos.astype(jnp.float32)[:, None] * inv
        cos, sin = jnp.cos(ang), jnp.sin(ang)
        x1, x2 = xa[..., :nf], xa[..., nf:]
        return jnp.concatenate([x1 * cos - x2 * sin, x1 * sin + x2 * cos], axis=-1)

    xf = x.astype(jnp.float32)
    out = jnp.concatenate([rot(xf[..., :half], row), rot(xf[..., half:], col)], axis=-1)
    return out.astype(x.dtype)


def modulation(cond, w_mod, b_mod):
    m = jax.nn.silu(cond) @ w_mod + b_mod
    return jnp.split(m[..., None, :], N_MOD, axis=-1)


def project(h, w_in, g_q_a, w_q_b, g_kv_a, g_qn, g_qr, g_kr):
    z = h @ w_in
    i1 = RET_QK
    i2 = i1 + RET_QK
    i3 = i2 + RET_WIDTH
    i4 = i3 + RET_WIDTH
    i5 = i4 + Q_RANK
    i6 = i5 + KV_RANK
    rq, rk, rv, rg, qa, kva, kr = jnp.split(z, [i1, i2, i3, i4, i5, i6], axis=-1)
    q = heads(rms_norm(qa, g_q_a) @ w_q_b, MLA_HEADS)
    qn = rms_norm(q[..., :MLA_NOPE], g_qn)
    qr = rms_norm(q[..., MLA_NOPE:], g_qr)
    ckv = rms_norm(kva, g_kv_a)
    kr = rms_norm(kr, g_kr)
    return (heads(rq, RET_HEADS), heads(rk, RET_HEADS) * (RET_DK ** -0.5), heads(rv, RET_HEADS),
            rg, qn, qr, ckv, kr)


def mla_up(ckv, w_kv_b, g_kn):
    kv = heads(ckv @ w_kv_b, MLA_HEADS)
    return rms_norm(kv[..., :MLA_NOPE], g_kn), kv[..., MLA_NOPE:]


def log_gamma(p):
    return jnp.log1p(-jnp.exp2(-p.astype(jnp.float32)))


def retention_chunkwise(q, k, v, lg, s0):
    b, h, n, _ = q.shape
    c = RET_CHUNK
    nc = n // c
    j = jnp.arange(c, dtype=jnp.float32)
    diff = j[:, None] - j[None, :]
    decay_intra = jnp.where(diff >= 0, jnp.exp(jnp.maximum(diff, 0.0)[None] * lg[:, None, None]), 0.0)
    q_decay = jnp.exp((j + 1.0)[None, :] * lg[:, None])[..., None]
    k_decay = jnp.exp((c - 1.0 - j)[None, :] * lg[:, None])[..., None]
    chunk_decay = jnp.exp(c * lg)[:, None, None]

    def to_chunks(t):
        return jnp.moveaxis(t.astype(jnp.float32).reshape(b, h, nc, c, t.shape[-1]), 2, 0)

    def step(s, inp):
        qi, ki, vi = inp
        scores = jnp.einsum('bhid,bhjd->bhij', qi, ki) * decay_intra
        inner = jnp.einsum('bhij,bhjv->bhiv', scores, vi)
        cross = jnp.einsum('bhid,bhdv->bhiv', qi, s) * q_decay
        s_new = s * chunk_decay + jnp.einsum('bhjd,bhjv->bhdv', ki * k_decay, vi)
        return s_new, inner + cross

    s_fin, out = lax.scan(step, s0.astype(jnp.float32), (to_chunks(q), to_chunks(k), to_chunks(v)))
    return jnp.moveaxis(out, 0, 2).reshape(b, h, n, -1), s_fin


def bidir_retention(q, k, v, p_fwd, p_bwd, s_f0, s_b0):
    o_f, s_f = retention_chunkwise(q, k, v, log_gamma(p_fwd), s_f0)
    flip = lambda t: jnp.flip(t, axis=2)
    o_b, s_b = retention_chunkwise(flip(q), flip(k), flip(v), log_gamma(p_bwd), s_b0)
    return o_f + flip(o_b), s_f, s_b


def head_group_norm(o, g, bias):
    mu = jnp.mean(o, axis=-1, keepdims=True)
    var = jnp.mean(jnp.square(o - mu), axis=-1, keepdims=True)
    y = (o - mu) * lax.rsqrt(var + EPS)
    b, h, n, d = y.shape
    y = y.transpose(0, 2, 1, 3).reshape(b, n, h * d)
    return y * g.astype(jnp.float32) + bias.astype(jnp.float32)


def block_attention(qn, qr, kn, kr, v):
    b, h, nq, _ = qn.shape
    nb = nq // ATTN_BLOCK
    scale = (MLA_NOPE + MLA_ROPE) ** -0.5

    def blk(qs):
        qnb, qrb = qs
        s = (jnp.einsum('bhqd,bhkd->bhqk', qnb, kn) + jnp.einsum('bhqd,bkd->bhqk', qrb, kr)).astype(jnp.float32) * scale
        p = jax.nn.softmax(s, axis=-1)
        return jnp.einsum('bhqk,bhkv->bhqv', p.astype(v.dtype), v)

    to_blocks = lambda t: jnp.moveaxis(t.reshape(b, h, nb, ATTN_BLOCK, t.shape[-1]), 2, 0)
    out = lax.map(blk, (to_blocks(qn), to_blocks(qr)))
    return jnp.moveaxis(out, 0, 2).reshape(b, h, nq, -1)


def trunk_layer(x, cond, grid, ctx, w_mod, b_mod, g_norm_mix, g_norm_ffn, w_in, g_q_a, w_q_b,
                g_kv_a, w_kv_b, g_qn, g_qr, g_kn, g_kr, ret_p_fwd, ret_p_bwd, g_ret_gn, b_ret_gn,
                w_o, w_ffn_in, w_ffn_out):
    sh1, sc1, gt1, sh2, sc2, gt2 = modulation(cond, w_mod, b_mod)
    h = rms_norm(x, g_norm_mix) * (1.0 + sc1) + sh1
    rq, rk, rv, rg, qn, qr, ckv, kr = project(h, w_in, g_q_a, w_q_b, g_kv_a, g_qn, g_qr, g_kr)
    kn, v = mla_up(ckv, w_kv_b, g_kn)
    if grid is None:
        zeros = jnp.zeros((x.shape[0], RET_HEADS, RET_DK, RET_DV), jnp.float32)
        s_f0, s_b0 = zeros, zeros
        kr_keys = kr
    else:
        row, col = grid
        rq, rk, qr = axial_rope(rq, row, col), axial_rope(rk, row, col), axial_rope(qr, row, col)
        ckv_c, kr_c, s_f0, s_b0 = ctx
        kn_c, v_c = mla_up(ckv_c, w_kv_b, g_kn)
        kn = jnp.concatenate([kn, kn_c], axis=2)
        v = jnp.concatenate([v, v_c], axis=2)
        kr_keys = jnp.concatenate([axial_rope(kr, row, col), kr_c], axis=1)
    ret, s_f, s_b = bidir_retention(rq, rk, rv, ret_p_fwd, ret_p_bwd, s_f0, s_b0)
    attn = block_attention(qn, qr, kn, kr_keys, v)
    b, n, _ = x.shape
    ret = head_group_norm(ret, g_ret_gn, b_ret_gn).astype(x.dtype) * jax.nn.silu(rg)
    attn = attn.transpose(0, 2, 1, 3).reshape(b, n, MLA_WIDTH)
    x = x + gt1 * (jnp.concatenate([ret, attn], axis=-1) @ w_o)
    h = rms_norm(x, g_norm_ffn) * (1.0 + sc2) + sh2
    gate, up = jnp.split(h @ w_ffn_in, 2, axis=-1)
    x = x + gt2 * ((jax.nn.silu(gate) * up) @ w_ffn_out)
    return x, (ckv, kr, s_f.astype(x.dtype), s_b.astype(x.dtype))


def setup_inputs(seed: int = 0) -> dict:
    key = jax.random.key(seed)
    keys = jax.random.split(key, 32)

    def nrm(i, shape, scale):
        return jax.random.normal(keys[i], shape, jnp.float32) * scale

    def gain(i, shape):
        return 1.0 + 0.05 * jax.random.normal(keys[i], shape, jnp.float32)

    decay_base = RET_DECAY_OFFSET + jnp.arange(RET_HEADS, dtype=jnp.float32)
    st_shape = (DEC_BATCH, DEPTH, RET_HEADS, RET_DK, RET_DV)
    return {
        'x_prompt': nrm(0, (BATCH, SEQ, D_MODEL), 1.0),
        'x_sample': nrm(1, (DEC_BATCH, DEC_SEQ, D_MODEL), 1.0),
        'cache_ckv': nrm(2, (DEC_BATCH, DEPTH, PAST_LEN, KV_RANK), 1.0),
        'cache_krope': nrm(3, (DEC_BATCH, DEPTH, PAST_LEN, MLA_ROPE), 1.0),
        'state_ret_fwd': nrm(4, st_shape, 0.5),
        'state_ret_bwd': nrm(5, st_shape, 0.5),
        'c': nrm(6, (DEC_BATCH, D_MODEL), 1.0),
        'c_ctx': nrm(7, (D_MODEL,), 1.0),
        'w_mod': nrm(8, (DEPTH, D_MODEL, N_MOD * D_MODEL), 0.5 * D_MODEL ** -0.5),
        'b_mod': nrm(9, (DEPTH, N_MOD * D_MODEL), 0.02),
        'g_norm_mix': gain(10, (DEPTH, D_MODEL)),
        'g_norm_ffn': gain(11, (DEPTH, D_MODEL)),
        'w_in': nrm(12, (DEPTH, D_MODEL, IN_COLS), D_MODEL ** -0.5),
        'g_q_a': gain(13, (DEPTH, Q_RANK)),
        'w_q_b': nrm(14, (DEPTH, Q_RANK, MLA_HEADS * (MLA_NOPE + MLA_ROPE)), Q_RANK ** -0.5),
        'g_kv_a': gain(15, (DEPTH, KV_RANK)),
        'w_kv_b': nrm(16, (DEPTH, KV_RANK, MLA_HEADS * (MLA_NOPE + MLA_V)), KV_RANK ** -0.5),
        'g_qn': gain(17, (DEPTH, MLA_NOPE)),
        'g_qr': gain(18, (DEPTH, MLA_ROPE)),
        'g_kn': gain(19, (DEPTH, MLA_NOPE)),
        'g_kr': gain(20, (DEPTH, MLA_ROPE)),
        'ret_p_fwd': decay_base + nrm(21, (DEPTH, RET_HEADS), 0.1),
        'ret_p_bwd': decay_base + nrm(22, (DEPTH, RET_HEADS), 0.1),
        'g_ret_gn': gain(23, (DEPTH, RET_WIDTH)),
        'b_ret_gn': nrm(24, (DEPTH, RET_WIDTH), 0.02),
        'w_o': nrm(25, (DEPTH, D_MIX, D_MODEL), D_MIX ** -0.5),
        'w_ffn_in': nrm(26, (DEPTH, D_MODEL, 2 * D_FF), D_MODEL ** -0.5),
        'w_ffn_out': nrm(27, (DEPTH, D_FF, D_MODEL), D_FF ** -0.5),
    }


def reference(x_prompt, x_sample, cache_ckv, cache_krope, state_ret_fwd, state_ret_bwd, c, c_ctx,
              w_mod, b_mod, g_norm_mix, g_norm_ffn, w_in, g_q_a, w_q_b, g_kv_a, w_kv_b,
              g_qn, g_qr, g_kn, g_kr, ret_p_fwd, ret_p_bwd, g_ret_gn, b_ret_gn, w_o,
              w_ffn_in, w_ffn_out):
    def layer_params(l):
        return (w_mod[l], b_mod[l], g_norm_mix[l], g_norm_ffn[l], w_in[l], g_q_a[l], w_q_b[l],
                g_kv_a[l], w_kv_b[l], g_qn[l], g_qr[l], g_kn[l], g_kr[l], ret_p_fwd[l],
                ret_p_bwd[l], g_ret_gn[l], b_ret_gn[l], w_o[l], w_ffn_in[l], w_ffn_out[l])

    x = x_prompt
    ckv_l, kr_l, sf_l, sb_l = [], [], [], []
    for l in range(DEPTH):
        x, (ckv, kr, s_f, s_b) = trunk_layer(x, c_ctx, None, None, *layer_params(l))
        ckv_l.append(ckv)
        kr_l.append(kr)
        sf_l.append(s_f)
        sb_l.append(s_b)
    y_prompt = x

    n_lat = x_sample.shape[1]
    rows = n_lat // GRID_W
    grid = (jnp.repeat(jnp.arange(rows), GRID_W), jnp.tile(jnp.arange(GRID_W), rows))
    y = x_sample
    for l in range(DEPTH):
        ctx = (cache_ckv[:, l], cache_krope[:, l], state_ret_fwd[:, l], state_ret_bwd[:, l])
        y, _ = trunk_layer(y, c, grid, ctx, *layer_params(l))

    new_ckv = jnp.stack(ckv_l, axis=1)
    new_krope = jnp.stack(kr_l, axis=1)
    new_ret_fwd = jnp.stack(sf_l, axis=1)
    new_ret_bwd = jnp.stack(sb_l, axis=1)
    return (y_prompt, y, new_ckv, new_krope, new_ret_fwd, new_ret_bwd)
```

```python
import os
import numpy as np
from contextlib import ExitStack
import concourse.bass as bass
import concourse.mybir as mybir
from concourse.bass_utils import run_bass_kernel_spmd

F32 = mybir.dt.float32
BF16 = mybir.dt.bfloat16
ALU = mybir.AluOpType
AF = mybir.ActivationFunctionType
AX = mybir.AxisListType

N_DMA_SEMS = 12
NCORES = 8
D = 1024
TOK = 1280
NTILE = 10
IN_COLS = 2464
DFF = 2816
EPS = 1e-6
LN2 = float(np.log(2.0))
ATT_SCALE = float(96 ** -0.5)

G_KVA, G_QN, G_QR, G_KN, G_KR, G_GN, B_GN = 0, 128, 192, 224, 288, 320, 832
GSM_N = 1344


class StopBuild(Exception):
    pass


class Sync:
    def __init__(self, nc, stack):
        self.nc = nc
        self.stack = stack
        self.engs = {"pe": nc.tensor, "act": nc.scalar, "dve": nc.vector,
                     "pool": nc.gpsimd, "sp": nc.sync}
        self.sem = {k: stack.enter_context(nc.semaphore("sem_" + k)) for k in self.engs}
        self.cnt = {k: 0 for k in self.engs}
        self.seen = {k: {} for k in self.engs}
        self.dsem = {q: [stack.enter_context(nc.semaphore("dsem_%s%d" % (q, i)))
                         for i in range(N_DMA_SEMS)] for q in ("sp", "pool")}
        self.dcnt = {q: 0 for q in self.dsem}
        self.ccsem = []
        self.res = {}
        self.n_wait = 0

    def _semof(self, kind, a):
        if kind == "eng":
            return self.sem[a]
        if kind == "dma":
            return self.dsem[a[0]][a[1]]
        return self.ccsem[a]

    def _wait(self, e, tok):
        if tok is None:
            return
        kind, a, v = tok
        key = (kind, a)
        if self.seen[e].get(key, 0) >= v:
            return
        self.seen[e][key] = v
        self.engs[e].wait_ge(self._semof(kind, a), v)
        self.n_wait += 1

    def _deps(self, e, reads, writes, is_dma):
        deps = []
        for r in reads:
            st = self.res.get(r)
            if st and st["w"] is not None:
                deps.append(st["w"])
        for w in writes:
            st = self.res.get(w)
            if st:
                for t in [st["w"]] + list(st["r"].values()):
                    if t is None:
                        continue
                    if (not is_dma) and t[0] == "eng" and t[1] == e:
                        continue
                    deps.append(t)
        for t in deps:
            self._wait(e, t)

    def _update(self, tok, reads, writes):
        for r in reads:
            st = self.res.setdefault(r, {"w": None, "r": {}})
            st["r"][(tok[0], tok[1])] = tok
        for w in writes:
            self.res[w] = {"w": tok, "r": {}}

    @staticmethod
    def _split(reads, writes):
        writes = list(writes) + [r for r in reads if r.startswith("ps:")]
        reads = [r for r in reads if not r.startswith("ps:")]
        return reads, writes

    def op(self, e, fn, reads=(), writes=()):
        reads, writes = self._split(reads, writes)
        self._deps(e, reads, writes, False)
        inst = fn(self.engs[e])
        self.cnt[e] += 1
        inst.then_inc(self.sem[e], 1)
        tok = ("eng", e, self.cnt[e])
        self._update(tok, reads, writes)
        return tok

    def dma(self, q, out, in_, reads=(), writes=(), **kw):
        self._deps(q, reads, writes, True)
        n = self.dcnt[q]
        self.dcnt[q] += 1
        slot = n % N_DMA_SEMS
        prev = 16 * (n // N_DMA_SEMS)
        if prev > 0:
            self._wait(q, ("dma", (q, slot), prev))
        inst = self.engs[q].dma_start(out=out, in_=in_, **kw)
        inst.then_inc(self.dsem[q][slot], 16)
        tok = ("dma", (q, slot), prev + 16)
        self._update(tok, reads, writes)
        return tok

    def collective(self, ins, outs, groups, reads, writes):
        self._deps("pool", reads, writes, True)
        if self.ccsem:
            self._wait("pool", ("cc", len(self.ccsem) - 1, 1))
        sem = self.stack.enter_context(self.nc.semaphore("ccsem%d" % len(self.ccsem)))
        self.ccsem.append(sem)
        inst = self.nc.gpsimd.collective_compute("AllGather", ALU.bypass, replica_groups=groups,
                                                 ins=ins, outs=outs)
        inst.then_inc(sem)
        tok = ("cc", len(self.ccsem) - 1, 1)
        self._update(tok, reads, writes)
        return tok

    def alias(self, new_names, old_names):
        merged = {}
        for o in old_names:
            st = self.res.get(o)
            if not st:
                continue
            toks = list(st["r"].values()) + ([st["w"]] if st["w"] is not None else [])
            for t in toks:
                k = (t[0], t[1])
                if k not in merged or merged[k][2] < t[2]:
                    merged[k] = t
        for n in new_names:
            st = self.res.setdefault(n, {"w": None, "r": {}})
            for k, t in merged.items():
                if k not in st["r"] or st["r"][k][2] < t[2]:
                    st["r"][k] = t

    def wait_all(self, e):
        for r, st in self.res.items():
            if st["w"] is not None:
                self._wait(e, st["w"])


def bcast(ap, shape):
    return ap.to_broadcast(shape)


def build_program(stop=99, dbg=()):
    nc = bass.Bass("TRN2", target_bir_lowering=False)
    dbg_out = {}

    def din(name, shape, dt=F32):
        return nc.dram_tensor(name, list(shape), dt, kind="ExternalInput").ap()

    def dout(name, shape, dt=F32):
        return nc.dram_tensor(name, list(shape), dt, kind="ExternalOutput").ap()

    xT_d = din("xT", [D, TOK])
    w_in_d = din("w_in", [2, D, IN_COLS])
    w_qb_d = din("w_q_b", [2, 256, 768])
    w_kvb_d = din("w_kv_b", [2, 128, 1024])
    w_o_d = din("w_o", [2, D, D])
    w_fi_d = din("w_ffn_in", [2, D, 2 * DFF])
    w_fo_d = din("w_ffn_out", [2, DFF, D])
    w_mod_d = din("w_mod_sh", [2, D, 1536])
    b_mod_d = din("b_mod_sh", [128, 12, 2])
    cond_d = din("condT", [128, 8, 3])
    sel_d = din("sel", [128, 2])
    gn_d = din("gnT", [128, 2, 2, 8])
    gqa_d = din("gqaT", [128, 2, 2])
    gsm_d = din("gsm", [2, GSM_N])
    retp_d = din("retp", [2, 16])
    cckv_d = din("c_ckv", [2, 256, 128])
    ckr_d = din("c_kr", [2, 256, 32])
    s0_d = din("s0", [2, 2, 8, 64, 64])
    tabs_d = din("tabs", [128, 4, 2, 256])
    idx_d = din("idxt", [128, 2, 2])
    idxq_d = din("idxq", [128, 2, 2])
    rope64_d = din("rope64", [128, 2, 2, 2, 16])
    rope32_d = din("rope32", [128, 2, 2, 2, 8])
    xco_d = din("xcoef", [128, 2, 2, 5])

    yT_d = dout("yT", [D, TOK])
    ockv_d = dout("o_ckv", [2, 1024, 128])
    okr_d = dout("o_kr", [2, 1024, 32])
    osf_d = dout("o_sf", [2, 4, 8, 64, 64])
    osb_d = dout("o_sb", [2, 4, 8, 64, 64])

    mb_c = nc.dram_tensor("mod_bounce", [32, 512], F32).ap()
    mg_c = nc.dram_tensor("mod_gath", [128, 512], F32).ap()
    mb_d = mb_c.rearrange("a b -> (a b)").rearrange("(p f) -> p f", f=128)
    mg_d = mg_c.rearrange("a b -> (a b)").rearrange("(p f) -> p f", f=128)
    kvb_c = [nc.dram_tensor("kv_bounce%d" % l, [80, 512], F32).ap() for l in range(2)]
    kvg_c = [nc.dram_tensor("kv_gath%d" % l, [320, 512], F32).ap() for l in range(2)]
    kvb_d = [a.rearrange("a b -> (a b)").rearrange("(t n) -> t n", n=160) for a in kvb_c]
    kvg_d = [a.rearrange("a b -> (a b)").rearrange("(t n) -> t n", n=160) for a in kvg_c]
    ub_c = [nc.dram_tensor("u_bounce%d" % l, [128, 512], F32).ap() for l in range(2)]
    ug_c = [nc.dram_tensor("u_gath%d" % l, [512, 512], F32).ap() for l in range(2)]
    ub_d = [a.rearrange("a b -> (a b)").rearrange("(t n) -> t n", n=64) for a in ub_c]
    ug_d = [a.rearrange("a b -> (a b)").rearrange("(t n) -> t n", n=64) for a in ug_c]

    def dbg_dump(S, name, ap, shape, dt, reads):
        if name in dbg:
            o = dout("dbg_" + name, shape, dt)
            S.dma("sp", o, ap, reads=reads, writes=["dbgo_" + name])
            dbg_out[name] = (shape, dt)

    with ExitStack() as st:
        S = Sync(nc, st)
        ncd = nc.allow_non_contiguous_dma(reason="small strided parameter loads")
        st.enter_context(ncd)

        def sb(name, shape, dt):
            return st.enter_context(nc.sbuf_tensor(name, list(shape), dt))

        xT = sb("xT_sb", [128, 8, TOK], F32)
        hm = sb("hm_sb", [128, 8, TOK], BF16)
        R1 = sb("R1", [128, 22 * 1024], BF16)
        R2 = R1[:, 0:8192] if os.environ.get("SHRINK") else sb("R2", [128, 8 * 1024], BF16)
        R3 = sb("R3", [128, 35 * 1024], BF16)
        w_in = R1[:, 0:8 * IN_COLS].rearrange("p (c n) -> p c n", c=8)
        w_qb = R1[:, 8 * IN_COLS:8 * IN_COLS + 1536].rearrange("p (c n) -> p c n", c=2)
        w_kvb = R1[:, 8 * IN_COLS + 1536:8 * IN_COLS + 2560]
        w_fo = R1[:, :].rearrange("p (c n) -> p c n", c=22)
        sKT = R1[:, 0:8 * TOK].rearrange("p (h t) -> p h t", h=8)
        sVX = R1[:, 8 * TOK:8 * TOK + 10 * 520].rearrange("p (k h e) -> p k h e", k=10, h=8)
        w_o = R2[:, :].rearrange("p (c n) -> p c n", c=8)
        w_fi = [R2[:, i * 4096:(i + 1) * 4096].rearrange("p (c n) -> p c n", c=8) for i in range(2)]
        actT = R3[:, 0:22 * TOK].rearrange("p (c t) -> p c t", c=22)

        R3N = 35 * 1024
        r3o = [0]

        def r3(nelem_bf16, dt=BF16):
            o = r3o[0]
            r3o[0] += (nelem_bf16 + 15) // 16 * 16
            assert r3o[0] <= R3N, r3o[0]
            v = R3[:, o:o + nelem_bf16]
            return v.bitcast(F32) if dt == F32 else v

        DT = r3(2 * 8 * 512, F32).rearrange("p (h j i) -> p h j i", h=8, j=2)
        seg_base = r3o[0]
        qk_off = r3o[0]
        q_tm = r3(2 * 512).rearrange("p (t n) -> p t n", t=2)
        k_tm = r3(2 * 512).rearrange("p (t n) -> p t n", t=2)
        mix_tm = R3[:, qk_off:qk_off + 2048].rearrange("p (t n) -> p t n", t=2)
        v_tm = r3(2 * 512).rearrange("p (t n) -> p t n", t=2)
        g_tm = r3(2 * 512).rearrange("p (t n) -> p t n", t=2)
        kdec = r3(2 * 2 * 512).rearrange("p (d t n) -> p d t n", d=2, t=2)
        qkT = r3(2 * 4 * 256).rearrange("p (w c i) -> p w c i", w=2, c=4)
        qa_n = r3(2 * 256).rearrange("p (t n) -> p t n", t=2)
        qaT = r3(2 * 256).rearrange("p (c i) -> p c i", c=2)
        ckv = r3(2 * 2 * 128, F32).rearrange("p (t n) -> p t n", t=2)
        kr = r3(2 * 2 * 32, F32).rearrange("p (t n) -> p t n", t=2)
        ckvb = r3(128)
        ckvT = r3(128)
        Qc_tm = r3(2 * 768).rearrange("p (t h e) -> p t h e", t=2, h=8)
        QcT = r3(8 * 256).rearrange("p (h i) -> p h i", h=8)
        Kc_tm = r3(768).rearrange("p (h e) -> p h e", h=8)
        kc_off = r3o[0]
        KcT = r3(8 * 256).rearrange("p (h i) -> p h i", h=8)
        vx_off = r3o[0]
        VX = r3(2 * 520).rearrange("p (t h e) -> p t h e", t=2, h=8)
        Sacc = R3[:, kc_off:kc_off + 2048].bitcast(F32).rearrange("p (d h v) -> p d h v", d=2, h=8)
        Sin = R3[:, vx_off:vx_off + 1024].rearrange("p (d h v) -> p d h v", d=2, h=8)
        sqt = r3(2 * 768, F32)
        fA = r3(2 * 512, F32)
        fB = r3(2 * 512, F32)
        AT = [r3(512).rearrange("p (j i) -> p j i", j=2) for _ in range(2)]
        PT = [r3(256) for _ in range(2)]
        Ust = r3(2 * 2 * 512, F32).rearrange("p (d n) -> p d n", d=2)
        Ug = r3(2 * 512, F32).rearrange("p (h v) -> p h v", h=8)
        kvin = r3(2 * 160, F32)
        SEGN = ["q_tm", "k_tm", "v_tm", "g_tm", "kdec", "qkT", "qa_n", "qaT", "ckv", "kr", "ckvb", "ckvT", "Qc_tm", "QcT",
                "Kc_tm", "KcT", "VX", "sqt", "sqtr", "sqtr2", "fqk", "fkr", "fA", "fB", "AT0", "AT1", "PT0", "PT1", "mix_tm",
                "Ust", "Sin", "Ug", "Sacc", "kvin"]
        seg_end = r3o[0]
        r3o[0] = seg_base
        nsq = r3(2 * 8 * 512, F32).rearrange("p (c t) -> p c t", c=8)
        s8 = r3(2 * 512, F32)
        rstd = r3(2 * 512, F32)
        ntmp = [r3(2 * 512, F32) for i in range(3)]
        tabs = r3(2 * 4 * 512, F32).rearrange("p (k j i) -> p k j i", k=4, j=2)
        dtmp = r3(2 * 2 * 512, F32).rearrange("p (k j i) -> p k j i", k=2, j=2)
        NRMN = ["nsq", "s8", "rstd", "ntmp0", "ntmp1", "ntmp2", "tabs", "dtmp0", "dtmp1"]
        print("R3 usage: seg_end=%d norm_end=%d of %d" % (seg_end, r3o[0], R3N))
        wm = R3[:, 8192:8192 + 2 * 8 * 1536].rearrange("p (l c n) -> p l c n", l=2, c=8)

        ones = sb("ones", [128, 128], F32)
        ident = sb("ident", [128, 128], BF16)
        identf = sb("identf", [128, 128], F32)
        epsc = sb("epsc", [128, 1], F32)
        onec = sb("onec", [128, 1], F32)
        condT = sb("condT_sb", [128, 8, 3], F32)
        sT = sb("sT", [128, 8, 3], BF16)
        sel = sb("sel_sb", [128, 2], F32)
        bmT = sb("bmT", [128, 12, 2], F32)
        ml = sb("ml", [128, 12, 2, 3], F32)
        MODT = sb("MODT", [128, 48, 2, 3], F32)
        MS = sb("MS", [128, 48, 2, 2], F32)
        gn = sb("gn_sb", [128, 2, 2, 8], F32)
        GG = sb("GG", [128, 2, 2, 2, 8], F32)
        gqa = sb("gqa_sb", [128, 2, 2], F32)
        gsm = sb("gsm_sb", [128, GSM_N], F32)
        gq96 = sb("gq96", [128, 96], F32)
        RP = sb("RP", [128, 16], F32)
        LG = sb("LG", [128, 2, 8], F32)
        KD = sb("KD", [128, 2, 2, 8], F32)
        KQ = sb("KQ", [128, 2, 2, 8], F32)
        idxq = sb("idxq_sb", [128, 2, 2], F32)
        idxt = sb("idxt_sb", [128, 2, 2], F32)
        rope64 = sb("rope64_sb", [128, 2, 2, 2, 16], F32)
        rope32 = sb("rope32_sb", [128, 2, 2, 2, 8], F32)
        xco = sb("xco_sb", [128, 2, 2, 5], F32)
        xcf = sb("xcf", [128, 2, 5, 8], F32)
        st8 = sb("st8", [128, 64], F32)
        sg = [sb("sg%d" % i, [128, 512], F32) for i in range(2)]

        print("SBUF bytes remaining after allocation:", nc.sbuf_bytes_remaining)
        PS = [st.enter_context(nc.psum_tensor("psb%d" % i, [128, 512], F32)) for i in range(8)]
        PSN = ["ps:%d" % i for i in range(8)]

        def mm(bank, out, lhsT, rhs, start, stop, reads):
            S.op("pe", lambda e: e.matmul(out, lhsT=lhsT, rhs=rhs, start=start, stop=stop),
                 reads=reads, writes=[PSN[bank]])

        def tr(bank, out, in_, reads):
            S.op("pe", lambda e: e.transpose(out, in_, ident[:in_.shape[0], :in_.shape[0]]),
                 reads=list(reads) + ["ident"], writes=[PSN[bank]])

        def act(out, in_, func, reads, writes, **kw):
            S.op("act", lambda e: e.activation(out=out, in_=in_, func=func, **kw), reads=reads, writes=writes)

        def tt(eng, out, in0, in1, op, reads, writes):
            S.op(eng, lambda e: e.tensor_tensor(out=out, in0=in0, in1=in1, op=op), reads=reads, writes=writes)

        def ts(eng, out, in0, s1, s2, op0, op1, reads, writes):
            if s2 is None:
                S.op(eng, lambda e: e.tensor_scalar(out=out, in0=in0, scalar1=s1, scalar2=None, op0=op0),
                     reads=reads, writes=writes)
            else:
                S.op(eng, lambda e: e.tensor_scalar(out=out, in0=in0, scalar1=s1, scalar2=s2, op0=op0, op1=op1),
                     reads=reads, writes=writes)

        def stt(eng, out, in0, scalar, in1, op0, op1, reads, writes):
            S.op(eng, lambda e: e.scalar_tensor_tensor(out=out, in0=in0, scalar=scalar, in1=in1, op0=op0, op1=op1),
                 reads=reads, writes=writes)

        def red(out, in_, reads, writes):
            S.op("dve", lambda e: e.tensor_reduce(out=out, in_=in_, axis=AX.X, op=ALU.add), reads=reads, writes=writes)

        def rsqrt(buf, scale, reads_writes):
            act(buf, buf, AF.Sqrt, reads=[reads_writes, "epsc"], writes=[reads_writes], scale=scale, bias=epsc[:buf.shape[0], :])
            S.op("dve", lambda e: e.reciprocal(out=buf, in_=buf), reads=[reads_writes], writes=[reads_writes])

        def ck(name):
            if stop == name:
                raise StopBuild()

        S.dma("sp", xT[:], xT_d.rearrange("(c p) t -> p c t", p=128), writes=["xT"])
        S.dma("sp", condT[:], cond_d, writes=["condT"])
        S.dma("sp", sel[:], sel_d, writes=["sel"])
        S.dma("sp", bmT[:], b_mod_d, writes=["bmT"])
        S.dma("sp", gn[:], gn_d, writes=["gn"])
        S.dma("sp", gqa[:], gqa_d, writes=["gqa"])
        S.dma("sp", idxt[:], idx_d, writes=["idxt"])
        S.dma("sp", idxq[:], idxq_d, writes=["idxq"])
        S.dma("sp", rope64[:], rope64_d, writes=["rope64"])
        S.dma("sp", rope32[:], rope32_d, writes=["rope32"])
        S.dma("sp", xco[:], xco_d, writes=["xco"])
        S.dma("pool", wm, w_mod_d.rearrange("l (c p) n -> p l c n", p=128), writes=["wm"])

        S.op("pool", lambda e: e.memset(ones[:], 1.0), writes=["ones"])
        S.op("pool", lambda e: e.memset(epsc[:], EPS), writes=["epsc"])
        S.op("pool", lambda e: e.memset(onec[:], 1.0), writes=["onec"])
        S.op("pool", lambda e: e.memset(identf[:], 0.0), writes=["identf"])
        S.op("pool", lambda e: e.affine_select(out=identf[:], in_=identf[:], pattern=[[-1, 128]],
                                               compare_op=ALU.not_equal, fill=1.0, base=0, channel_multiplier=1),
             reads=["identf"], writes=["identf"])
        S.op("dve", lambda e: e.tensor_copy(ident[:], identf[:]), reads=["identf"], writes=["ident"])

        def modulation_setup():
            ck("t0")
            act(sT[:], condT[:], AF.Silu, reads=["condT"], writes=["sT"])
            pm = PS[0][:, 0:72].rearrange("p (j l k) -> p j l k", j=12, l=2)
            for l in range(2):
                for cj in range(12):
                    for c in range(8):
                        mm(0, pm[:, cj, l, :], wm[:, l, c, cj * 128:(cj + 1) * 128], sT[:, c, :], c == 0, c == 7,
                           reads=["wm", "sT"])
            tt("dve", ml[:], pm, bcast(bmT[:].unsqueeze(3), [128, 12, 2, 3]), ALU.add, reads=[PSN[0], "bmT"], writes=["ml"])
            ck("t1")
            S.dma("sp", mb_d[:, 0:72], ml[:].rearrange("p j l k -> p (j l k)"), reads=["ml"], writes=["mb_d"])
            S.collective([mb_c], [mg_c], [[0, 1, 2, 3], [4, 5, 6, 7]], reads=["mb_d"], writes=["mg_d"])
            S.dma("sp", MODT[:].rearrange("p (r j) l k -> p r (j l k)", r=4), mg_d.rearrange("(r p) f -> p r f", p=128)[:, :, 0:72],
                  reads=["mg_d"], writes=["MODT"])
            ck("t2")
            S.op("dve", lambda e: e.tensor_copy(MS[:, :, :, 0], MODT[:, :, :, 0]), reads=["MODT"], writes=["MS0"])
            ts("dve", MS[:, :, :, 1], MODT[:, :, :, 1], sel[:, 0:1], None, ALU.mult, None, reads=["MODT", "sel"], writes=["MS1"])
            stt("dve", MS[:, :, :, 1], MODT[:, :, :, 2], sel[:, 1:2], MS[:, :, :, 1], ALU.mult, ALU.add,
                reads=["MODT", "sel", "MS1"], writes=["MS1"])
            for l in range(2):
                for grp in range(2):
                    for which, k in ((0, 1), (1, 4)):
                        stt("dve", GG[:, l, grp, which, :], MS[:, k * 8:(k + 1) * 8, l, grp], 1.0, gn[:, l, which, :],
                            ALU.add, ALU.mult, reads=["MS0", "MS1", "gn"], writes=["GG"])

        def modcol(k, c, l, grp):
            return MS[:, k * 8 + c, l, grp:grp + 1]

        GROUPS = [(0, 512, 0), (512, 1024, 0), (1024, 1280, 1)]

        def norm_phase(l, which):
            kk_sh = 0 if which == 0 else 3
            for gi, (t0, t1, grp) in enumerate(GROUPS):
                n = t1 - t0
                act(nsq[:, :, 0:n], xT[:, :, t0:t1], AF.Square, reads=["xT:%d" % gi], writes=["nsq"])
                red(s8[:, 0:n], nsq[:, :, 0:n].rearrange("p c t -> p t c"), reads=["nsq"], writes=["s8"])
                mm(7, PS[7][:, 0:n], ones[:], s8[:, 0:n], True, True, reads=["ones", "s8"])
                act(rstd[:, 0:n], PS[7][:, 0:n], AF.Sqrt, reads=[PSN[7], "epsc"], writes=["rstd"], scale=1.0 / D, bias=epsc[:])
                S.op("dve", lambda e: e.reciprocal(out=rstd[:, 0:n], in_=rstd[:, 0:n]), reads=["rstd"], writes=["rstd"])
                for c in range(8):
                    tb = ntmp[c % 3]
                    stt("dve", tb[:, 0:n], xT[:, c, t0:t1], GG[:, l, grp, which, c:c + 1], rstd[:, 0:n], ALU.mult, ALU.mult,
                        reads=["xT:%d" % gi, "GG", "rstd"], writes=["ntmp%d" % (c % 3)])
                    act(hm[:, c, t0:t1], tb[:, 0:n], AF.Identity, reads=["ntmp%d" % (c % 3), "MS0", "MS1"],
                        writes=hmg(gi), bias=modcol(kk_sh, c, l, grp), scale=1.0)


        S.alias(["xT:0", "xT:1", "xT:2"], ["xT"])

        GSEG = [[0, 1], [2, 3], [4]]

        def hmg(gi):
            return ["hm:s%d" % sg_ for sg_ in GSEG[gi]]

        def layer_tables(l):
            S.dma("sp", gsm[:], gsm_d[l].partition_broadcast(128), writes=["gsm"])
            S.dma("sp", RP[:], retp_d[l].partition_broadcast(128), writes=["RP"])
            S.dma("sp", tabs, tabs_d, writes=["tabs"])
            ts("dve", gq96[:], gsm[:, G_QN:G_QN + 96], ATT_SCALE, None, ALU.mult, None, reads=["gsm"], writes=["gq96"])
            act(LG[:].rearrange("p d h -> p (d h)"), RP[:], AF.Exp, reads=["RP"], writes=["LG"], scale=-LN2)
            act(LG[:].rearrange("p d h -> p (d h)"), LG[:].rearrange("p d h -> p (d h)"), AF.Ln, reads=["LG", "onec"], writes=["LG"],
                scale=-1.0, bias=onec[:])
            for h in range(8):
                act(dtmp[:, 0], tabs[:, 0], AF.Exp, reads=["tabs", "LG"], writes=["dtmp0"], scale=LG[:, 0, h:h + 1])
                act(dtmp[:, 1], tabs[:, 1], AF.Exp, reads=["tabs", "LG"], writes=["dtmp1"], scale=LG[:, 1, h:h + 1])
                tt("dve", dtmp[:, 0], dtmp[:, 0], tabs[:, 2], ALU.mult, reads=["dtmp0", "tabs"], writes=["dtmp0"])
                tt("dve", dtmp[:, 1], dtmp[:, 1], tabs[:, 3], ALU.mult, reads=["dtmp1", "tabs"], writes=["dtmp1"])
                tt("dve", DT[:, h], dtmp[:, 0], dtmp[:, 1], ALU.add, reads=["dtmp0", "dtmp1"], writes=["DT"])
            for d in range(2):
                for jt in range(2):
                    act(KD[:, d, jt, :], LG[:, d, :], AF.Exp, reads=["LG", "idxt"], writes=["KD"], scale=idxt[:, d, jt:jt + 1])
                for it in range(2):
                    act(KQ[:, d, it, :], LG[:, d, :], AF.Exp, reads=["LG", "idxq"], writes=["KQ"], scale=idxq[:, d, it:it + 1])
            for d in range(2):
                tt("dve", xcf[:, d], bcast(LG[:, d, :].unsqueeze(1), [128, 5, 8]), bcast(xco[:, 0, d, :].unsqueeze(2), [128, 5, 8]),
                   ALU.mult, reads=["LG", "xco"], writes=["xcf"])
            act(xcf[:].rearrange("p d s h -> p (d s h)"), xcf[:].rearrange("p d s h -> p (d s h)"), AF.Exp, reads=["xcf"], writes=["xcf"])
            for d in range(2):
                tt("dve", xcf[:, d], xcf[:, d], bcast(xco[:, 1, d, :].unsqueeze(2), [128, 5, 8]), ALU.mult,
                   reads=["xcf", "xco"], writes=["xcf"])

        def rope(out, src, tl, tab, H, nf, reads, writes, scale=None):
            sv = src.rearrange("p (h f x n) -> p h f x n", h=H, f=2, x=2)
            ov = out.rearrange("p (h f x n) -> p h f x n", h=H, f=2, x=2)
            C = bcast(tab[:, tl, 0].unsqueeze(1), [128, H, 2, nf])
            Sn = bcast(tab[:, tl, 1].unsqueeze(1), [128, H, 2, nf])
            a = fA[:, 0:H * 2 * nf].rearrange("p (h f n) -> p h f n", h=H, f=2)
            b = fB[:, 0:H * 2 * nf].rearrange("p (h f n) -> p h f n", h=H, f=2)
            x1, x2 = sv[:, :, :, 0, :], sv[:, :, :, 1, :]
            tt("dve", a, x1, C, ALU.mult, reads=reads, writes=["fA"])
            tt("dve", b, x2, Sn, ALU.mult, reads=reads, writes=["fB"])
            tt("dve", ov[:, :, :, 0, :], a, b, ALU.subtract, reads=["fA", "fB"], writes=writes)
            tt("dve", a, x1, Sn, ALU.mult, reads=reads, writes=["fA"])
            tt("dve", b, x2, C, ALU.mult, reads=reads, writes=["fB"])
            tt("dve", ov[:, :, :, 1, :], a, b, ALU.add, reads=["fA", "fB"], writes=writes)

        def seg_project(l, seg, emit_out=True):
            is_s = seg == 4
            S.alias(["q_tm", "k_tm"], ["mix_tm"])
            S.alias(["KcT", "VX"], ["Sacc", "Sin"])
            gi = 2 if is_s else seg // 2
            for tl in range(2):
                tile = seg * 2 + tl
                t0 = tile * 128
                for cg in range(5):
                    c0 = cg * 512
                    ncol = min(512, IN_COLS - c0)
                    for c in range(8):
                        mm(cg, PS[cg][:, 0:ncol], hm[:, c, t0:t0 + 128], w_in[:, c, c0:c0 + ncol], c == 0, c == 7,
                           reads=["hm:s%d" % seg, "w_in"])
                if is_s:
                    rope(fqk0, PS[0][:, :], tl, rope64, 8, 16, reads=[PSN[0], "rope64"], writes=["fqk"])
                    S.op("act", lambda e: e.copy(q_tm[:, tl, :], fqk0), reads=["fqk"], writes=["q_tm"])
                else:
                    S.op("act", lambda e: e.copy(q_tm[:, tl, :], PS[0][:, :]), reads=[PSN[0]], writes=["q_tm"])
                if is_s:
                    rope(fqk0, PS[1][:, :], tl, rope64, 8, 16, reads=[PSN[1], "rope64"], writes=["fqk"])
                    S.op("act", lambda e: e.mul(k_tm[:, tl, :], fqk0, 0.125), reads=["fqk"], writes=["k_tm"])
                else:
                    S.op("act", lambda e: e.mul(k_tm[:, tl, :], PS[1][:, :], 0.125), reads=[PSN[1]], writes=["k_tm"])
                S.op("dve", lambda e: e.tensor_copy(v_tm[:, tl, :], PS[2][:, :]), reads=[PSN[2]], writes=["v_tm"])
                act(g_tm[:, tl, :], PS[3][:, :], AF.Silu, reads=[PSN[3]], writes=["g_tm"])
                zq = PS[4]
                act(sqt[:, 0:416], zq[:, 0:416], AF.Square, reads=[PSN[4]], writes=["sqt"])
                red(st8[:, 0:1], sqt[:, 0:256], reads=["sqt"], writes=["st8"])
                red(st8[:, 1:2], sqt[:, 256:384], reads=["sqt"], writes=["st8"])
                red(st8[:, 2:3], sqt[:, 384:416], reads=["sqt"], writes=["st8"])
                act(st8[:, 0:1], st8[:, 0:1], AF.Sqrt, reads=["st8", "epsc"], writes=["st8"], scale=1.0 / 256, bias=epsc[:])
                act(st8[:, 1:2], st8[:, 1:2], AF.Sqrt, reads=["st8", "epsc"], writes=["st8"], scale=1.0 / 128, bias=epsc[:])
                act(st8[:, 2:3], st8[:, 2:3], AF.Sqrt, reads=["st8", "epsc"], writes=["st8"], scale=1.0 / 32, bias=epsc[:])
                S.op("dve", lambda e: e.reciprocal(out=st8[:, 0:3], in_=st8[:, 0:3]), reads=["st8"], writes=["st8"])
                ts("dve", qa_n[:, tl, :], zq[:, 0:256], st8[:, 0:1], None, ALU.mult, None, reads=[PSN[4], "st8"], writes=["qa_n"])
                stt("dve", ckv[:, tl, :], zq[:, 256:384], st8[:, 1:2], gsm[:, G_KVA:G_KVA + 128], ALU.mult, ALU.mult,
                    reads=[PSN[4], "st8", "gsm"], writes=["ckv"])
                if is_s:
                    stt("dve", fB[:, 256:288], zq[:, 384:416], st8[:, 2:3], gsm[:, G_KR:G_KR + 32], ALU.mult, ALU.mult,
                        reads=[PSN[4], "st8", "gsm"], writes=["fkr"])
                    rope(kr[:, tl, :], fB[:, 256:288], tl, rope32, 1, 8, reads=["fkr", "rope32"], writes=["kr"])
                else:
                    stt("dve", kr[:, tl, :], zq[:, 384:416], st8[:, 2:3], gsm[:, G_KR:G_KR + 32], ALU.mult, ALU.mult,
                        reads=[PSN[4], "st8", "gsm"], writes=["kr"])
            if not emit_out:
                return
            if is_s:
                S.dma("sp", kvb_d[l][:, 0:128].rearrange("(t p) n -> p t n", p=128), ckv[:], reads=["ckv"], writes=["kvb_d%d" % l])
                S.dma("sp", kvb_d[l][:, 128:160].rearrange("(t p) n -> p t n", p=128), kr[:], reads=["kr"], writes=["kvb_d%d" % l])
            else:
                S.dma("sp", ockv_d[l, seg * 256:(seg + 1) * 256, :].rearrange("(t p) n -> p t n", p=128), ckv[:], reads=["ckv"], writes=["ockv"])
                S.dma("sp", okr_d[l, seg * 256:(seg + 1) * 256, :].rearrange("(t p) n -> p t n", p=128), kr[:], reads=["kr"], writes=["okr"])

        fqk0 = sqt[:, 0:512]

        def seg_q(l, seg):
            is_s = seg == 4
            for tl in range(2):
                pt = PS[5][:, 0:128].bitcast(BF16).rearrange("p (c i) -> p c i", c=2)
                for c in range(2):
                    tr(5, pt[:, c, :], qa_n[:, tl, c * 128:(c + 1) * 128], reads=["qa_n"])
                for c in range(2):
                    act(qaT[:, c, tl * 128:(tl + 1) * 128], pt[:, c, :], AF.Copy, reads=[PSN[5], "gqa"], writes=["qaT"],
                        scale=gqa[:, l, c:c + 1])
                for half in range(2):
                    for c in range(2):
                        mm(6 + half, PS[6 + half][:, 0:384], qaT[:, c, tl * 128:(tl + 1) * 128],
                           w_qb[:, c, half * 384:(half + 1) * 384], c == 0, c == 1, reads=["qaT", "w_qb"])
                for half in range(2):
                    qv = PS[6 + half][:, 0:384].rearrange("p (h e) -> p h e", h=4)
                    sq = sqt[:, half * 384:(half + 1) * 384].rearrange("p (h e) -> p h e", h=4)
                    act(sq, qv, AF.Square, reads=[PSN[6 + half]], writes=["sqt"])
                    red(st8[:, 8 + half * 4:12 + half * 4], sq[:, :, 0:64], reads=["sqt"], writes=["st8q"])
                    red(st8[:, 16 + half * 4:20 + half * 4], sq[:, :, 64:96], reads=["sqt"], writes=["st8q"])
                act(st8[:, 8:16], st8[:, 8:16], AF.Sqrt, reads=["st8q", "epsc"], writes=["st8q"], scale=1.0 / 64, bias=epsc[:])
                act(st8[:, 16:24], st8[:, 16:24], AF.Sqrt, reads=["st8q", "epsc"], writes=["st8q"], scale=1.0 / 32, bias=epsc[:])
                S.op("dve", lambda e: e.reciprocal(out=st8[:, 8:24], in_=st8[:, 8:24]), reads=["st8q"], writes=["st8q"])
                for half in range(2):
                    qv = PS[6 + half][:, 0:384].rearrange("p (h e) -> p h e", h=4)
                    dst = fA[:, 0:384].rearrange("p (h e) -> p h e", h=4) if half == 0 else fB[:, 0:384].rearrange("p (h e) -> p h e", h=4)
                    nm = "fA" if half == 0 else "fB"
                    tt("dve", dst[:, :, 0:64], qv[:, :, 0:64], bcast(st8[:, 8 + half * 4:12 + half * 4].unsqueeze(2), [128, 4, 64]),
                       ALU.mult, reads=[PSN[6 + half], "st8q"], writes=[nm])
                    tt("dve", dst[:, :, 64:96], qv[:, :, 64:96], bcast(st8[:, 16 + half * 4:20 + half * 4].unsqueeze(2), [128, 4, 32]),
                       ALU.mult, reads=[PSN[6 + half], "st8q"], writes=[nm])
                    if not is_s:
                        tt("dve", Qc_tm[:, tl, half * 4:(half + 1) * 4, :], dst, bcast(gq96[:].unsqueeze(1), [128, 4, 96]), ALU.mult,
                           reads=[nm, "gq96"], writes=["Qc_tm"])
                    else:
                        tt("dve", dst, dst, bcast(gq96[:].unsqueeze(1), [128, 4, 96]), ALU.mult, reads=[nm, "gq96"], writes=[nm])
                        S.op("act", lambda e: e.copy(Qc_tm[:, tl, half * 4:(half + 1) * 4, 0:64], dst[:, :, 0:64]), reads=[nm], writes=["Qc_tm"])
                        S.op("act", lambda e: e.copy(sqt[:, 0:128].rearrange("p (h e) -> p h e", h=4), dst[:, :, 64:96]), reads=[nm], writes=["sqtr"])
                        rope(sqt[:, 128:256], sqt[:, 0:128], tl, rope32, 4, 8, reads=["sqtr", "rope32"], writes=["sqtr2"])
                        S.op("act", lambda e: e.copy(Qc_tm[:, tl, half * 4:(half + 1) * 4, 64:96],
                                                     sqt[:, 128:256].rearrange("p (h e) -> p h e", h=4)), reads=["sqtr2"], writes=["Qc_tm"])
                pq = PS[5][:, :].bitcast(BF16).rearrange("p (h i) -> p h i", h=8)
                for h in range(8):
                    tr(5, pq[0:96, h, :], Qc_tm[:, tl, h, :], reads=["Qc_tm"])
                S.op("act", lambda e: e.copy(QcT[0:96, :, tl * 128:(tl + 1) * 128], pq[0:96, :, :]), reads=[PSN[5]], writes=["QcT"])

        def mla_up(src_ckv, src_kr, KT_dst, VX_dst, reads, kt_name, vx_name):
            S.op("dve", lambda e: e.tensor_copy(ckvb, src_ckv), reads=reads, writes=["ckvb"])
            ck("m0a")
            pt = PS[5][:, 0:64].bitcast(BF16)
            tr(5, pt, ckvb, reads=["ckvb"])
            ck("m0b")
            S.op("act", lambda e: e.copy(ckvT, pt), reads=[PSN[5]], writes=["ckvT"])
            ck("m1")
            for half in range(2):
                mm(6 + half, PS[6 + half][:, :], ckvT, w_kvb[:, half * 512:(half + 1) * 512], True, True, reads=["ckvT", "w_kvb"])
            for half in range(2):
                kv = PS[6 + half][:, :].rearrange("p (h e) -> p h e", h=4)
                sq = sqt[:, half * 256:(half + 1) * 256].rearrange("p (h e) -> p h e", h=4)
                act(sq, kv[:, :, 0:64], AF.Square, reads=[PSN[6 + half]], writes=["sqt"])
                red(st8[:, 24 + half * 4:28 + half * 4], sq, reads=["sqt"], writes=["st8k"])
                S.op("act", lambda e: e.copy(VX_dst[:, half * 4:(half + 1) * 4, 0:64], kv[:, :, 64:128]), reads=[PSN[6 + half]], writes=[vx_name])
            ck("m2")
            act(st8[:, 24:32], st8[:, 24:32], AF.Sqrt, reads=["st8k", "epsc"], writes=["st8k"], scale=1.0 / 64, bias=epsc[:])
            S.op("dve", lambda e: e.reciprocal(out=st8[:, 24:32], in_=st8[:, 24:32]), reads=["st8k"], writes=["st8k"])
            for half in range(2):
                kv = PS[6 + half][:, :].rearrange("p (h e) -> p h e", h=4)
                tmp = fA[:, 0:256].rearrange("p (h e) -> p h e", h=4) if half == 0 else fB[:, 0:256].rearrange("p (h e) -> p h e", h=4)
                nm = "fA" if half == 0 else "fB"
                tt("dve", tmp, kv[:, :, 0:64], bcast(st8[:, 24 + half * 4:28 + half * 4].unsqueeze(2), [128, 4, 64]), ALU.mult,
                   reads=[PSN[6 + half], "st8k"], writes=[nm])
                tt("dve", Kc_tm[:, half * 4:(half + 1) * 4, 0:64], tmp, bcast(gsm[:, G_KN:G_KN + 64].unsqueeze(1), [128, 4, 64]), ALU.mult,
                   reads=[nm, "gsm"], writes=["Kc_tm"])
            ck("m3")
            S.op("act", lambda e: e.copy(Kc_tm[:, :, 64:96], bcast(src_kr.unsqueeze(1), [128, 8, 32])), reads=reads, writes=["Kc_tm"])
            ck("m4")
            S.op("dve", lambda e: e.memset(VX_dst[:, :, 64:65], 1.0), writes=[vx_name])
            ck("m5")
            pk = PS[5][:, :].bitcast(BF16).rearrange("p (h i) -> p h i", h=8)
            for h in range(8):
                tr(5, pk[0:96, h, :], Kc_tm[:, h, :], reads=["Kc_tm"])
            S.op("act", lambda e: e.copy(KT_dst, pk[0:96, :, :]), reads=[PSN[5]], writes=[kt_name])

        def seg_retention(l, seg, emit_out=True):
            is_s = seg == 4
            for w, (src, nm) in enumerate(((q_tm, "q_tm"), (k_tm, "k_tm"))):
                for tl in range(2):
                    pt = PS[5][:, 0:256].bitcast(BF16).rearrange("p (c i) -> p c i", c=4)
                    for c in range(4):
                        tr(5, pt[:, c, :], src[:, tl, c * 128:(c + 1) * 128], reads=[nm])
                    S.op("act", lambda e: e.copy(qkT[:, w, :, tl * 128:(tl + 1) * 128], pt), reads=[PSN[5]], writes=["qkT"])
            for d in range(2):
                for tl in range(2):
                    tt("dve", kdec[:, d, tl, :].rearrange("p (h e) -> p h e", h=8), k_tm[:, tl, :].rearrange("p (h e) -> p h e", h=8),
                       bcast(KD[:, d, tl, :].unsqueeze(2), [128, 8, 64]), ALU.mult, reads=["k_tm", "KD"], writes=["kdec"])
            for d, bank in ((0, 4), (1, 6)):
                for pr in range(4):
                    for tl in range(2):
                        mm(bank, PS[bank][:, pr * 128:(pr + 1) * 128], kdec[:, d, tl, pr * 128:(pr + 1) * 128],
                           v_tm[:, tl, pr * 128:(pr + 1) * 128], tl == 0, tl == 1, reads=["kdec", "v_tm"])
                S.op("act", lambda e: e.copy(Ust[:, d, :], PS[bank][:, :]), reads=[PSN[bank]], writes=["Ust"])
            Uv = Ust[:].rearrange("p d (r e) -> p d r e", r=4)
            if not emit_out:
                return
            if is_s:
                dst = ub_d[l].rearrange("(d r x k) v -> k d r x v", d=2, r=4, x=2)
                for d in range(2):
                    S.dma("sp", dst[:, d, :, 0, :], Uv[0:64, d, :, 0:64], reads=["Ust"], writes=["ub_d%d" % l])
                    S.dma("sp", dst[:, d, :, 1, :], Uv[64:128, d, :, 64:128], reads=["Ust"], writes=["ub_d%d" % l])
            else:
                for d, od in ((0, osf_d), (1, osb_d)):
                    dst = od[l, seg].rearrange("(r x) k v -> k r x v", x=2)
                    S.dma("sp", dst[:, :, 0, :], Uv[0:64, d, :, 0:64], reads=["Ust"], writes=["ost"])
                    S.dma("sp", dst[:, :, 1, :], Uv[64:128, d, :, 64:128], reads=["Ust"], writes=["ost"])

        def seg_retention_out(l, seg):
            is_s = seg == 4
            S.alias(["mix_tm"], ["q_tm", "k_tm"])
            for h in range(8):
                c, po = h // 2, (h % 2) * 64
                bank = h % 2
                pst = PS[bank][:, :].rearrange("p (j i) -> p j i", j=2)
                for jt in range(2):
                    mm(bank, pst[:, jt, :], qkT[po:po + 64, 1, c, jt * 128:(jt + 1) * 128], qkT[po:po + 64, 0, c, :], True, True,
                       reads=["qkT"])
                at = AT[h % 2]
                tt("dve", at, pst, DT[:, h], ALU.mult, reads=[PSN[bank], "DT"], writes=["AT%d" % (h % 2)])
                for it in range(2):
                    ob = 2 + it
                    o = PS[ob][:, h * 64:(h + 1) * 64]
                    for jt in range(2):
                        mm(ob, o, at[:, jt, it * 128:(it + 1) * 128], v_tm[:, jt, h * 64:(h + 1) * 64], jt == 0, jt == 1,
                           reads=["AT%d" % (h % 2), "v_tm"])
                    if is_s:
                        for d in range(2):
                            cb = 4 + it * 2 + d
                            mm(cb, PS[cb][:, h * 64:(h + 1) * 64], qkT[po:po + 64, 0, c, it * 128:(it + 1) * 128], Sin[po:po + 64, d, h, :],
                               True, True, reads=["qkT", "Sin"])
            for it in range(2):
                ob = 2 + it
                ov = PS[ob][:, :].rearrange("p (h e) -> p h e", h=8)
                ovn = [PSN[ob]]
                if is_s:
                    osb = sqt[:, 0:512].rearrange("p (h e) -> p h e", h=8)
                    for d in range(2):
                        cb = 4 + it * 2 + d
                        tmp = (fA if d == 0 else fB)[:, 0:512].rearrange("p (h e) -> p h e", h=8)
                        tt("dve", tmp, PS[cb][:, :].rearrange("p (h e) -> p h e", h=8), bcast(KQ[:, d, it, :].unsqueeze(2), [128, 8, 64]),
                           ALU.mult, reads=[PSN[cb], "KQ"], writes=["fA" if d == 0 else "fB"])
                    tt("dve", osb, ov, fA[:, 0:512].rearrange("p (h e) -> p h e", h=8), ALU.add, reads=[PSN[ob], "fA"], writes=["sqt"])
                    tt("dve", osb, osb, fB[:, 0:512].rearrange("p (h e) -> p h e", h=8), ALU.add, reads=["sqt", "fB"], writes=["sqt"])
                    ov = osb
                    ovn = ["sqt"]
                red(st8[:, 32:40], ov, reads=ovn, writes=["st8g"])
                ts("dve", st8[:, 32:40], st8[:, 32:40], 1.0 / 64, None, ALU.mult, None, reads=["st8g"], writes=["st8g"])
                dv_ = fA[:, 0:512].rearrange("p (h e) -> p h e", h=8)
                tt("dve", dv_, ov, bcast(st8[:, 32:40].unsqueeze(2), [128, 8, 64]), ALU.subtract, reads=ovn + ["st8g"], writes=["fA"])
                sq = fB[:, 0:512].rearrange("p (h e) -> p h e", h=8)
                act(sq, dv_, AF.Square, reads=["fA"], writes=["fB"])
                red(st8[:, 40:48], sq, reads=["fB"], writes=["st8h"])
                act(st8[:, 40:48], st8[:, 40:48], AF.Sqrt, reads=["st8h", "epsc"], writes=["st8h"], scale=1.0 / 64, bias=epsc[:])
                S.op("dve", lambda e: e.reciprocal(out=st8[:, 40:48], in_=st8[:, 40:48]), reads=["st8h"], writes=["st8h"])
                tt("dve", dv_, dv_, bcast(st8[:, 40:48].unsqueeze(2), [128, 8, 64]), ALU.mult, reads=["fA", "st8h"], writes=["fA"])
                tt("dve", fA[:, 0:512], fA[:, 0:512], gsm[:, G_GN:G_GN + 512], ALU.mult, reads=["fA", "gsm"], writes=["fA"])
                tt("dve", fA[:, 0:512], fA[:, 0:512], gsm[:, B_GN:B_GN + 512], ALU.add, reads=["fA", "gsm"], writes=["fA"])
                tt("dve", mix_tm[:, it, 0:512], fA[:, 0:512], g_tm[:, it, :], ALU.mult, reads=["fA", "g_tm"], writes=["mix_tm"])

        def seg_attention(l, seg, nkt, KT, VXs, kt_name, vx_name):
            for hg in range(2):
                for hh in range(4):
                    h = hg * 4 + hh
                    for kt in range(nkt):
                        sbk = (h * nkt + kt) % 2
                        ps_ = PS[sbk][:, 0:256]
                        mm(sbk, ps_, KT[0:96, h, kt * 128:(kt + 1) * 128], QcT[0:96, h, :], True, True, reads=[kt_name, "QcT"])
                        p = PT[(h * nkt + kt) % 2]
                        pn = "PT%d" % ((h * nkt + kt) % 2)
                        act(p, ps_, AF.Exp, reads=[PSN[sbk]], writes=[pn])
                        for it in range(2):
                            ob = 2 + it if hg == 0 else 4 + 2 * it
                            mm(ob, PS[ob][:, hh * 65:(hh + 1) * 65], p[:, it * 128:(it + 1) * 128], VXs(kt)[:, h, :], kt == 0, kt == nkt - 1,
                               reads=[pn, vx_name])
            for hg in range(2):
                for it in range(2):
                    ob = 2 + it if hg == 0 else 4 + 2 * it
                    ov = PS[ob][:, 0:260].rearrange("p (h e) -> p h e", h=4)
                    S.op("dve", lambda e: e.reciprocal(out=st8[:, 48:52], in_=ov[:, :, 64]), reads=[PSN[ob]], writes=["st8a"])
                    tt("dve", mix_tm[:, it, 512 + hg * 256:512 + (hg + 1) * 256].rearrange("p (h e) -> p h e", h=4), ov[:, :, 0:64],
                       bcast(st8[:, 48:52].unsqueeze(2), [128, 4, 64]), ALU.mult, reads=[PSN[ob], "st8a"], writes=["mix_tm"])

        def seg_mix_out(seg):
            gi = 2 if seg == 4 else seg // 2
            for tl in range(2):
                t0 = (seg * 2 + tl) * 128
                pt = PS[7][:, :].bitcast(BF16).rearrange("p (c i) -> p c i", c=8)
                for c in range(8):
                    tr(7, pt[:, c, :], mix_tm[:, tl, c * 128:(c + 1) * 128], reads=["mix_tm"])
                S.op("act", lambda e: e.copy(hm[:, :, t0:t0 + 128], pt), reads=[PSN[7]], writes=["hm:s%d" % seg])

        def sample_exchange(l):
            S.collective([kvb_c[l]], [kvg_c[l]], [[0, 1, 2, 3], [4, 5, 6, 7]], reads=["kvb_d%d" % l], writes=["kvg_d%d" % l])
            S.collective([ub_c[l]], [ug_c[l]], [[0, 1, 2, 3], [4, 5, 6, 7]], reads=["ub_d%d" % l], writes=["ug_d%d" % l])

        def sample_states(l):
            ugv = ug_d[l].rearrange("(s d h k) v -> k s d h v", s=4, d=2, h=8)
            for d in range(2):
                for s_ in range(5):
                    if s_ < 4:
                        S.dma("sp", Ug[0:64], ugv[:, s_, d], reads=["ug_d%d" % l], writes=["Ug"])
                    else:
                        S.dma("sp", Ug[0:64], s0_d[l, d].rearrange("h k v -> k h v"), writes=["Ug"])
                    cf = bcast(xcf[0:64, d, s_, :].unsqueeze(2), [64, 8, 64])
                    if s_ == 0:
                        tt("dve", Sacc[0:64, d], Ug[0:64], cf, ALU.mult, reads=["Ug", "xcf"], writes=["Sacc"])
                    else:
                        tt("dve", fA[0:64, 0:512].rearrange("p (h e) -> p h e", h=8), Ug[0:64], cf, ALU.mult,
                           reads=["Ug", "xcf"], writes=["fA"])
                        tt("dve", Sacc[0:64, d], Sacc[0:64, d], fA[0:64, 0:512].rearrange("p (h e) -> p h e", h=8), ALU.add,
                           reads=["Sacc", "fA"], writes=["Sacc"])
            S.op("act", lambda e: e.copy(Sin[0:64], Sacc[0:64]), reads=["Sacc"], writes=["Sin"])
            S.dma("sp", Sin[64:128], Sin[0:64], reads=["Sin"], writes=["Sin"])

        def sample_keys(l):
            S.alias(["sKT", "sVX"], ["w_in"])
            kg = kvg_d[l].rearrange("(t p) n -> t p n", p=128)
            for kt in range(10):
                if kt < 8:
                    S.dma("sp", kvin[:, :], kg[kt], reads=["kvg_d%d" % l], writes=["kvin"])
                else:
                    S.dma("sp", kvin[:, 0:128], cckv_d[l, (kt - 8) * 128:(kt - 7) * 128, :], writes=["kvin"])
                    S.dma("sp", kvin[:, 128:160], ckr_d[l, (kt - 8) * 128:(kt - 7) * 128, :], writes=["kvin"])
                mla_up(kvin[:, 0:128], kvin[:, 128:160], sKT[0:96, :, kt * 128:(kt + 1) * 128], sVX[:, kt], ["kvin"], "sKT", "sVX")

        def phase_wo(l):
            for gi, (t0, t1, grp) in enumerate(GROUPS):
                n = t1 - t0
                for j in range(8):
                    bank = j % 4
                    for c in range(8):
                        mm(bank, PS[bank][:, 0:n], w_o[:, c, j * 128:(j + 1) * 128], hm[:, c, t0:t1], c == 0, c == 7,
                           reads=["w_o"] + hmg(gi))
                    stt("dve", xT[:, j, t0:t1], PS[bank][:, 0:n], modcol(2, j, l, grp), xT[:, j, t0:t1], ALU.mult, ALU.add,
                        reads=[PSN[bank], "MS0", "MS1", "xT:%d" % gi], writes=["xT:%d" % gi])

        def phase_ffn(l):
            S.alias(["actT"], ["DT", "wm"] + SEGN + NRMN)
            S.alias(["wfi0", "wfi1"], ["w_o"])
            S.alias(["w_fo"], ["w_in", "w_qb", "w_kvb", "sKT", "sVX"])
            nblk = 22
            for j in range(nblk):
                buf = j % 2
                wv = w_fi[buf]
                S.dma("pool", wv[:, :, 0:128], w_fi_d[l, :, j * 128:(j + 1) * 128].rearrange("(c p) n -> p c n", p=128), writes=["wfi%d" % buf])
                S.dma("pool", wv[:, :, 128:256], w_fi_d[l, :, DFF + j * 128:DFF + (j + 1) * 128].rearrange("(c p) n -> p c n", p=128),
                      writes=["wfi%d" % buf])
                if j == 2:
                    S.dma("pool", w_fo, w_fo_d[l].rearrange("(c p) n -> p c n", p=128), writes=["w_fo"])
                for gi, (t0, t1, grp) in enumerate(GROUPS):
                    n = t1 - t0
                    pb = ((j * 3 + gi) % 2) * 2
                    for half in range(2):
                        for c in range(8):
                            mm(pb + half, PS[pb + half][:, 0:n], wv[:, c, half * 128:(half + 1) * 128], hm[:, c, t0:t1], c == 0, c == 7,
                               reads=["wfi%d" % buf] + hmg(gi))
                    sgb = sg[(j * 3 + gi) % 2]
                    sgn = "sg%d" % ((j * 3 + gi) % 2)
                    act(sgb[:, 0:n], PS[pb][:, 0:n], AF.Silu, reads=[PSN[pb]], writes=[sgn])
                    tt("dve", actT[:, j, t0:t1], sgb[:, 0:n], PS[pb + 1][:, 0:n], ALU.mult, reads=[sgn, PSN[pb + 1]], writes=["actT"])
            for gi, (t0, t1, grp) in enumerate(GROUPS):
                n = t1 - t0
                for j in range(8):
                    bank = 4 + (j % 4)
                    for c in range(22):
                        mm(bank, PS[bank][:, 0:n], w_fo[:, c, j * 128:(j + 1) * 128], actT[:, c, t0:t1], c == 0, c == 21,
                           reads=["w_fo", "actT"])
                    stt("dve", xT[:, j, t0:t1], PS[bank][:, 0:n], modcol(5, j, l, grp), xT[:, j, t0:t1], ALU.mult, ALU.add,
                        reads=[PSN[bank], "MS0", "MS1", "xT:%d" % gi], writes=["xT:%d" % gi])

        def main_program():
            modulation_setup()
            S.alias(["DT"] + NRMN, ["wm"])
            ck("s0")
            for l in range(2):
                if l > 0:
                    S.alias(["w_in", "w_qb", "w_kvb"], ["w_fo"])
                    S.alias(["w_o"], ["wfi0", "wfi1"])
                    S.alias(["DT"] + NRMN, ["actT"])
                S.dma("pool", w_in, w_in_d[l].rearrange("(c p) n -> p c n", p=128), writes=["w_in"])
                S.dma("pool", w_qb, w_qb_d[l].rearrange("(c p) n -> p c n", p=128), writes=["w_qb"])
                S.dma("pool", w_kvb, w_kvb_d[l], writes=["w_kvb"])
                S.dma("pool", w_o, w_o_d[l].rearrange("(c p) n -> p c n", p=128), writes=["w_o"])
                ck("s1")
                layer_tables(l)
                ck("s2")
                norm_phase(l, 0)
                ck("a0")
                if stop == 1:
                    dbg_dump(S, "hm", hm[:], [128, 8, TOK], BF16, ["hm:s%d" % i for i in range(5)])
                    break
                S.alias(SEGN, NRMN)
                seg_project(l, 4)
                ck("a1")
                seg_retention(l, 4)
                ck("a2")
                sample_exchange(l)
                ck("a3")
                for seg in range(4):
                    seg_project(l, seg)
                    seg_retention(l, seg)
                    ck("b1")
                    seg_retention_out(l, seg)
                    ck("b2")
                    seg_q(l, seg)
                    ck("b3")
                    for tl in range(2):
                        mla_up(ckv[:, tl, :], kr[:, tl, :], KcT[0:96, :, tl * 128:(tl + 1) * 128], VX[:, tl], ["ckv", "kr"], "KcT", "VX")
                    ck("b4")
                    seg_attention(l, seg, 2, KcT, lambda kt: VX[:, kt], "KcT", "VX")
                    ck("b5")
                    seg_mix_out(seg)
                    ck("b6")
                seg_project(l, 4, emit_out=False)
                seg_retention(l, 4, emit_out=False)
                S.alias(["Sacc", "Sin"], ["KcT", "VX"])
                ck("c0")
                sample_states(l)
                ck("c1")
                seg_retention_out(l, 4)
                seg_q(l, 4)
                ck("c2")
                sample_keys(l)
                ck("c3")
                seg_attention(l, 4, 10, sKT, lambda kt: sVX[:, kt], "sKT", "sVX")
                seg_mix_out(4)
                if stop == 2:
                    dbg_dump(S, "mixT", hm[:], [128, 8, TOK], BF16, ["hm:s%d" % i for i in range(5)])
                    break
                phase_wo(l)
                S.alias(NRMN, SEGN)
                norm_phase(l, 1)
                phase_ffn(l)
                if stop == 3:
                    break


        try:
            main_program()
        except StopBuild:
            pass

        for gi, (t0, t1, grp) in enumerate(GROUPS):
            S.dma("sp", yT_d.rearrange("(c p) t -> p c t", p=128)[:, :, t0:t1], xT[:, :, t0:t1], reads=["xT:%d" % gi], writes=["yT_d%d" % gi])
        S.wait_all("sp")
        print("program built: waits=%d counts=%s dmas=%s r3=%d" % (S.n_wait, S.cnt, S.dcnt, r3o[0]))
    return nc, dbg_out


def _rope_tab(pos_row, pos_col, nf):
    inv = (10000.0 ** (-np.arange(nf, dtype=np.float64) / nf))
    ar = pos_row[:, None].astype(np.float64) * inv
    ac = pos_col[:, None].astype(np.float64) * inv
    C = np.stack([np.cos(ar), np.cos(ac)], 1)
    Sn = np.stack([np.sin(ar), np.sin(ac)], 1)
    return np.stack([C, Sn], 1).astype(np.float32)


def make_inputs(inp):
    f = lambda a: np.ascontiguousarray(np.asarray(a, dtype=np.float32))
    x_prompt, x_sample = f(inp["x_prompt"]), f(inp["x_sample"])
    shared = {k: f(inp[k]) for k in ("w_in", "w_q_b", "w_kv_b", "w_o", "w_ffn_in", "w_ffn_out")}
    w_mod, b_mod = f(inp["w_mod"]), f(inp["b_mod"])
    conds = np.stack([f(inp["c_ctx"]), f(inp["c"])[0], f(inp["c"])[1]], 0)
    condT = f(conds.reshape(3, 8, 128).transpose(2, 1, 0))
    gnT = f(np.stack([f(inp["g_norm_mix"]), f(inp["g_norm_ffn"])], 1).reshape(2, 2, 8, 128).transpose(3, 0, 1, 2))
    gqaT = f(f(inp["g_q_a"]).reshape(2, 2, 128).transpose(2, 0, 1))
    gsm = f(np.concatenate([f(inp[k]) for k in ("g_kv_a", "g_qn", "g_qr", "g_kn", "g_kr", "g_ret_gn", "b_ret_gn")], 1))
    assert gsm.shape == (2, GSM_N)
    retp = f(np.concatenate([f(inp["ret_p_fwd"]), f(inp["ret_p_bwd"])], 1))
    j = np.arange(128)[:, None, None] + 128 * np.arange(2)[None, :, None]
    i = np.arange(256)[None, None, :]
    diff = (i - j).astype(np.float32)
    tabs = f(np.stack([np.maximum(diff, 0), np.maximum(-diff, 0), (diff >= 0).astype(np.float32),
                       (diff <= 0).astype(np.float32)], 1))
    jj = np.arange(128)[:, None] + 128 * np.arange(2)[None, :]
    idxt = f(np.stack([255.0 - jj, jj.astype(np.float64)], 1))
    idxq = f(np.stack([jj + 1.0, 256.0 - jj], 1))
    maps = []
    for core in range(NCORES):
        r, seq = core % 4, core // 4
        xs = np.concatenate([x_prompt[4 * core + b] for b in range(4)] + [x_sample[seq, 256 * r:256 * (r + 1)]], 0)
        n = 256 * r + np.arange(256)
        sel = np.zeros((128, 2), np.float32)
        sel[:, seq] = 1.0
        xco = np.zeros((2, 2, 5), np.float32)
        for s_ in range(4):
            if s_ < r:
                xco[0, 0, s_] = 256.0 * (r - 1 - s_); xco[1, 0, s_] = 1.0
            if s_ > r:
                xco[0, 1, s_] = 256.0 * (s_ - r - 1); xco[1, 1, s_] = 1.0
        xco[0, 0, 4] = 256.0 * r; xco[1, 0, 4] = 1.0
        xco[0, 1, 4] = 256.0 * (3 - r); xco[1, 1, 4] = 1.0
        m = dict(shared)
        m.update({
            "xT": f(xs.T),
            "w_mod_sh": f(w_mod[:, :, 1536 * r:1536 * (r + 1)]),
            "b_mod_sh": f(b_mod[:, 1536 * r:1536 * (r + 1)].reshape(2, 12, 128).transpose(2, 1, 0)),
            "condT": condT, "sel": sel, "gnT": gnT, "gqaT": gqaT, "gsm": gsm, "retp": retp,
            "c_ckv": f(inp["cache_ckv"])[seq], "c_kr": f(inp["cache_krope"])[seq],
            "s0": f(np.stack([f(inp["state_ret_fwd"])[seq], f(inp["state_ret_bwd"])[seq]], 1)),
            "tabs": tabs, "idxt": idxt, "idxq": idxq,
            "rope64": f(_rope_tab(n // 64, n % 64, 16).reshape(2, 128, 2, 2, 16).transpose(1, 0, 2, 3, 4)),
            "rope32": f(_rope_tab(n // 64, n % 64, 8).reshape(2, 128, 2, 2, 8).transpose(1, 0, 2, 3, 4)),
            "xcoef": f(np.broadcast_to(xco[None], (128, 2, 2, 5))),
        })
        maps.append(m)
    return maps


_CACHE = {}


def run(inp, stop=99, dbg=()):
    key = (stop, tuple(dbg))
    if key not in _CACHE:
        _CACHE[key] = build_program(stop, dbg)
    nc, dbg_out = _CACHE[key]
    maps = make_inputs(inp)
    res = run_bass_kernel_spmd(nc, maps, core_ids=list(range(NCORES)))
    return res.results


def kernel(**inp):
    R = run(inp)
    y_prompt = np.zeros((32, 256, D), np.float32)
    y_sample = np.zeros((2, 1024, D), np.float32)
    new_ckv = np.zeros((32, 2, 256, 128), np.float32)
    new_kr = np.zeros((32, 2, 256, 32), np.float32)
    new_sf = np.zeros((32, 2, 8, 64, 64), np.float32)
    new_sb = np.zeros((32, 2, 8, 64, 64), np.float32)
    for core in range(NCORES):
        r, seq = core % 4, core // 4
        o = R[core]
        y = np.asarray(o["yT"]).T
        y_prompt[4 * core:4 * core + 4] = y[0:1024].reshape(4, 256, D)
        y_sample[seq, 256 * r:256 * (r + 1)] = y[1024:1280]
        new_ckv[4 * core:4 * core + 4] = np.asarray(o["o_ckv"]).reshape(2, 4, 256, 128).transpose(1, 0, 2, 3)
        new_kr[4 * core:4 * core + 4] = np.asarray(o["o_kr"]).reshape(2, 4, 256, 32).transpose(1, 0, 2, 3)
        new_sf[4 * core:4 * core + 4] = np.asarray(o["o_sf"]).transpose(1, 0, 2, 3, 4)
        new_sb[4 * core:4 * core + 4] = np.asarray(o["o_sb"]).transpose(1, 0, 2, 3, 4)
    return (y_prompt, y_sample, new_ckv, new_kr, new_sf, new_sb)
```

```python
import os
import numpy as np
from contextlib import ExitStack
import concourse.bass as bass
import concourse.mybir as mybir
from concourse.bass_utils import run_bass_kernel_spmd

F32 = mybir.dt.float32
BF16 = mybir.dt.bfloat16
ALU = mybir.AluOpType
AF = mybir.ActivationFunctionType
AX = mybir.AxisListType

N_DMA_SEMS = 12
NCORES = 8
D = 1024
TOK = 1280
NTILE = 10
IN_COLS = 2464
DFF = 2816
EPS = 1e-6
LN2 = float(np.log(2.0))
ATT_SCALE = float(96 ** -0.5)

G_KVA, G_QN, G_QR, G_KN, G_KR, G_GN, B_GN = 0, 128, 192, 224, 288, 320, 832
GSM_N = 1344


class StopBuild(Exception):
    pass


class Sync:
    def __init__(self, nc, stack):
        self.nc = nc
        self.stack = stack
        self.engs = {"pe": nc.tensor, "act": nc.scalar, "dve": nc.vector,
                     "pool": nc.gpsimd, "sp": nc.sync}
        self.sem = {k: stack.enter_context(nc.semaphore("sem_" + k)) for k in self.engs}
        self.cnt = {k: 0 for k in self.engs}
        self.seen = {k: {} for k in self.engs}
        self.dsem = {q: [stack.enter_context(nc.semaphore("dsem_%s%d" % (q, i)))
                         for i in range(N_DMA_SEMS)] for q in ("sp", "pool")}
        self.dcnt = {q: 0 for q in self.dsem}
        self.ccsem = []
        self.res = {}
        self.n_wait = 0

    def _semof(self, kind, a):
        if kind == "eng":
            return self.sem[a]
        if kind == "dma":
            return self.dsem[a[0]][a[1]]
        return self.ccsem[a]

    def _wait(self, e, tok):
        if tok is None:
            return
        kind, a, v = tok
        key = (kind, a)
        if self.seen[e].get(key, 0) >= v:
            return
        self.seen[e][key] = v
        self.engs[e].wait_ge(self._semof(kind, a), v)
        self.n_wait += 1

    def _deps(self, e, reads, writes, is_dma):
        deps = []
        for r in reads:
            st = self.res.get(r)
            if st and st["w"] is not None:
                deps.append(st["w"])
        for w in writes:
            st = self.res.get(w)
            if st:
                for t in [st["w"]] + list(st["r"].values()):
                    if t is None:
                        continue
                    if (not is_dma) and t[0] == "eng" and t[1] == e:
                        continue
                    deps.append(t)
        for t in deps:
            self._wait(e, t)

    def _update(self, tok, reads, writes):
        for r in reads:
            st = self.res.setdefault(r, {"w": None, "r": {}})
            st["r"][(tok[0], tok[1])] = tok
        for w in writes:
            self.res[w] = {"w": tok, "r": {}}

    @staticmethod
    def _split(reads, writes):
        writes = list(writes) + [r for r in reads if r.startswith("ps:")]
        reads = [r for r in reads if not r.startswith("ps:")]
        return reads, writes

    def op(self, e, fn, reads=(), writes=(), inc=True):
        reads, writes = self._split(reads, writes)
        self._deps(e, reads, writes, False)
        inst = fn(self.engs[e])
        if inc:
            self.cnt[e] += 1
            inst.then_inc(self.sem[e], 1)
            tok = ("eng", e, self.cnt[e])
        else:
            tok = ("eng", e, self.cnt[e] + 1)
        self._update(tok, reads, writes)
        return tok

    def dma(self, q, out, in_, reads=(), writes=(), **kw):
        self._deps(q, reads, writes, True)
        n = self.dcnt[q]
        self.dcnt[q] += 1
        slot = n % N_DMA_SEMS
        prev = 16 * (n // N_DMA_SEMS)
        if prev > 0:
            self._wait(q, ("dma", (q, slot), prev))
        inst = self.engs[q].dma_start(out=out, in_=in_, **kw)
        inst.then_inc(self.dsem[q][slot], 16)
        tok = ("dma", (q, slot), prev + 16)
        self._update(tok, reads, writes)
        return tok

    def collective(self, ins, outs, groups, reads, writes):
        self._deps("pool", reads, writes, True)
        if self.ccsem:
            self._wait("pool", ("cc", len(self.ccsem) - 1, 1))
        sem = self.stack.enter_context(self.nc.semaphore("ccsem%d" % len(self.ccsem)))
        self.ccsem.append(sem)
        inst = self.nc.gpsimd.collective_compute("AllGather", ALU.bypass, replica_groups=groups,
                                                 ins=ins, outs=outs)
        inst.then_inc(sem)
        tok = ("cc", len(self.ccsem) - 1, 1)
        self._update(tok, reads, writes)
        return tok

    def alias(self, new_names, old_names):
        merged = {}
        for o in old_names:
            st = self.res.get(o)
            if not st:
                continue
            toks = list(st["r"].values()) + ([st["w"]] if st["w"] is not None else [])
            for t in toks:
                k = (t[0], t[1])
                if k not in merged or merged[k][2] < t[2]:
                    merged[k] = t
        for n in new_names:
            st = self.res.setdefault(n, {"w": None, "r": {}})
            for k, t in merged.items():
                if k not in st["r"] or st["r"][k][2] < t[2]:
                    st["r"][k] = t

    def wait_all(self, e):
        for r, st in self.res.items():
            if st["w"] is not None:
                self._wait(e, st["w"])


def bcast(ap, shape):
    return ap.to_broadcast(shape)


def build_program(stop=99, dbg=()):
    nc = bass.Bass("TRN2", target_bir_lowering=False)
    dbg_out = {}

    def din(name, shape, dt=F32):
        return nc.dram_tensor(name, list(shape), dt, kind="ExternalInput").ap()

    def dout(name, shape, dt=F32):
        return nc.dram_tensor(name, list(shape), dt, kind="ExternalOutput").ap()

    xT_d = din("xT", [D, TOK])
    w_in_d = din("w_in", [2, D, IN_COLS])
    w_qb_d = din("w_q_b", [2, 256, 768])
    w_kvb_d = din("w_kv_b", [2, 128, 1024])
    w_o_d = din("w_o", [2, D, D])
    w_fi_d = din("w_ffn_in", [2, D, 2 * DFF])
    w_fo_d = din("w_ffn_out", [2, DFF, D])
    w_mod_d = din("w_mod_sh", [2, D, 1536])
    b_mod_d = din("b_mod_sh", [128, 12, 2])
    cond_d = din("condT", [128, 8, 3])
    sel_d = din("sel", [128, 2])
    gn_d = din("gnT", [128, 2, 2, 8])
    gqa_d = din("gqaT", [128, 2, 2])
    gsm_d = din("gsm", [2, GSM_N])
    retp_d = din("retp", [2, 16])
    cckv_d = din("c_ckv", [2, 256, 128])
    ckr_d = din("c_kr", [2, 256, 32])
    s0_d = din("s0", [2, 2, 8, 64, 64])
    tabs_d = din("tabs", [128, 4, 2, 256])
    idx_d = din("idxt", [128, 2, 2])
    idxq_d = din("idxq", [128, 2, 2])
    rope64_d = din("rope64", [128, 2, 2, 2, 16])
    rope32_d = din("rope32", [128, 2, 2, 2, 8])
    xco_d = din("xcoef", [128, 2, 2, 5])

    yT_d = dout("yT", [D, TOK])
    ockv_d = dout("o_ckv", [2, 1024, 128])
    okr_d = dout("o_kr", [2, 1024, 32])
    osf_d = dout("o_sf", [2, 4, 8, 64, 64])
    osb_d = dout("o_sb", [2, 4, 8, 64, 64])

    mb_c = nc.dram_tensor("mod_bounce", [32, 512], F32).ap()
    mg_c = nc.dram_tensor("mod_gath", [128, 512], F32).ap()
    mb_d = mb_c.rearrange("a b -> (a b)").rearrange("(p f) -> p f", f=128)
    mg_d = mg_c.rearrange("a b -> (a b)").rearrange("(p f) -> p f", f=128)
    kvb_c = [nc.dram_tensor("kv_bounce%d" % l, [80, 512], F32).ap() for l in range(2)]
    kvg_c = [nc.dram_tensor("kv_gath%d" % l, [320, 512], F32).ap() for l in range(2)]
    kvb_d = [a.rearrange("a b -> (a b)").rearrange("(t n) -> t n", n=160) for a in kvb_c]
    kvg_d = [a.rearrange("a b -> (a b)").rearrange("(t n) -> t n", n=160) for a in kvg_c]
    ub_c = [nc.dram_tensor("u_bounce%d" % l, [128, 512], F32).ap() for l in range(2)]
    ug_c = [nc.dram_tensor("u_gath%d" % l, [512, 512], F32).ap() for l in range(2)]
    ub_d = [a.rearrange("a b -> (a b)").rearrange("(t n) -> t n", n=64) for a in ub_c]
    ug_d = [a.rearrange("a b -> (a b)").rearrange("(t n) -> t n", n=64) for a in ug_c]

    def dbg_dump(S, name, ap, shape, dt, reads):
        if name in dbg:
            o = dout("dbg_" + name, shape, dt)
            S.dma("sp", o, ap, reads=reads, writes=["dbgo_" + name])
            dbg_out[name] = (shape, dt)

    with ExitStack() as st:
        S = Sync(nc, st)
        ncd = nc.allow_non_contiguous_dma(reason="small strided parameter loads")
        st.enter_context(ncd)

        def sb(name, shape, dt):
            return st.enter_context(nc.sbuf_tensor(name, list(shape), dt))

        xT = sb("xT_sb", [128, 8, TOK], F32)
        hm = sb("hm_sb", [128, 8, TOK], BF16)
        R1 = sb("R1", [128, 22 * 1024], BF16)
        R2 = R1[:, 0:8192] if os.environ.get("SHRINK") else sb("R2", [128, 8 * 1024], BF16)
        R3 = sb("R3", [128, 35 * 1024], BF16)
        w_in = R1[:, 0:8 * IN_COLS].rearrange("p (c n) -> p c n", c=8)
        w_qb = R1[:, 8 * IN_COLS:8 * IN_COLS + 1536].rearrange("p (c n) -> p c n", c=2)
        w_kvb = R1[:, 8 * IN_COLS + 1536:8 * IN_COLS + 2560]
        w_fo = R1[:, :].rearrange("p (c n) -> p c n", c=22)
        sKT = R1[:, 0:8 * TOK].rearrange("p (h t) -> p h t", h=8)
        sVX = R1[:, 8 * TOK:8 * TOK + 10 * 520].rearrange("p (k h e) -> p k h e", k=10, h=8)
        w_o = R2[:, :].rearrange("p (c n) -> p c n", c=8)
        w_fi = [R2[:, i * 4096:(i + 1) * 4096].rearrange("p (c n) -> p c n", c=8) for i in range(2)]
        actT = R3[:, 0:22 * TOK].rearrange("p (c t) -> p c t", c=22)

        R3N = 35 * 1024
        r3o = [0]

        def r3(nelem_bf16, dt=BF16):
            o = r3o[0]
            r3o[0] += (nelem_bf16 + 15) // 16 * 16
            assert r3o[0] <= R3N, r3o[0]
            v = R3[:, o:o + nelem_bf16]
            return v.bitcast(F32) if dt == F32 else v

        DT = r3(2 * 8 * 512, F32).rearrange("p (h j i) -> p h j i", h=8, j=2)
        seg_base = r3o[0]
        qk_off = r3o[0]
        q_tm = r3(2 * 512).rearrange("p (t n) -> p t n", t=2)
        k_tm = r3(2 * 512).rearrange("p (t n) -> p t n", t=2)
        mix_tm = R3[:, qk_off:qk_off + 2048].rearrange("p (t n) -> p t n", t=2)
        v_tm = r3(2 * 512).rearrange("p (t n) -> p t n", t=2)
        g_tm = r3(2 * 512).rearrange("p (t n) -> p t n", t=2)
        kdec = r3(2 * 2 * 512).rearrange("p (d t n) -> p d t n", d=2, t=2)
        qkT = r3(2 * 4 * 256).rearrange("p (w c i) -> p w c i", w=2, c=4)
        qa_n = r3(2 * 256).rearrange("p (t n) -> p t n", t=2)
        qaT = r3(2 * 256).rearrange("p (c i) -> p c i", c=2)
        ckv = r3(2 * 2 * 128, F32).rearrange("p (t n) -> p t n", t=2)
        kr = r3(2 * 2 * 32, F32).rearrange("p (t n) -> p t n", t=2)
        ckvb = r3(128)
        ckvT = r3(128)
        Qc_tm = r3(2 * 768).rearrange("p (t h e) -> p t h e", t=2, h=8)
        QcT = r3(8 * 256).rearrange("p (h i) -> p h i", h=8)
        Kc_tm = r3(768).rearrange("p (h e) -> p h e", h=8)
        kc_off = r3o[0]
        KcT = r3(8 * 256).rearrange("p (h i) -> p h i", h=8)
        vx_off = r3o[0]
        VX = r3(2 * 520).rearrange("p (t h e) -> p t h e", t=2, h=8)
        Sacc = R3[:, kc_off:kc_off + 2048].bitcast(F32).rearrange("p (d h v) -> p d h v", d=2, h=8)
        Sin = R3[:, vx_off:vx_off + 1024].rearrange("p (d h v) -> p d h v", d=2, h=8)
        sqt = r3(2 * 768, F32)
        fA = r3(2 * 512, F32)
        fB = r3(2 * 512, F32)
        AT = [r3(512).rearrange("p (j i) -> p j i", j=2) for _ in range(2)]
        PT = [r3(256) for _ in range(2)]
        Ust = r3(2 * 2 * 512, F32).rearrange("p (d n) -> p d n", d=2)
        Ug = r3(2 * 512, F32).rearrange("p (h v) -> p h v", h=8)
        kvin = r3(2 * 160, F32)
        SEGN = ["q_tm", "k_tm", "v_tm", "g_tm", "kdec", "qkT", "qa_n", "qaT", "ckv", "kr", "ckvb", "ckvT", "Qc_tm", "QcT",
                "Kc_tm", "KcT", "VX", "sqt", "sqtr", "sqtr2", "fqk", "fkr", "fA", "fB", "AT0", "AT1", "PT0", "PT1", "mix_tm",
                "Ust", "Sin", "Ug", "Sacc", "kvin"]
        seg_end = r3o[0]
        r3o[0] = seg_base
        nsq = r3(2 * 8 * 512, F32).rearrange("p (c t) -> p c t", c=8)
        s8 = r3(2 * 512, F32)
        rstd = r3(2 * 512, F32)
        ntmp = [r3(2 * 512, F32) for i in range(3)]
        tabs = r3(2 * 4 * 512, F32).rearrange("p (k j i) -> p k j i", k=4, j=2)
        dtmp = r3(2 * 2 * 512, F32).rearrange("p (k j i) -> p k j i", k=2, j=2)
        NRMN = ["nsq", "s8", "rstd", "ntmp0", "ntmp1", "ntmp2", "tabs", "dtmp0", "dtmp1"]
        print("R3 usage: seg_end=%d norm_end=%d of %d" % (seg_end, r3o[0], R3N))
        wm = R3[:, 8192:8192 + 2 * 8 * 1536].rearrange("p (l c n) -> p l c n", l=2, c=8)

        ones = sb("ones", [128, 128], F32)
        ident = sb("ident", [128, 128], BF16)
        identf = sb("identf", [128, 128], F32)
        epsc = sb("epsc", [128, 1], F32)
        onec = sb("onec", [128, 1], F32)
        condT = sb("condT_sb", [128, 8, 3], F32)
        sT = sb("sT", [128, 8, 3], BF16)
        sel = sb("sel_sb", [128, 2], F32)
        bmT = sb("bmT", [128, 12, 2], F32)
        ml = sb("ml", [128, 12, 2, 3], F32)
        MODT = sb("MODT", [128, 48, 2, 3], F32)
        MS = sb("MS", [128, 48, 2, 2], F32)
        gn = sb("gn_sb", [128, 2, 2, 8], F32)
        GG = sb("GG", [128, 2, 2, 2, 8], F32)
        gqa = sb("gqa_sb", [128, 2, 2], F32)
        gsm = sb("gsm_sb", [128, GSM_N], F32)
        gq96 = sb("gq96", [128, 96], F32)
        RP = sb("RP", [128, 16], F32)
        LG = sb("LG", [128, 2, 8], F32)
        KD = sb("KD", [128, 2, 2, 8], F32)
        KQ = sb("KQ", [128, 2, 2, 8], F32)
        idxq = sb("idxq_sb", [128, 2, 2], F32)
        idxt = sb("idxt_sb", [128, 2, 2], F32)
        rope64 = sb("rope64_sb", [128, 2, 2, 2, 16], F32)
        rope32 = sb("rope32_sb", [128, 2, 2, 2, 8], F32)
        xco = sb("xco_sb", [128, 2, 2, 5], F32)
        xcf = sb("xcf", [128, 2, 5, 8], F32)
        st8 = sb("st8", [128, 64], F32)
        sg = [sb("sg%d" % i, [128, 512], F32) for i in range(2)]

        print("SBUF bytes remaining after allocation:", nc.sbuf_bytes_remaining)
        PS = [st.enter_context(nc.psum_tensor("psb%d" % i, [128, 512], F32)) for i in range(8)]
        PSN = ["ps:%d" % i for i in range(8)]

        def mm(bank, out, lhsT, rhs, start, stop, reads):
            S.op("pe", lambda e: e.matmul(out, lhsT=lhsT, rhs=rhs, start=start, stop=stop),
                 reads=reads, writes=[PSN[bank]], inc=bool(stop))

        def tr(bank, out, in_, reads, last=True):
            S.op("pe", lambda e: e.transpose(out, in_, ident[:in_.shape[0], :in_.shape[0]]),
                 reads=list(reads) + ["ident"], writes=[PSN[bank]], inc=last)

        def act(out, in_, func, reads, writes, **kw):
            S.op("act", lambda e: e.activation(out=out, in_=in_, func=func, **kw), reads=reads, writes=writes)

        def tt(eng, out, in0, in1, op, reads, writes):
            S.op(eng, lambda e: e.tensor_tensor(out=out, in0=in0, in1=in1, op=op), reads=reads, writes=writes)

        def ts(eng, out, in0, s1, s2, op0, op1, reads, writes):
            if s2 is None:
                S.op(eng, lambda e: e.tensor_scalar(out=out, in0=in0, scalar1=s1, scalar2=None, op0=op0),
                     reads=reads, writes=writes)
            else:
                S.op(eng, lambda e: e.tensor_scalar(out=out, in0=in0, scalar1=s1, scalar2=s2, op0=op0, op1=op1),
                     reads=reads, writes=writes)

        def stt(eng, out, in0, scalar, in1, op0, op1, reads, writes):
            S.op(eng, lambda e: e.scalar_tensor_tensor(out=out, in0=in0, scalar=scalar, in1=in1, op0=op0, op1=op1),
                 reads=reads, writes=writes)

        def red(out, in_, reads, writes):
            S.op("dve", lambda e: e.tensor_reduce(out=out, in_=in_, axis=AX.X, op=ALU.add), reads=reads, writes=writes)

        def rsqrt(buf, scale, reads_writes):
            act(buf, buf, AF.Sqrt, reads=[reads_writes, "epsc"], writes=[reads_writes], scale=scale, bias=epsc[:buf.shape[0], :])
            S.op("dve", lambda e: e.reciprocal(out=buf, in_=buf), reads=[reads_writes], writes=[reads_writes])

        def ck(name):
            if stop == name:
                raise StopBuild()

        S.dma("sp", xT[:], xT_d.rearrange("(c p) t -> p c t", p=128), writes=["xT"])
        S.dma("sp", condT[:], cond_d, writes=["condT"])
        S.dma("sp", sel[:], sel_d, writes=["sel"])
        S.dma("sp", bmT[:], b_mod_d, writes=["bmT"])
        S.dma("sp", gn[:], gn_d, writes=["gn"])
        S.dma("sp", gqa[:], gqa_d, writes=["gqa"])
        S.dma("sp", idxt[:], idx_d, writes=["idxt"])
        S.dma("sp", idxq[:], idxq_d, writes=["idxq"])
        S.dma("sp", rope64[:], rope64_d, writes=["rope64"])
        S.dma("sp", rope32[:], rope32_d, writes=["rope32"])
        S.dma("sp", xco[:], xco_d, writes=["xco"])
        S.dma("pool", wm, w_mod_d.rearrange("l (c p) n -> p l c n", p=128), writes=["wm"])

        S.op("pool", lambda e: e.memset(ones[:], 1.0), writes=["ones"])
        S.op("pool", lambda e: e.memset(epsc[:], EPS), writes=["epsc"])
        S.op("pool", lambda e: e.memset(onec[:], 1.0), writes=["onec"])
        S.op("pool", lambda e: e.memset(identf[:], 0.0), writes=["identf"])
        S.op("pool", lambda e: e.affine_select(out=identf[:], in_=identf[:], pattern=[[-1, 128]],
                                               compare_op=ALU.not_equal, fill=1.0, base=0, channel_multiplier=1),
             reads=["identf"], writes=["identf"])
        S.op("dve", lambda e: e.tensor_copy(ident[:], identf[:]), reads=["identf"], writes=["ident"])

        def modulation_setup():
            ck("t0")
            act(sT[:], condT[:], AF.Silu, reads=["condT"], writes=["sT"])
            pm = PS[0][:, 0:72].rearrange("p (j l k) -> p j l k", j=12, l=2)
            for l in range(2):
                for cj in range(12):
                    for c in range(8):
                        mm(0, pm[:, cj, l, :], wm[:, l, c, cj * 128:(cj + 1) * 128], sT[:, c, :], c == 0, c == 7,
                           reads=["wm", "sT"])
            tt("dve", ml[:], pm, bcast(bmT[:].unsqueeze(3), [128, 12, 2, 3]), ALU.add, reads=[PSN[0], "bmT"], writes=["ml"])
            ck("t1")
            S.dma("sp", mb_d[:, 0:72], ml[:].rearrange("p j l k -> p (j l k)"), reads=["ml"], writes=["mb_d"])
            S.collective([mb_c], [mg_c], [[0, 1, 2, 3], [4, 5, 6, 7]], reads=["mb_d"], writes=["mg_d"])
            S.dma("sp", MODT[:].rearrange("p (r j) l k -> p r (j l k)", r=4), mg_d.rearrange("(r p) f -> p r f", p=128)[:, :, 0:72],
                  reads=["mg_d"], writes=["MODT"])
            ck("t2")
            S.op("dve", lambda e: e.tensor_copy(MS[:, :, :, 0], MODT[:, :, :, 0]), reads=["MODT"], writes=["MS0"])
            ts("dve", MS[:, :, :, 1], MODT[:, :, :, 1], sel[:, 0:1], None, ALU.mult, None, reads=["MODT", "sel"], writes=["MS1"])
            stt("dve", MS[:, :, :, 1], MODT[:, :, :, 2], sel[:, 1:2], MS[:, :, :, 1], ALU.mult, ALU.add,
                reads=["MODT", "sel", "MS1"], writes=["MS1"])
            for l in range(2):
                for grp in range(2):
                    for which, k in ((0, 1), (1, 4)):
                        stt("dve", GG[:, l, grp, which, :], MS[:, k * 8:(k + 1) * 8, l, grp], 1.0, gn[:, l, which, :],
                            ALU.add, ALU.mult, reads=["MS0", "MS1", "gn"], writes=["GG"])

        def modcol(k, c, l, grp):
            return MS[:, k * 8 + c, l, grp:grp + 1]

        GROUPS = [(0, 512, 0), (512, 1024, 0), (1024, 1280, 1)]

        def norm_phase(l, which):
            kk_sh = 0 if which == 0 else 3
            for gi, (t0, t1, grp) in enumerate(GROUPS):
                n = t1 - t0
                act(nsq[:, :, 0:n], xT[:, :, t0:t1], AF.Square, reads=["xT:%d" % gi], writes=["nsq"])
                red(s8[:, 0:n], nsq[:, :, 0:n].rearrange("p c t -> p t c"), reads=["nsq"], writes=["s8"])
                mm(7, PS[7][:, 0:n], ones[:], s8[:, 0:n], True, True, reads=["ones", "s8"])
                act(rstd[:, 0:n], PS[7][:, 0:n], AF.Sqrt, reads=[PSN[7], "epsc"], writes=["rstd"], scale=1.0 / D, bias=epsc[:])
                S.op("dve", lambda e: e.reciprocal(out=rstd[:, 0:n], in_=rstd[:, 0:n]), reads=["rstd"], writes=["rstd"])
                for c in range(8):
                    tb = ntmp[c % 3]
                    stt("dve", tb[:, 0:n], xT[:, c, t0:t1], GG[:, l, grp, which, c:c + 1], rstd[:, 0:n], ALU.mult, ALU.mult,
                        reads=["xT:%d" % gi, "GG", "rstd"], writes=["ntmp%d" % (c % 3)])
                    act(hm[:, c, t0:t1], tb[:, 0:n], AF.Identity, reads=["ntmp%d" % (c % 3), "MS0", "MS1"],
                        writes=hmg(gi), bias=modcol(kk_sh, c, l, grp), scale=1.0)


        S.alias(["xT:0", "xT:1", "xT:2"], ["xT"])

        GSEG = [[0, 1], [2, 3], [4]]

        def hmg(gi):
            return ["hm:s%d" % sg_ for sg_ in GSEG[gi]]

        def layer_tables(l):
            S.dma("sp", gsm[:], gsm_d[l].partition_broadcast(128), writes=["gsm"])
            S.dma("sp", RP[:], retp_d[l].partition_broadcast(128), writes=["RP"])
            S.dma("sp", tabs, tabs_d, writes=["tabs"])
            ts("dve", gq96[:], gsm[:, G_QN:G_QN + 96], ATT_SCALE, None, ALU.mult, None, reads=["gsm"], writes=["gq96"])
            act(LG[:].rearrange("p d h -> p (d h)"), RP[:], AF.Exp, reads=["RP"], writes=["LG"], scale=-LN2)
            act(LG[:].rearrange("p d h -> p (d h)"), LG[:].rearrange("p d h -> p (d h)"), AF.Ln, reads=["LG", "onec"], writes=["LG"],
                scale=-1.0, bias=onec[:])
            for h in range(8):
                act(dtmp[:, 0], tabs[:, 0], AF.Exp, reads=["tabs", "LG"], writes=["dtmp0"], scale=LG[:, 0, h:h + 1])
                act(dtmp[:, 1], tabs[:, 1], AF.Exp, reads=["tabs", "LG"], writes=["dtmp1"], scale=LG[:, 1, h:h + 1])
                tt("dve", dtmp[:, 0], dtmp[:, 0], tabs[:, 2], ALU.mult, reads=["dtmp0", "tabs"], writes=["dtmp0"])
                tt("dve", dtmp[:, 1], dtmp[:, 1], tabs[:, 3], ALU.mult, reads=["dtmp1", "tabs"], writes=["dtmp1"])
                tt("dve", DT[:, h], dtmp[:, 0], dtmp[:, 1], ALU.add, reads=["dtmp0", "dtmp1"], writes=["DT"])
            for d in range(2):
                for jt in range(2):
                    act(KD[:, d, jt, :], LG[:, d, :], AF.Exp, reads=["LG", "idxt"], writes=["KD"], scale=idxt[:, d, jt:jt + 1])
                for it in range(2):
                    act(KQ[:, d, it, :], LG[:, d, :], AF.Exp, reads=["LG", "idxq"], writes=["KQ"], scale=idxq[:, d, it:it + 1])
            for d in range(2):
                tt("dve", xcf[:, d], bcast(LG[:, d, :].unsqueeze(1), [128, 5, 8]), bcast(xco[:, 0, d, :].unsqueeze(2), [128, 5, 8]),
                   ALU.mult, reads=["LG", "xco"], writes=["xcf"])
            act(xcf[:].rearrange("p d s h -> p (d s h)"), xcf[:].rearrange("p d s h -> p (d s h)"), AF.Exp, reads=["xcf"], writes=["xcf"])
            for d in range(2):
                tt("dve", xcf[:, d], xcf[:, d], bcast(xco[:, 1, d, :].unsqueeze(2), [128, 5, 8]), ALU.mult,
                   reads=["xcf", "xco"], writes=["xcf"])

        def rope(out, src, tl, tab, H, nf, reads, writes, scale=None):
            sv = src.rearrange("p (h f x n) -> p h f x n", h=H, f=2, x=2)
            ov = out.rearrange("p (h f x n) -> p h f x n", h=H, f=2, x=2)
            C = bcast(tab[:, tl, 0].unsqueeze(1), [128, H, 2, nf])
            Sn = bcast(tab[:, tl, 1].unsqueeze(1), [128, H, 2, nf])
            a = fA[:, 0:H * 2 * nf].rearrange("p (h f n) -> p h f n", h=H, f=2)
            b = fB[:, 0:H * 2 * nf].rearrange("p (h f n) -> p h f n", h=H, f=2)
            x1, x2 = sv[:, :, :, 0, :], sv[:, :, :, 1, :]
            tt("dve", a, x1, C, ALU.mult, reads=reads, writes=["fA"])
            tt("dve", b, x2, Sn, ALU.mult, reads=reads, writes=["fB"])
            tt("dve", ov[:, :, :, 0, :], a, b, ALU.subtract, reads=["fA", "fB"], writes=writes)
            tt("dve", a, x1, Sn, ALU.mult, reads=reads, writes=["fA"])
            tt("dve", b, x2, C, ALU.mult, reads=reads, writes=["fB"])
            tt("dve", ov[:, :, :, 1, :], a, b, ALU.add, reads=["fA", "fB"], writes=writes)

        def seg_project(l, seg, emit_out=True):
            is_s = seg == 4
            S.alias(["q_tm", "k_tm"], ["mix_tm"])
            S.alias(["KcT", "VX"], ["Sacc", "Sin"])
            gi = 2 if is_s else seg // 2
            for tl in range(2):
                tile = seg * 2 + tl
                t0 = tile * 128
                for cg in range(5):
                    c0 = cg * 512
                    ncol = min(512, IN_COLS - c0)
                    for c in range(8):
                        mm(cg, PS[cg][:, 0:ncol], hm[:, c, t0:t0 + 128], w_in[:, c, c0:c0 + ncol], c == 0, c == 7,
                           reads=["hm:s%d" % seg, "w_in"])
                if is_s:
                    rope(fqk0, PS[0][:, :], tl, rope64, 8, 16, reads=[PSN[0], "rope64"], writes=["fqk"])
                    S.op("act", lambda e: e.copy(q_tm[:, tl, :], fqk0), reads=["fqk"], writes=["q_tm"])
                else:
                    S.op("act", lambda e: e.copy(q_tm[:, tl, :], PS[0][:, :]), reads=[PSN[0]], writes=["q_tm"])
                if is_s:
                    rope(fqk0, PS[1][:, :], tl, rope64, 8, 16, reads=[PSN[1], "rope64"], writes=["fqk"])
                    S.op("act", lambda e: e.mul(k_tm[:, tl, :], fqk0, 0.125), reads=["fqk"], writes=["k_tm"])
                else:
                    S.op("act", lambda e: e.mul(k_tm[:, tl, :], PS[1][:, :], 0.125), reads=[PSN[1]], writes=["k_tm"])
                S.op("dve", lambda e: e.tensor_copy(v_tm[:, tl, :], PS[2][:, :]), reads=[PSN[2]], writes=["v_tm"])
                act(g_tm[:, tl, :], PS[3][:, :], AF.Silu, reads=[PSN[3]], writes=["g_tm"])
                zq = PS[4]
                act(sqt[:, 0:416], zq[:, 0:416], AF.Square, reads=[PSN[4]], writes=["sqt"])
                red(st8[:, 0:1], sqt[:, 0:256], reads=["sqt"], writes=["st8"])
                red(st8[:, 1:2], sqt[:, 256:384], reads=["sqt"], writes=["st8"])
                red(st8[:, 2:3], sqt[:, 384:416], reads=["sqt"], writes=["st8"])
                act(st8[:, 0:1], st8[:, 0:1], AF.Sqrt, reads=["st8", "epsc"], writes=["st8"], scale=1.0 / 256, bias=epsc[:])
                act(st8[:, 1:2], st8[:, 1:2], AF.Sqrt, reads=["st8", "epsc"], writes=["st8"], scale=1.0 / 128, bias=epsc[:])
                act(st8[:, 2:3], st8[:, 2:3], AF.Sqrt, reads=["st8", "epsc"], writes=["st8"], scale=1.0 / 32, bias=epsc[:])
                S.op("dve", lambda e: e.reciprocal(out=st8[:, 0:3], in_=st8[:, 0:3]), reads=["st8"], writes=["st8"])
                ts("dve", qa_n[:, tl, :], zq[:, 0:256], st8[:, 0:1], None, ALU.mult, None, reads=[PSN[4], "st8"], writes=["qa_n"])
                stt("dve", ckv[:, tl, :], zq[:, 256:384], st8[:, 1:2], gsm[:, G_KVA:G_KVA + 128], ALU.mult, ALU.mult,
                    reads=[PSN[4], "st8", "gsm"], writes=["ckv"])
                if is_s:
                    stt("dve", fB[:, 256:288], zq[:, 384:416], st8[:, 2:3], gsm[:, G_KR:G_KR + 32], ALU.mult, ALU.mult,
                        reads=[PSN[4], "st8", "gsm"], writes=["fkr"])
                    rope(kr[:, tl, :], fB[:, 256:288], tl, rope32, 1, 8, reads=["fkr", "rope32"], writes=["kr"])
                else:
                    stt("dve", kr[:, tl, :], zq[:, 384:416], st8[:, 2:3], gsm[:, G_KR:G_KR + 32], ALU.mult, ALU.mult,
                        reads=[PSN[4], "st8", "gsm"], writes=["kr"])
            if not emit_out:
                return
            if is_s:
                S.dma("sp", kvb_d[l][:, 0:128].rearrange("(t p) n -> p t n", p=128), ckv[:], reads=["ckv"], writes=["kvb_d%d" % l])
                S.dma("sp", kvb_d[l][:, 128:160].rearrange("(t p) n -> p t n", p=128), kr[:], reads=["kr"], writes=["kvb_d%d" % l])
            else:
                S.dma("sp", ockv_d[l, seg * 256:(seg + 1) * 256, :].rearrange("(t p) n -> p t n", p=128), ckv[:], reads=["ckv"], writes=["ockv"])
                S.dma("sp", okr_d[l, seg * 256:(seg + 1) * 256, :].rearrange("(t p) n -> p t n", p=128), kr[:], reads=["kr"], writes=["okr"])

        fqk0 = sqt[:, 0:512]

        def seg_q(l, seg):
            is_s = seg == 4
            for tl in range(2):
                pt = PS[5][:, 0:128].bitcast(BF16).rearrange("p (c i) -> p c i", c=2)
                for c in range(2):
                    tr(5, pt[:, c, :], qa_n[:, tl, c * 128:(c + 1) * 128], reads=["qa_n"], last=(c == 1))
                for c in range(2):
                    act(qaT[:, c, tl * 128:(tl + 1) * 128], pt[:, c, :], AF.Copy, reads=[PSN[5], "gqa"], writes=["qaT"],
                        scale=gqa[:, l, c:c + 1])
                for half in range(2):
                    for c in range(2):
                        mm(6 + half, PS[6 + half][:, 0:384], qaT[:, c, tl * 128:(tl + 1) * 128],
                           w_qb[:, c, half * 384:(half + 1) * 384], c == 0, c == 1, reads=["qaT", "w_qb"])
                for half in range(2):
                    qv = PS[6 + half][:, 0:384].rearrange("p (h e) -> p h e", h=4)
                    sq = sqt[:, half * 384:(half + 1) * 384].rearrange("p (h e) -> p h e", h=4)
                    act(sq, qv, AF.Square, reads=[PSN[6 + half]], writes=["sqt"])
                    red(st8[:, 8 + half * 4:12 + half * 4], sq[:, :, 0:64], reads=["sqt"], writes=["st8q"])
                    red(st8[:, 16 + half * 4:20 + half * 4], sq[:, :, 64:96], reads=["sqt"], writes=["st8q"])
                act(st8[:, 8:16], st8[:, 8:16], AF.Sqrt, reads=["st8q", "epsc"], writes=["st8q"], scale=1.0 / 64, bias=epsc[:])
                act(st8[:, 16:24], st8[:, 16:24], AF.Sqrt, reads=["st8q", "epsc"], writes=["st8q"], scale=1.0 / 32, bias=epsc[:])
                S.op("dve", lambda e: e.reciprocal(out=st8[:, 8:24], in_=st8[:, 8:24]), reads=["st8q"], writes=["st8q"])
                for half in range(2):
                    qv = PS[6 + half][:, 0:384].rearrange("p (h e) -> p h e", h=4)
                    dst = fA[:, 0:384].rearrange("p (h e) -> p h e", h=4) if half == 0 else fB[:, 0:384].rearrange("p (h e) -> p h e", h=4)
                    nm = "fA" if half == 0 else "fB"
                    tt("dve", dst[:, :, 0:64], qv[:, :, 0:64], bcast(st8[:, 8 + half * 4:12 + half * 4].unsqueeze(2), [128, 4, 64]),
                       ALU.mult, reads=[PSN[6 + half], "st8q"], writes=[nm])
                    tt("dve", dst[:, :, 64:96], qv[:, :, 64:96], bcast(st8[:, 16 + half * 4:20 + half * 4].unsqueeze(2), [128, 4, 32]),
                       ALU.mult, reads=[PSN[6 + half], "st8q"], writes=[nm])
                    if not is_s:
                        tt("dve", Qc_tm[:, tl, half * 4:(half + 1) * 4, :], dst, bcast(gq96[:].unsqueeze(1), [128, 4, 96]), ALU.mult,
                           reads=[nm, "gq96"], writes=["Qc_tm"])
                    else:
                        tt("dve", dst, dst, bcast(gq96[:].unsqueeze(1), [128, 4, 96]), ALU.mult, reads=[nm, "gq96"], writes=[nm])
                        S.op("act", lambda e: e.copy(Qc_tm[:, tl, half * 4:(half + 1) * 4, 0:64], dst[:, :, 0:64]), reads=[nm], writes=["Qc_tm"])
                        S.op("act", lambda e: e.copy(sqt[:, 0:128].rearrange("p (h e) -> p h e", h=4), dst[:, :, 64:96]), reads=[nm], writes=["sqtr"])
                        rope(sqt[:, 128:256], sqt[:, 0:128], tl, rope32, 4, 8, reads=["sqtr", "rope32"], writes=["sqtr2"])
                        S.op("act", lambda e: e.copy(Qc_tm[:, tl, half * 4:(half + 1) * 4, 64:96],
                                                     sqt[:, 128:256].rearrange("p (h e) -> p h e", h=4)), reads=["sqtr2"], writes=["Qc_tm"])
                pq = PS[5][:, :].bitcast(BF16).rearrange("p (h i) -> p h i", h=8)
                for h in range(8):
                    tr(5, pq[0:96, h, :], Qc_tm[:, tl, h, :], reads=["Qc_tm"], last=(h == 7))
                S.op("act", lambda e: e.copy(QcT[0:96, :, tl * 128:(tl + 1) * 128], pq[0:96, :, :]), reads=[PSN[5]], writes=["QcT"])

        def mla_up(src_ckv, src_kr, KT_dst, VX_dst, reads, kt_name, vx_name):
            S.op("dve", lambda e: e.tensor_copy(ckvb, src_ckv), reads=reads, writes=["ckvb"])
            ck("m0a")
            pt = PS[5][:, 0:64].bitcast(BF16)
            tr(5, pt, ckvb, reads=["ckvb"])
            ck("m0b")
            S.op("act", lambda e: e.copy(ckvT, pt), reads=[PSN[5]], writes=["ckvT"])
            ck("m1")
            for half in range(2):
                mm(6 + half, PS[6 + half][:, :], ckvT, w_kvb[:, half * 512:(half + 1) * 512], True, True, reads=["ckvT", "w_kvb"])
            for half in range(2):
                kv = PS[6 + half][:, :].rearrange("p (h e) -> p h e", h=4)
                sq = sqt[:, half * 256:(half + 1) * 256].rearrange("p (h e) -> p h e", h=4)
                act(sq, kv[:, :, 0:64], AF.Square, reads=[PSN[6 + half]], writes=["sqt"])
                red(st8[:, 24 + half * 4:28 + half * 4], sq, reads=["sqt"], writes=["st8k"])
                S.op("act", lambda e: e.copy(VX_dst[:, half * 4:(half + 1) * 4, 0:64], kv[:, :, 64:128]), reads=[PSN[6 + half]], writes=[vx_name])
            ck("m2")
            act(st8[:, 24:32], st8[:, 24:32], AF.Sqrt, reads=["st8k", "epsc"], writes=["st8k"], scale=1.0 / 64, bias=epsc[:])
            S.op("dve", lambda e: e.reciprocal(out=st8[:, 24:32], in_=st8[:, 24:32]), reads=["st8k"], writes=["st8k"])
            for half in range(2):
                kv = PS[6 + half][:, :].rearrange("p (h e) -> p h e", h=4)
                tmp = fA[:, 0:256].rearrange("p (h e) -> p h e", h=4) if half == 0 else fB[:, 0:256].rearrange("p (h e) -> p h e", h=4)
                nm = "fA" if half == 0 else "fB"
                tt("dve", tmp, kv[:, :, 0:64], bcast(st8[:, 24 + half * 4:28 + half * 4].unsqueeze(2), [128, 4, 64]), ALU.mult,
                   reads=[PSN[6 + half], "st8k"], writes=[nm])
                tt("dve", Kc_tm[:, half * 4:(half + 1) * 4, 0:64], tmp, bcast(gsm[:, G_KN:G_KN + 64].unsqueeze(1), [128, 4, 64]), ALU.mult,
                   reads=[nm, "gsm"], writes=["Kc_tm"])
            ck("m3")
            S.op("act", lambda e: e.copy(Kc_tm[:, :, 64:96], bcast(src_kr.unsqueeze(1), [128, 8, 32])), reads=reads, writes=["Kc_tm"])
            ck("m4")
            S.op("dve", lambda e: e.memset(VX_dst[:, :, 64:65], 1.0), writes=[vx_name])
            ck("m5")
            pk = PS[5][:, :].bitcast(BF16).rearrange("p (h i) -> p h i", h=8)
            for h in range(8):
                tr(5, pk[0:96, h, :], Kc_tm[:, h, :], reads=["Kc_tm"], last=(h == 7))
            S.op("act", lambda e: e.copy(KT_dst, pk[0:96, :, :]), reads=[PSN[5]], writes=[kt_name])

        def seg_retention(l, seg, emit_out=True):
            is_s = seg == 4
            for w, (src, nm) in enumerate(((q_tm, "q_tm"), (k_tm, "k_tm"))):
                for tl in range(2):
                    pt = PS[5][:, 0:256].bitcast(BF16).rearrange("p (c i) -> p c i", c=4)
                    for c in range(4):
                        tr(5, pt[:, c, :], src[:, tl, c * 128:(c + 1) * 128], reads=[nm], last=(c == 3))
                    S.op("act", lambda e: e.copy(qkT[:, w, :, tl * 128:(tl + 1) * 128], pt), reads=[PSN[5]], writes=["qkT"])
            for d in range(2):
                for tl in range(2):
                    tt("dve", kdec[:, d, tl, :].rearrange("p (h e) -> p h e", h=8), k_tm[:, tl, :].rearrange("p (h e) -> p h e", h=8),
                       bcast(KD[:, d, tl, :].unsqueeze(2), [128, 8, 64]), ALU.mult, reads=["k_tm", "KD"], writes=["kdec"])
            for d, bank in ((0, 4), (1, 6)):
                for pr in range(4):
                    for tl in range(2):
                        mm(bank, PS[bank][:, pr * 128:(pr + 1) * 128], kdec[:, d, tl, pr * 128:(pr + 1) * 128],
                           v_tm[:, tl, pr * 128:(pr + 1) * 128], tl == 0, tl == 1, reads=["kdec", "v_tm"])
                S.op("act", lambda e: e.copy(Ust[:, d, :], PS[bank][:, :]), reads=[PSN[bank]], writes=["Ust"])
            Uv = Ust[:].rearrange("p d (r e) -> p d r e", r=4)
            if not emit_out:
                return
            if is_s:
                dst = ub_d[l].rearrange("(d r x k) v -> k d r x v", d=2, r=4, x=2)
                for d in range(2):
                    S.dma("sp", dst[:, d, :, 0, :], Uv[0:64, d, :, 0:64], reads=["Ust"], writes=["ub_d%d" % l])
                    S.dma("sp", dst[:, d, :, 1, :], Uv[64:128, d, :, 64:128], reads=["Ust"], writes=["ub_d%d" % l])
            else:
                for d, od in ((0, osf_d), (1, osb_d)):
                    dst = od[l, seg].rearrange("(r x) k v -> k r x v", x=2)
                    S.dma("sp", dst[:, :, 0, :], Uv[0:64, d, :, 0:64], reads=["Ust"], writes=["ost"])
                    S.dma("sp", dst[:, :, 1, :], Uv[64:128, d, :, 64:128], reads=["Ust"], writes=["ost"])

        def seg_retention_out(l, seg):
            is_s = seg == 4
            S.alias(["mix_tm"], ["q_tm", "k_tm"])
            for h in range(8):
                c, po = h // 2, (h % 2) * 64
                bank = h % 2
                pst = PS[bank][:, :].rearrange("p (j i) -> p j i", j=2)
                for jt in range(2):
                    mm(bank, pst[:, jt, :], qkT[po:po + 64, 1, c, jt * 128:(jt + 1) * 128], qkT[po:po + 64, 0, c, :], True, True,
                       reads=["qkT"])
                at = AT[h % 2]
                tt("dve", at, pst, DT[:, h], ALU.mult, reads=[PSN[bank], "DT"], writes=["AT%d" % (h % 2)])
                for it in range(2):
                    ob = 2 + it
                    o = PS[ob][:, h * 64:(h + 1) * 64]
                    for jt in range(2):
                        mm(ob, o, at[:, jt, it * 128:(it + 1) * 128], v_tm[:, jt, h * 64:(h + 1) * 64], jt == 0, jt == 1,
                           reads=["AT%d" % (h % 2), "v_tm"])
                    if is_s:
                        for d in range(2):
                            cb = 4 + it * 2 + d
                            mm(cb, PS[cb][:, h * 64:(h + 1) * 64], qkT[po:po + 64, 0, c, it * 128:(it + 1) * 128], Sin[po:po + 64, d, h, :],
                               True, True, reads=["qkT", "Sin"])
            for it in range(2):
                ob = 2 + it
                ov = PS[ob][:, :].rearrange("p (h e) -> p h e", h=8)
                ovn = [PSN[ob]]
                if is_s:
                    osb = sqt[:, 0:512].rearrange("p (h e) -> p h e", h=8)
                    for d in range(2):
                        cb = 4 + it * 2 + d
                        tmp = (fA if d == 0 else fB)[:, 0:512].rearrange("p (h e) -> p h e", h=8)
                        tt("dve", tmp, PS[cb][:, :].rearrange("p (h e) -> p h e", h=8), bcast(KQ[:, d, it, :].unsqueeze(2), [128, 8, 64]),
                           ALU.mult, reads=[PSN[cb], "KQ"], writes=["fA" if d == 0 else "fB"])
                    tt("dve", osb, ov, fA[:, 0:512].rearrange("p (h e) -> p h e", h=8), ALU.add, reads=[PSN[ob], "fA"], writes=["sqt"])
                    tt("dve", osb, osb, fB[:, 0:512].rearrange("p (h e) -> p h e", h=8), ALU.add, reads=["sqt", "fB"], writes=["sqt"])
                    ov = osb
                    ovn = ["sqt"]
                red(st8[:, 32:40], ov, reads=ovn, writes=["st8g"])
                ts("dve", st8[:, 32:40], st8[:, 32:40], 1.0 / 64, None, ALU.mult, None, reads=["st8g"], writes=["st8g"])
                dv_ = fA[:, 0:512].rearrange("p (h e) -> p h e", h=8)
                tt("dve", dv_, ov, bcast(st8[:, 32:40].unsqueeze(2), [128, 8, 64]), ALU.subtract, reads=ovn + ["st8g"], writes=["fA"])
                sq = fB[:, 0:512].rearrange("p (h e) -> p h e", h=8)
                act(sq, dv_, AF.Square, reads=["fA"], writes=["fB"])
                red(st8[:, 40:48], sq, reads=["fB"], writes=["st8h"])
                act(st8[:, 40:48], st8[:, 40:48], AF.Sqrt, reads=["st8h", "epsc"], writes=["st8h"], scale=1.0 / 64, bias=epsc[:])
                S.op("dve", lambda e: e.reciprocal(out=st8[:, 40:48], in_=st8[:, 40:48]), reads=["st8h"], writes=["st8h"])
                tt("dve", dv_, dv_, bcast(st8[:, 40:48].unsqueeze(2), [128, 8, 64]), ALU.mult, reads=["fA", "st8h"], writes=["fA"])
                tt("dve", fA[:, 0:512], fA[:, 0:512], gsm[:, G_GN:G_GN + 512], ALU.mult, reads=["fA", "gsm"], writes=["fA"])
                tt("dve", fA[:, 0:512], fA[:, 0:512], gsm[:, B_GN:B_GN + 512], ALU.add, reads=["fA", "gsm"], writes=["fA"])
                tt("dve", mix_tm[:, it, 0:512], fA[:, 0:512], g_tm[:, it, :], ALU.mult, reads=["fA", "g_tm"], writes=["mix_tm"])

        def seg_attention(l, seg, nkt, KT, VXs, kt_name, vx_name):
            for hg in range(2):
                for hh in range(4):
                    h = hg * 4 + hh
                    for kt in range(nkt):
                        sbk = (h * nkt + kt) % 2
                        ps_ = PS[sbk][:, 0:256]
                        mm(sbk, ps_, KT[0:96, h, kt * 128:(kt + 1) * 128], QcT[0:96, h, :], True, True, reads=[kt_name, "QcT"])
                        p = PT[(h * nkt + kt) % 2]
                        pn = "PT%d" % ((h * nkt + kt) % 2)
                        act(p, ps_, AF.Exp, reads=[PSN[sbk]], writes=[pn])
                        for it in range(2):
                            ob = 2 + it if hg == 0 else 4 + 2 * it
                            mm(ob, PS[ob][:, hh * 65:(hh + 1) * 65], p[:, it * 128:(it + 1) * 128], VXs(kt)[:, h, :], kt == 0, kt == nkt - 1,
                               reads=[pn, vx_name])
            for hg in range(2):
                for it in range(2):
                    ob = 2 + it if hg == 0 else 4 + 2 * it
                    ov = PS[ob][:, 0:260].rearrange("p (h e) -> p h e", h=4)
                    S.op("dve", lambda e: e.reciprocal(out=st8[:, 48:52], in_=ov[:, :, 64]), reads=[PSN[ob]], writes=["st8a"])
                    tt("dve", mix_tm[:, it, 512 + hg * 256:512 + (hg + 1) * 256].rearrange("p (h e) -> p h e", h=4), ov[:, :, 0:64],
                       bcast(st8[:, 48:52].unsqueeze(2), [128, 4, 64]), ALU.mult, reads=[PSN[ob], "st8a"], writes=["mix_tm"])

        def seg_mix_out(seg):
            gi = 2 if seg == 4 else seg // 2
            for tl in range(2):
                t0 = (seg * 2 + tl) * 128
                pt = PS[7][:, :].bitcast(BF16).rearrange("p (c i) -> p c i", c=8)
                for c in range(8):
                    tr(7, pt[:, c, :], mix_tm[:, tl, c * 128:(c + 1) * 128], reads=["mix_tm"], last=(c == 7))
                S.op("act", lambda e: e.copy(hm[:, :, t0:t0 + 128], pt), reads=[PSN[7]], writes=["hm:s%d" % seg])

        def sample_exchange(l):
            S.collective([kvb_c[l]], [kvg_c[l]], [[0, 1, 2, 3], [4, 5, 6, 7]], reads=["kvb_d%d" % l], writes=["kvg_d%d" % l])
            S.collective([ub_c[l]], [ug_c[l]], [[0, 1, 2, 3], [4, 5, 6, 7]], reads=["ub_d%d" % l], writes=["ug_d%d" % l])

        def sample_states(l):
            ugv = ug_d[l].rearrange("(s d h k) v -> k s d h v", s=4, d=2, h=8)
            for d in range(2):
                for s_ in range(5):
                    if s_ < 4:
                        S.dma("sp", Ug[0:64], ugv[:, s_, d], reads=["ug_d%d" % l], writes=["Ug"])
                    else:
                        S.dma("sp", Ug[0:64], s0_d[l, d].rearrange("h k v -> k h v"), writes=["Ug"])
                    cf = bcast(xcf[0:64, d, s_, :].unsqueeze(2), [64, 8, 64])
                    if s_ == 0:
                        tt("dve", Sacc[0:64, d], Ug[0:64], cf, ALU.mult, reads=["Ug", "xcf"], writes=["Sacc"])
                    else:
                        tt("dve", fA[0:64, 0:512].rearrange("p (h e) -> p h e", h=8), Ug[0:64], cf, ALU.mult,
                           reads=["Ug", "xcf"], writes=["fA"])
                        tt("dve", Sacc[0:64, d], Sacc[0:64, d], fA[0:64, 0:512].rearrange("p (h e) -> p h e", h=8), ALU.add,
                           reads=["Sacc", "fA"], writes=["Sacc"])
            S.op("act", lambda e: e.copy(Sin[0:64], Sacc[0:64]), reads=["Sacc"], writes=["Sin"])
            S.dma("sp", Sin[64:128], Sin[0:64], reads=["Sin"], writes=["Sin"])

        def sample_keys(l):
            S.alias(["sKT", "sVX"], ["w_in"])
            kg = kvg_d[l].rearrange("(t p) n -> t p n", p=128)
            for kt in range(10):
                if kt < 8:
                    S.dma("sp", kvin[:, :], kg[kt], reads=["kvg_d%d" % l], writes=["kvin"])
                else:
                    S.dma("sp", kvin[:, 0:128], cckv_d[l, (kt - 8) * 128:(kt - 7) * 128, :], writes=["kvin"])
                    S.dma("sp", kvin[:, 128:160], ckr_d[l, (kt - 8) * 128:(kt - 7) * 128, :], writes=["kvin"])
                mla_up(kvin[:, 0:128], kvin[:, 128:160], sKT[0:96, :, kt * 128:(kt + 1) * 128], sVX[:, kt], ["kvin"], "sKT", "sVX")

        def phase_wo(l):
            for gi, (t0, t1, grp) in enumerate(GROUPS):
                n = t1 - t0
                for j in range(8):
                    bank = j % 4
                    for c in range(8):
                        mm(bank, PS[bank][:, 0:n], w_o[:, c, j * 128:(j + 1) * 128], hm[:, c, t0:t1], c == 0, c == 7,
                           reads=["w_o"] + hmg(gi))
                    stt("dve", xT[:, j, t0:t1], PS[bank][:, 0:n], modcol(2, j, l, grp), xT[:, j, t0:t1], ALU.mult, ALU.add,
                        reads=[PSN[bank], "MS0", "MS1", "xT:%d" % gi], writes=["xT:%d" % gi])

        def phase_ffn(l):
            S.alias(["actT"], ["DT", "wm"] + SEGN + NRMN)
            S.alias(["wfi0", "wfi1"], ["w_o"])
            S.alias(["w_fo"], ["w_in", "w_qb", "w_kvb", "sKT", "sVX"])
            nblk = 22
            for j in range(nblk):
                buf = j % 2
                wv = w_fi[buf]
                S.dma("pool", wv[:, :, 0:128], w_fi_d[l, :, j * 128:(j + 1) * 128].rearrange("(c p) n -> p c n", p=128), writes=["wfi%d" % buf])
                S.dma("pool", wv[:, :, 128:256], w_fi_d[l, :, DFF + j * 128:DFF + (j + 1) * 128].rearrange("(c p) n -> p c n", p=128),
                      writes=["wfi%d" % buf])
                if j == 2:
                    S.dma("pool", w_fo, w_fo_d[l].rearrange("(c p) n -> p c n", p=128), writes=["w_fo"])
                for gi, (t0, t1, grp) in enumerate(GROUPS):
                    n = t1 - t0
                    pb = ((j * 3 + gi) % 2) * 2
                    for half in range(2):
                        for c in range(8):
                            mm(pb + half, PS[pb + half][:, 0:n], wv[:, c, half * 128:(half + 1) * 128], hm[:, c, t0:t1], c == 0, c == 7,
                               reads=["wfi%d" % buf] + hmg(gi))
                    sgb = sg[(j * 3 + gi) % 2]
                    sgn = "sg%d" % ((j * 3 + gi) % 2)
                    act(sgb[:, 0:n], PS[pb][:, 0:n], AF.Silu, reads=[PSN[pb]], writes=[sgn])
                    tt("dve", actT[:, j, t0:t1], sgb[:, 0:n], PS[pb + 1][:, 0:n], ALU.mult, reads=[sgn, PSN[pb + 1]], writes=["actT"])
            for gi, (t0, t1, grp) in enumerate(GROUPS):
                n = t1 - t0
                for j in range(8):
                    bank = 4 + (j % 4)
                    for c in range(22):
                        mm(bank, PS[bank][:, 0:n], w_fo[:, c, j * 128:(j + 1) * 128], actT[:, c, t0:t1], c == 0, c == 21,
                           reads=["w_fo", "actT"])
                    stt("dve", xT[:, j, t0:t1], PS[bank][:, 0:n], modcol(5, j, l, grp), xT[:, j, t0:t1], ALU.mult, ALU.add,
                        reads=[PSN[bank], "MS0", "MS1", "xT:%d" % gi], writes=["xT:%d" % gi])

        def main_program():
            modulation_setup()
            S.alias(["DT"] + NRMN, ["wm"])
            ck("s0")
            for l in range(2):
                if l > 0:
                    S.alias(["w_in", "w_qb", "w_kvb"], ["w_fo"])
                    S.alias(["w_o"], ["wfi0", "wfi1"])
                    S.alias(["DT"] + NRMN, ["actT"])
                S.dma("pool", w_in, w_in_d[l].rearrange("(c p) n -> p c n", p=128), writes=["w_in"])
                S.dma("pool", w_qb, w_qb_d[l].rearrange("(c p) n -> p c n", p=128), writes=["w_qb"])
                S.dma("pool", w_kvb, w_kvb_d[l], writes=["w_kvb"])
                S.dma("pool", w_o, w_o_d[l].rearrange("(c p) n -> p c n", p=128), writes=["w_o"])
                ck("s1")
                layer_tables(l)
                ck("s2")
                norm_phase(l, 0)
                ck("a0")
                if stop == 1:
                    dbg_dump(S, "hm", hm[:], [128, 8, TOK], BF16, ["hm:s%d" % i for i in range(5)])
                    break
                S.alias(SEGN, NRMN)
                seg_project(l, 4)
                ck("a1")
                seg_retention(l, 4)
                ck("a2")
                sample_exchange(l)
                ck("a3")
                for seg in range(4):
                    seg_project(l, seg)
                    seg_retention(l, seg)
                    ck("b1")
                    seg_retention_out(l, seg)
                    ck("b2")
                    seg_q(l, seg)
                    ck("b3")
                    for tl in range(2):
                        mla_up(ckv[:, tl, :], kr[:, tl, :], KcT[0:96, :, tl * 128:(tl + 1) * 128], VX[:, tl], ["ckv", "kr"], "KcT", "VX")
                    ck("b4")
                    seg_attention(l, seg, 2, KcT, lambda kt: VX[:, kt], "KcT", "VX")
                    ck("b5")
                    seg_mix_out(seg)
                    ck("b6")
                seg_project(l, 4, emit_out=False)
                seg_retention(l, 4, emit_out=False)
                S.alias(["Sacc", "Sin"], ["KcT", "VX"])
                ck("c0")
                sample_states(l)
                ck("c1")
                seg_retention_out(l, 4)
                seg_q(l, 4)
                ck("c2")
                sample_keys(l)
                ck("c3")
                seg_attention(l, 4, 10, sKT, lambda kt: sVX[:, kt], "sKT", "sVX")
                seg_mix_out(4)
                if stop == 2:
                    dbg_dump(S, "mixT", hm[:], [128, 8, TOK], BF16, ["hm:s%d" % i for i in range(5)])
                    break
                phase_wo(l)
                S.alias(NRMN, SEGN)
                norm_phase(l, 1)
                phase_ffn(l)
                if stop == 3:
                    break


        try:
            main_program()
        except StopBuild:
            pass

        for gi, (t0, t1, grp) in enumerate(GROUPS):
            S.dma("sp", yT_d.rearrange("(c p) t -> p c t", p=128)[:, :, t0:t1], xT[:, :, t0:t1], reads=["xT:%d" % gi], writes=["yT_d%d" % gi])
        S.wait_all("sp")
        print("program built: waits=%d counts=%s dmas=%s r3=%d" % (S.n_wait, S.cnt, S.dcnt, r3o[0]))
    return nc, dbg_out


def _rope_tab(pos_row, pos_col, nf):
    inv = (10000.0 ** (-np.arange(nf, dtype=np.float64) / nf))
    ar = pos_row[:, None].astype(np.float64) * inv
    ac = pos_col[:, None].astype(np.float64) * inv
    C = np.stack([np.cos(ar), np.cos(ac)], 1)
    Sn = np.stack([np.sin(ar), np.sin(ac)], 1)
    return np.stack([C, Sn], 1).astype(np.float32)


def make_inputs(inp):
    f = lambda a: np.ascontiguousarray(np.asarray(a, dtype=np.float32))
    x_prompt, x_sample = f(inp["x_prompt"]), f(inp["x_sample"])
    shared = {k: f(inp[k]) for k in ("w_in", "w_q_b", "w_kv_b", "w_o", "w_ffn_in", "w_ffn_out")}
    w_mod, b_mod = f(inp["w_mod"]), f(inp["b_mod"])
    conds = np.stack([f(inp["c_ctx"]), f(inp["c"])[0], f(inp["c"])[1]], 0)
    condT = f(conds.reshape(3, 8, 128).transpose(2, 1, 0))
    gnT = f(np.stack([f(inp["g_norm_mix"]), f(inp["g_norm_ffn"])], 1).reshape(2, 2, 8, 128).transpose(3, 0, 1, 2))
    gqaT = f(f(inp["g_q_a"]).reshape(2, 2, 128).transpose(2, 0, 1))
    gsm = f(np.concatenate([f(inp[k]) for k in ("g_kv_a", "g_qn", "g_qr", "g_kn", "g_kr", "g_ret_gn", "b_ret_gn")], 1))
    assert gsm.shape == (2, GSM_N)
    retp = f(np.concatenate([f(inp["ret_p_fwd"]), f(inp["ret_p_bwd"])], 1))
    j = np.arange(128)[:, None, None] + 128 * np.arange(2)[None, :, None]
    i = np.arange(256)[None, None, :]
    diff = (i - j).astype(np.float32)
    tabs = f(np.stack([np.maximum(diff, 0), np.maximum(-diff, 0), (diff >= 0).astype(np.float32),
                       (diff <= 0).astype(np.float32)], 1))
    jj = np.arange(128)[:, None] + 128 * np.arange(2)[None, :]
    idxt = f(np.stack([255.0 - jj, jj.astype(np.float64)], 1))
    idxq = f(np.stack([jj + 1.0, 256.0 - jj], 1))
    maps = []
    for core in range(NCORES):
        r, seq = core % 4, core // 4
        xs = np.concatenate([x_prompt[4 * core + b] for b in range(4)] + [x_sample[seq, 256 * r:256 * (r + 1)]], 0)
        n = 256 * r + np.arange(256)
        sel = np.zeros((128, 2), np.float32)
        sel[:, seq] = 1.0
        xco = np.zeros((2, 2, 5), np.float32)
        for s_ in range(4):
            if s_ < r:
                xco[0, 0, s_] = 256.0 * (r - 1 - s_); xco[1, 0, s_] = 1.0
            if s_ > r:
                xco[0, 1, s_] = 256.0 * (s_ - r - 1); xco[1, 1, s_] = 1.0
        xco[0, 0, 4] = 256.0 * r; xco[1, 0, 4] = 1.0
        xco[0, 1, 4] = 256.0 * (3 - r); xco[1, 1, 4] = 1.0
        m = dict(shared)
        m.update({
            "xT": f(xs.T),
            "w_mod_sh": f(w_mod[:, :, 1536 * r:1536 * (r + 1)]),
            "b_mod_sh": f(b_mod[:, 1536 * r:1536 * (r + 1)].reshape(2, 12, 128).transpose(2, 1, 0)),
            "condT": condT, "sel": sel, "gnT": gnT, "gqaT": gqaT, "gsm": gsm, "retp": retp,
            "c_ckv": f(inp["cache_ckv"])[seq], "c_kr": f(inp["cache_krope"])[seq],
            "s0": f(np.stack([f(inp["state_ret_fwd"])[seq], f(inp["state_ret_bwd"])[seq]], 1)),
            "tabs": tabs, "idxt": idxt, "idxq": idxq,
            "rope64": f(_rope_tab(n // 64, n % 64, 16).reshape(2, 128, 2, 2, 16).transpose(1, 0, 2, 3, 4)),
            "rope32": f(_rope_tab(n // 64, n % 64, 8).reshape(2, 128, 2, 2, 8).transpose(1, 0, 2, 3, 4)),
            "xcoef": f(np.broadcast_to(xco[None], (128, 2, 2, 5))),
        })
        maps.append(m)
    return maps


_CACHE = {}


def run(inp, stop=99, dbg=()):
    key = (stop, tuple(dbg))
    if key not in _CACHE:
        _CACHE[key] = build_program(stop, dbg)
    nc, dbg_out = _CACHE[key]
    maps = make_inputs(inp)
    res = run_bass_kernel_spmd(nc, maps, core_ids=list(range(NCORES)))
    return res.results


def kernel(**inp):
    R = run(inp)
    y_prompt = np.zeros((32, 256, D), np.float32)
    y_sample = np.zeros((2, 1024, D), np.float32)
    new_ckv = np.zeros((32, 2, 256, 128), np.float32)
    new_kr = np.zeros((32, 2, 256, 32), np.float32)
    new_sf = np.zeros((32, 2, 8, 64, 64), np.float32)
    new_sb = np.zeros((32, 2, 8, 64, 64), np.float32)
    for core in range(NCORES):
        r, seq = core % 4, core // 4
        o = R[core]
        y = np.asarray(o["yT"]).T
        y_prompt[4 * core:4 * core + 4] = y[0:1024].reshape(4, 256, D)
        y_sample[seq, 256 * r:256 * (r + 1)] = y[1024:1280]
        new_ckv[4 * core:4 * core + 4] = np.asarray(o["o_ckv"]).reshape(2, 4, 256, 128).transpose(1, 0, 2, 3)
        new_kr[4 * core:4 * core + 4] = np.asarray(o["o_kr"]).reshape(2, 4, 256, 32).transpose(1, 0, 2, 3)
        new_sf[4 * core:4 * core + 4] = np.asarray(o["o_sf"]).transpose(1, 0, 2, 3, 4)
        new_sb[4 * core:4 * core + 4] = np.asarray(o["o_sb"]).transpose(1, 0, 2, 3, 4)
    return (y_prompt, y_sample, new_ckv, new_kr, new_sf, new_sb)
```

```python
import os
import numpy as np
from contextlib import ExitStack
import concourse.bass as bass
import concourse.mybir as mybir
from concourse.bass_utils import run_bass_kernel_spmd

F32 = mybir.dt.float32
BF16 = mybir.dt.bfloat16
ALU = mybir.AluOpType
AF = mybir.ActivationFunctionType
AX = mybir.AxisListType

N_DMA_SEMS = 12
NCORES = 8
D = 1024
TOK = 1280
NTILE = 10
IN_COLS = 2464
DFF = 2816
EPS = 1e-6
LN2 = float(np.log(2.0))
ATT_SCALE = float(96 ** -0.5)

G_KVA, G_QN, G_QR, G_KN, G_KR, G_GN, B_GN = 0, 128, 192, 224, 288, 320, 832
GSM_N = 1344


class StopBuild(Exception):
    pass


class Sync:
    def __init__(self, nc, stack):
        self.nc = nc
        self.stack = stack
        self.engs = {"pe": nc.tensor, "act": nc.scalar, "dve": nc.vector,
                     "pool": nc.gpsimd, "sp": nc.sync}
        self.sem = {k: stack.enter_context(nc.semaphore("sem_" + k)) for k in self.engs}
        self.cnt = {k: 0 for k in self.engs}
        self.seen = {k: {} for k in self.engs}
        self.dsem = {q: [stack.enter_context(nc.semaphore("dsem_%s%d" % (q, i)))
                         for i in range(N_DMA_SEMS)] for q in ("sp", "pool")}
        self.dcnt = {q: 0 for q in self.dsem}
        self.ccsem = []
        self.res = {}
        self.n_wait = 0

    def _semof(self, kind, a):
        if kind == "eng":
            return self.sem[a]
        if kind == "dma":
            return self.dsem[a[0]][a[1]]
        return self.ccsem[a]

    def _wait(self, e, tok):
        if tok is None:
            return
        kind, a, v = tok
        key = (kind, a)
        if self.seen[e].get(key, 0) >= v:
            return
        self.seen[e][key] = v
        self.engs[e].wait_ge(self._semof(kind, a), v)
        self.n_wait += 1

    def _deps(self, e, reads, writes, is_dma):
        deps = []
        for r in reads:
            st = self.res.get(r)
            if st and st["w"] is not None:
                deps.append(st["w"])
        for w in writes:
            st = self.res.get(w)
            if st:
                for t in [st["w"]] + list(st["r"].values()):
                    if t is None:
                        continue
                    if (not is_dma) and t[0] == "eng" and t[1] == e:
                        continue
                    deps.append(t)
        for t in deps:
            self._wait(e, t)

    def _update(self, tok, reads, writes):
        for r in reads:
            st = self.res.setdefault(r, {"w": None, "r": {}})
            st["r"][(tok[0], tok[1])] = tok
        for w in writes:
            self.res[w] = {"w": tok, "r": {}}

    @staticmethod
    def _split(reads, writes):
        writes = list(writes) + [r for r in reads if r.startswith("ps:")]
        reads = [r for r in reads if not r.startswith("ps:")]
        return reads, writes

    def op(self, e, fn, reads=(), writes=(), inc=True):
        reads, writes = self._split(reads, writes)
        self._deps(e, reads, writes, False)
        inst = fn(self.engs[e])
        if inc:
            self.cnt[e] += 1
            inst.then_inc(self.sem[e], 1)
            tok = ("eng", e, self.cnt[e])
        else:
            tok = ("eng", e, self.cnt[e] + 1)
        self._update(tok, reads, writes)
        return tok

    def dma(self, q, out, in_, reads=(), writes=(), **kw):
        self._deps(q, reads, writes, True)
        n = self.dcnt[q]
        self.dcnt[q] += 1
        slot = n % N_DMA_SEMS
        prev = 16 * (n // N_DMA_SEMS)
        if prev > 0:
            self._wait(q, ("dma", (q, slot), prev))
        inst = self.engs[q].dma_start(out=out, in_=in_, **kw)
        inst.then_inc(self.dsem[q][slot], 16)
        tok = ("dma", (q, slot), prev + 16)
        self._update(tok, reads, writes)
        return tok

    def collective(self, ins, outs, groups, reads, writes):
        self._deps("pool", reads, writes, True)
        if self.ccsem:
            self._wait("pool", ("cc", len(self.ccsem) - 1, 1))
        sem = self.stack.enter_context(self.nc.semaphore("ccsem%d" % len(self.ccsem)))
        self.ccsem.append(sem)
        inst = self.nc.gpsimd.collective_compute("AllGather", ALU.bypass, replica_groups=groups,
                                                 ins=ins, outs=outs)
        inst.then_inc(sem)
        tok = ("cc", len(self.ccsem) - 1, 1)
        self._update(tok, reads, writes)
        return tok

    def alias(self, new_names, old_names):
        merged = {}
        for o in old_names:
            st = self.res.get(o)
            if not st:
                continue
            toks = list(st["r"].values()) + ([st["w"]] if st["w"] is not None else [])
            for t in toks:
                k = (t[0], t[1])
                if k not in merged or merged[k][2] < t[2]:
                    merged[k] = t
        for n in new_names:
            st = self.res.setdefault(n, {"w": None, "r": {}})
            for k, t in merged.items():
                if k not in st["r"] or st["r"][k][2] < t[2]:
                    st["r"][k] = t

    def wait_all(self, e):
        for r, st in self.res.items():
            if st["w"] is not None:
                self._wait(e, st["w"])


def bcast(ap, shape):
    return ap.to_broadcast(shape)


def build_program(stop=99, dbg=()):
    nc = bass.Bass("TRN2", target_bir_lowering=False)
    dbg_out = {}

    def din(name, shape, dt=F32):
        return nc.dram_tensor(name, list(shape), dt, kind="ExternalInput").ap()

    def dout(name, shape, dt=F32):
        return nc.dram_tensor(name, list(shape), dt, kind="ExternalOutput").ap()

    xT_d = din("xT", [D, TOK])
    w_in_d = din("w_in", [2, D, IN_COLS])
    w_qb_d = din("w_q_b", [2, 256, 768])
    w_kvb_d = din("w_kv_b", [2, 128, 1024])
    w_o_d = din("w_o", [2, D, D])
    w_fi_d = din("w_ffn_in", [2, D, 2 * DFF])
    w_fo_d = din("w_ffn_out", [2, DFF, D])
    w_mod_d = din("w_mod_sh", [2, D, 1536])
    b_mod_d = din("b_mod_sh", [128, 12, 2])
    cond_d = din("condT", [128, 8, 3])
    sel_d = din("sel", [128, 2])
    gn_d = din("gnT", [128, 2, 2, 8])
    gqa_d = din("gqaT", [128, 2, 2])
    gsm_d = din("gsm", [2, GSM_N])
    retp_d = din("retp", [2, 16])
    cckv_d = din("c_ckv", [2, 256, 128])
    ckr_d = din("c_kr", [2, 256, 32])
    s0_d = din("s0", [2, 2, 8, 64, 64])
    tabs_d = din("tabs", [128, 4, 2, 256])
    idx_d = din("idxt", [128, 2, 2])
    idxq_d = din("idxq", [128, 2, 2])
    rope64_d = din("rope64", [128, 2, 2, 2, 16])
    rope32_d = din("rope32", [128, 2, 2, 2, 8])
    xco_d = din("xcoef", [128, 2, 2, 5])

    yT_d = dout("yT", [D, TOK])
    ockv_d = dout("o_ckv", [2, 1024, 128])
    okr_d = dout("o_kr", [2, 1024, 32])
    osf_d = dout("o_sf", [2, 4, 8, 64, 64])
    osb_d = dout("o_sb", [2, 4, 8, 64, 64])

    mb_c = nc.dram_tensor("mod_bounce", [32, 512], F32).ap()
    mg_c = nc.dram_tensor("mod_gath", [128, 512], F32).ap()
    mb_d = mb_c.rearrange("a b -> (a b)").rearrange("(p f) -> p f", f=128)
    mg_d = mg_c.rearrange("a b -> (a b)").rearrange("(p f) -> p f", f=128)
    kvb_c = [nc.dram_tensor("kv_bounce%d" % l, [80, 512], F32).ap() for l in range(2)]
    kvg_c = [nc.dram_tensor("kv_gath%d" % l, [320, 512], F32).ap() for l in range(2)]
    kvb_d = [a.rearrange("a b -> (a b)").rearrange("(t n) -> t n", n=160) for a in kvb_c]
    kvg_d = [a.rearrange("a b -> (a b)").rearrange("(t n) -> t n", n=160) for a in kvg_c]
    ub_c = [nc.dram_tensor("u_bounce%d" % l, [128, 512], F32).ap() for l in range(2)]
    ug_c = [nc.dram_tensor("u_gath%d" % l, [512, 512], F32).ap() for l in range(2)]
    ub_d = [a.rearrange("a b -> (a b)").rearrange("(t n) -> t n", n=64) for a in ub_c]
    ug_d = [a.rearrange("a b -> (a b)").rearrange("(t n) -> t n", n=64) for a in ug_c]

    sp_qk = [nc.dram_tensor("spill_qk%d" % l, [128, 2048], BF16).ap() for l in range(2)]
    sp_v = [nc.dram_tensor("spill_v%d" % l, [128, 1024], BF16).ap() for l in range(2)]
    sp_g = [nc.dram_tensor("spill_g%d" % l, [128, 1024], BF16).ap() for l in range(2)]
    sp_qa = [nc.dram_tensor("spill_qa%d" % l, [128, 512], BF16).ap() for l in range(2)]

    def dbg_dump(S, name, ap, shape, dt, reads):
        if name in dbg:
            o = dout("dbg_" + name, shape, dt)
            S.dma("sp", o, ap, reads=reads, writes=["dbgo_" + name])
            dbg_out[name] = (shape, dt)

    with ExitStack() as st:
        S = Sync(nc, st)
        ncd = nc.allow_non_contiguous_dma(reason="small strided parameter loads")
        st.enter_context(ncd)

        def sb(name, shape, dt):
            return st.enter_context(nc.sbuf_tensor(name, list(shape), dt))

        xT = sb("xT_sb", [128, 8, TOK], F32)
        hm = sb("hm_sb", [128, 8, TOK], BF16)
        R1 = sb("R1", [128, 22 * 1024], BF16)
        R2 = R1[:, 0:8192] if os.environ.get("SHRINK") else sb("R2", [128, 8 * 1024], BF16)
        R3 = sb("R3", [128, 35 * 1024], BF16)
        w_in = R1[:, 0:8 * IN_COLS].rearrange("p (c n) -> p c n", c=8)
        w_qb = R1[:, 8 * IN_COLS:8 * IN_COLS + 1536].rearrange("p (c n) -> p c n", c=2)
        w_kvb = R1[:, 8 * IN_COLS + 1536:8 * IN_COLS + 2560]
        w_fo = R1[:, :].rearrange("p (c n) -> p c n", c=22)
        sKT = R1[:, 0:8 * TOK].rearrange("p (h t) -> p h t", h=8)
        sVX = R1[:, 8 * TOK:8 * TOK + 10 * 520].rearrange("p (k h e) -> p k h e", k=10, h=8)
        w_o = R2[:, :].rearrange("p (c n) -> p c n", c=8)
        w_fi = [R2[:, i * 4096:(i + 1) * 4096].rearrange("p (c n) -> p c n", c=8) for i in range(2)]
        actT = R3[:, 0:22 * TOK].rearrange("p (c t) -> p c t", c=22)

        R3N = 35 * 1024
        r3o = [0]

        def r3(nelem_bf16, dt=BF16):
            o = r3o[0]
            r3o[0] += (nelem_bf16 + 15) // 16 * 16
            assert r3o[0] <= R3N, r3o[0]
            v = R3[:, o:o + nelem_bf16]
            return v.bitcast(F32) if dt == F32 else v

        DT = r3(2 * 8 * 512, F32).rearrange("p (h j i) -> p h j i", h=8, j=2)
        seg_base = r3o[0]
        qk_off = r3o[0]
        q_tm = r3(2 * 512).rearrange("p (t n) -> p t n", t=2)
        k_tm = r3(2 * 512).rearrange("p (t n) -> p t n", t=2)
        mix_tm = R3[:, qk_off:qk_off + 2048].rearrange("p (t n) -> p t n", t=2)
        v_tm = r3(2 * 512).rearrange("p (t n) -> p t n", t=2)
        g_tm = r3(2 * 512).rearrange("p (t n) -> p t n", t=2)
        kdec = r3(2 * 2 * 512).rearrange("p (d t n) -> p d t n", d=2, t=2)
        qkT = r3(2 * 4 * 256).rearrange("p (w c i) -> p w c i", w=2, c=4)
        qa_n = r3(2 * 256).rearrange("p (t n) -> p t n", t=2)
        qaT = r3(2 * 256).rearrange("p (c i) -> p c i", c=2)
        ckv = r3(2 * 2 * 128, F32).rearrange("p (t n) -> p t n", t=2)
        kr = r3(2 * 2 * 32, F32).rearrange("p (t n) -> p t n", t=2)
        ckvb = r3(128)
        ckvT = r3(128)
        Qc_tm = r3(2 * 768).rearrange("p (t h e) -> p t h e", t=2, h=8)
        QcT = r3(8 * 256).rearrange("p (h i) -> p h i", h=8)
        Kc_tm = r3(768).rearrange("p (h e) -> p h e", h=8)
        kc_off = r3o[0]
        KcT = r3(8 * 256).rearrange("p (h i) -> p h i", h=8)
        vx_off = r3o[0]
        VX = r3(2 * 520).rearrange("p (t h e) -> p t h e", t=2, h=8)
        Sacc = R3[:, kc_off:kc_off + 2048].bitcast(F32).rearrange("p (d h v) -> p d h v", d=2, h=8)
        Sin = R3[:, vx_off:vx_off + 1024].rearrange("p (d h v) -> p d h v", d=2, h=8)
        sqt = r3(2 * 768, F32)
        fA = r3(2 * 512, F32)
        fB = r3(2 * 512, F32)
        AT = [r3(512).rearrange("p (j i) -> p j i", j=2) for _ in range(2)]
        PT = [r3(256) for _ in range(2)]
        Ust = r3(2 * 2 * 512, F32).rearrange("p (d n) -> p d n", d=2)
        Ug = r3(2 * 512, F32).rearrange("p (h v) -> p h v", h=8)
        kvin = r3(2 * 160, F32)
        SEGN = ["q_tm", "k_tm", "v_tm", "g_tm", "kdec", "qkT", "qa_n", "qaT", "ckv", "kr", "ckvb", "ckvT", "Qc_tm", "QcT",
                "Kc_tm", "KcT", "VX", "sqt", "sqtr", "sqtr2", "fqk", "fkr", "fA", "fB", "AT0", "AT1", "PT0", "PT1", "mix_tm",
                "Ust", "Sin", "Ug", "Sacc", "kvin"]
        seg_end = r3o[0]
        r3o[0] = seg_base
        nsq = r3(2 * 8 * 512, F32).rearrange("p (c t) -> p c t", c=8)
        s8 = r3(2 * 512, F32)
        rstd = r3(2 * 512, F32)
        ntmp = [r3(2 * 512, F32) for i in range(3)]
        tabs = r3(2 * 4 * 512, F32).rearrange("p (k j i) -> p k j i", k=4, j=2)
        dtmp = r3(2 * 2 * 512, F32).rearrange("p (k j i) -> p k j i", k=2, j=2)
        NRMN = ["nsq", "s8", "rstd", "ntmp0", "ntmp1", "ntmp2", "tabs", "dtmp0", "dtmp1"]
        print("R3 usage: seg_end=%d norm_end=%d of %d" % (seg_end, r3o[0], R3N))
        wm = R3[:, 8192:8192 + 2 * 8 * 1536].rearrange("p (l c n) -> p l c n", l=2, c=8)

        ones = sb("ones", [128, 128], F32)
        ident = sb("ident", [128, 128], BF16)
        identf = sb("identf", [128, 128], F32)
        epsc = sb("epsc", [128, 1], F32)
        onec = sb("onec", [128, 1], F32)
        condT = sb("condT_sb", [128, 8, 3], F32)
        sT = sb("sT", [128, 8, 3], BF16)
        sel = sb("sel_sb", [128, 2], F32)
        bmT = sb("bmT", [128, 12, 2], F32)
        ml = sb("ml", [128, 12, 2, 3], F32)
        MODT = sb("MODT", [128, 48, 2, 3], F32)
        MS = sb("MS", [128, 48, 2, 2], F32)
        gn = sb("gn_sb", [128, 2, 2, 8], F32)
        GG = sb("GG", [128, 2, 2, 2, 8], F32)
        gqa = sb("gqa_sb", [128, 2, 2], F32)
        gsm = sb("gsm_sb", [128, GSM_N], F32)
        gq96 = sb("gq96", [128, 96], F32)
        RP = sb("RP", [128, 16], F32)
        LG = sb("LG", [128, 2, 8], F32)
        KD = sb("KD", [128, 2, 2, 8], F32)
        KQ = sb("KQ", [128, 2, 2, 8], F32)
        idxq = sb("idxq_sb", [128, 2, 2], F32)
        idxt = sb("idxt_sb", [128, 2, 2], F32)
        rope64 = sb("rope64_sb", [128, 2, 2, 2, 16], F32)
        rope32 = sb("rope32_sb", [128, 2, 2, 2, 8], F32)
        xco = sb("xco_sb", [128, 2, 2, 5], F32)
        xcf = sb("xcf", [128, 2, 5, 8], F32)
        st8 = sb("st8", [128, 64], F32)
        sg = [sb("sg%d" % i, [128, 512], F32) for i in range(2)]

        print("SBUF bytes remaining after allocation:", nc.sbuf_bytes_remaining)
        PS = [st.enter_context(nc.psum_tensor("psb%d" % i, [128, 512], F32)) for i in range(8)]
        PSN = ["ps:%d" % i for i in range(8)]

        def mm(bank, out, lhsT, rhs, start, stop, reads):
            S.op("pe", lambda e: e.matmul(out, lhsT=lhsT, rhs=rhs, start=start, stop=stop),
                 reads=reads, writes=[PSN[bank]], inc=bool(stop))

        def tr(bank, out, in_, reads, last=True):
            S.op("pe", lambda e: e.transpose(out, in_, ident[:in_.shape[0], :in_.shape[0]]),
                 reads=list(reads) + ["ident"], writes=[PSN[bank]], inc=last)

        def act(out, in_, func, reads, writes, **kw):
            S.op("act", lambda e: e.activation(out=out, in_=in_, func=func, **kw), reads=reads, writes=writes)

        def tt(eng, out, in0, in1, op, reads, writes):
            S.op(eng, lambda e: e.tensor_tensor(out=out, in0=in0, in1=in1, op=op), reads=reads, writes=writes)

        def ts(eng, out, in0, s1, s2, op0, op1, reads, writes):
            if s2 is None:
                S.op(eng, lambda e: e.tensor_scalar(out=out, in0=in0, scalar1=s1, scalar2=None, op0=op0),
                     reads=reads, writes=writes)
            else:
                S.op(eng, lambda e: e.tensor_scalar(out=out, in0=in0, scalar1=s1, scalar2=s2, op0=op0, op1=op1),
                     reads=reads, writes=writes)

        def stt(eng, out, in0, scalar, in1, op0, op1, reads, writes):
            S.op(eng, lambda e: e.scalar_tensor_tensor(out=out, in0=in0, scalar=scalar, in1=in1, op0=op0, op1=op1),
                 reads=reads, writes=writes)

        def red(out, in_, reads, writes):
            S.op("dve", lambda e: e.tensor_reduce(out=out, in_=in_, axis=AX.X, op=ALU.add), reads=reads, writes=writes)

        def rsqrt(buf, scale, reads_writes):
            act(buf, buf, AF.Sqrt, reads=[reads_writes, "epsc"], writes=[reads_writes], scale=scale, bias=epsc[:buf.shape[0], :])
            S.op("dve", lambda e: e.reciprocal(out=buf, in_=buf), reads=[reads_writes], writes=[reads_writes])

        def ck(name):
            if stop == name:
                raise StopBuild()

        S.dma("sp", xT[:], xT_d.rearrange("(c p) t -> p c t", p=128), writes=["xT"])
        S.dma("sp", condT[:], cond_d, writes=["condT"])
        S.dma("sp", sel[:], sel_d, writes=["sel"])
        S.dma("sp", bmT[:], b_mod_d, writes=["bmT"])
        S.dma("sp", gn[:], gn_d, writes=["gn"])
        S.dma("sp", gqa[:], gqa_d, writes=["gqa"])
        S.dma("sp", idxt[:], idx_d, writes=["idxt"])
        S.dma("sp", idxq[:], idxq_d, writes=["idxq"])
        S.dma("sp", rope64[:], rope64_d, writes=["rope64"])
        S.dma("sp", rope32[:], rope32_d, writes=["rope32"])
        S.dma("sp", xco[:], xco_d, writes=["xco"])
        S.dma("pool", wm, w_mod_d.rearrange("l (c p) n -> p l c n", p=128), writes=["wm"])

        S.op("pool", lambda e: e.memset(ones[:], 1.0), writes=["ones"])
        S.op("pool", lambda e: e.memset(epsc[:], EPS), writes=["epsc"])
        S.op("pool", lambda e: e.memset(onec[:], 1.0), writes=["onec"])
        S.op("pool", lambda e: e.memset(identf[:], 0.0), writes=["identf"])
        S.op("pool", lambda e: e.affine_select(out=identf[:], in_=identf[:], pattern=[[-1, 128]],
                                               compare_op=ALU.not_equal, fill=1.0, base=0, channel_multiplier=1),
             reads=["identf"], writes=["identf"])
        S.op("dve", lambda e: e.tensor_copy(ident[:], identf[:]), reads=["identf"], writes=["ident"])

        def modulation_setup():
            ck("t0")
            act(sT[:], condT[:], AF.Silu, reads=["condT"], writes=["sT"])
            pm = PS[0][:, 0:72].rearrange("p (j l k) -> p j l k", j=12, l=2)
            for l in range(2):
                for cj in range(12):
                    for c in range(8):
                        mm(0, pm[:, cj, l, :], wm[:, l, c, cj * 128:(cj + 1) * 128], sT[:, c, :], c == 0, c == 7,
                           reads=["wm", "sT"])
            tt("dve", ml[:], pm, bcast(bmT[:].unsqueeze(3), [128, 12, 2, 3]), ALU.add, reads=[PSN[0], "bmT"], writes=["ml"])
            ck("t1")
            S.dma("sp", mb_d[:, 0:72], ml[:].rearrange("p j l k -> p (j l k)"), reads=["ml"], writes=["mb_d"])
            S.collective([mb_c], [mg_c], [[0, 1, 2, 3], [4, 5, 6, 7]], reads=["mb_d"], writes=["mg_d"])
            S.dma("sp", MODT[:].rearrange("p (r j) l k -> p r (j l k)", r=4), mg_d.rearrange("(r p) f -> p r f", p=128)[:, :, 0:72],
                  reads=["mg_d"], writes=["MODT"])
            ck("t2")
            S.op("dve", lambda e: e.tensor_copy(MS[:, :, :, 0], MODT[:, :, :, 0]), reads=["MODT"], writes=["MS0"])
            ts("dve", MS[:, :, :, 1], MODT[:, :, :, 1], sel[:, 0:1], None, ALU.mult, None, reads=["MODT", "sel"], writes=["MS1"])
            stt("dve", MS[:, :, :, 1], MODT[:, :, :, 2], sel[:, 1:2], MS[:, :, :, 1], ALU.mult, ALU.add,
                reads=["MODT", "sel", "MS1"], writes=["MS1"])
            for l in range(2):
                for grp in range(2):
                    for which, k in ((0, 1), (1, 4)):
                        stt("dve", GG[:, l, grp, which, :], MS[:, k * 8:(k + 1) * 8, l, grp], 1.0, gn[:, l, which, :],
                            ALU.add, ALU.mult, reads=["MS0", "MS1", "gn"], writes=["GG"])

        def modcol(k, c, l, grp):
            return MS[:, k * 8 + c, l, grp:grp + 1]

        GROUPS = [(0, 512, 0), (512, 1024, 0), (1024, 1280, 1)]

        def norm_phase(l, which):
            kk_sh = 0 if which == 0 else 3
            for gi, (t0, t1, grp) in enumerate(GROUPS):
                n = t1 - t0
                act(nsq[:, :, 0:n], xT[:, :, t0:t1], AF.Square, reads=["xT:%d" % gi], writes=["nsq"])
                red(s8[:, 0:n], nsq[:, :, 0:n].rearrange("p c t -> p t c"), reads=["nsq"], writes=["s8"])
                mm(7, PS[7][:, 0:n], ones[:], s8[:, 0:n], True, True, reads=["ones", "s8"])
                act(rstd[:, 0:n], PS[7][:, 0:n], AF.Sqrt, reads=[PSN[7], "epsc"], writes=["rstd"], scale=1.0 / D, bias=epsc[:])
                S.op("dve", lambda e: e.reciprocal(out=rstd[:, 0:n], in_=rstd[:, 0:n]), reads=["rstd"], writes=["rstd"])
                for c in range(8):
                    tb = ntmp[c % 3]
                    stt("dve", tb[:, 0:n], xT[:, c, t0:t1], GG[:, l, grp, which, c:c + 1], rstd[:, 0:n], ALU.mult, ALU.mult,
                        reads=["xT:%d" % gi, "GG", "rstd"], writes=["ntmp%d" % (c % 3)])
                    act(hm[:, c, t0:t1], tb[:, 0:n], AF.Identity, reads=["ntmp%d" % (c % 3), "MS0", "MS1"],
                        writes=hmg(gi), bias=modcol(kk_sh, c, l, grp), scale=1.0)


        S.alias(["xT:0", "xT:1", "xT:2"], ["xT"])

        GSEG = [[0, 1], [2, 3], [4]]

        def hmg(gi):
            return ["hm:s%d" % sg_ for sg_ in GSEG[gi]]

        def layer_tables(l):
            S.dma("sp", gsm[:], gsm_d[l].partition_broadcast(128), writes=["gsm"])
            S.dma("sp", RP[:], retp_d[l].partition_broadcast(128), writes=["RP"])
            S.dma("sp", tabs, tabs_d, writes=["tabs"])
            ts("dve", gq96[:], gsm[:, G_QN:G_QN + 96], ATT_SCALE, None, ALU.mult, None, reads=["gsm"], writes=["gq96"])
            act(LG[:].rearrange("p d h -> p (d h)"), RP[:], AF.Exp, reads=["RP"], writes=["LG"], scale=-LN2)
            act(LG[:].rearrange("p d h -> p (d h)"), LG[:].rearrange("p d h -> p (d h)"), AF.Ln, reads=["LG", "onec"], writes=["LG"],
                scale=-1.0, bias=onec[:])
            for h in range(8):
                act(dtmp[:, 0], tabs[:, 0], AF.Exp, reads=["tabs", "LG"], writes=["dtmp0"], scale=LG[:, 0, h:h + 1])
                act(dtmp[:, 1], tabs[:, 1], AF.Exp, reads=["tabs", "LG"], writes=["dtmp1"], scale=LG[:, 1, h:h + 1])
                tt("dve", dtmp[:, 0], dtmp[:, 0], tabs[:, 2], ALU.mult, reads=["dtmp0", "tabs"], writes=["dtmp0"])
                tt("dve", dtmp[:, 1], dtmp[:, 1], tabs[:, 3], ALU.mult, reads=["dtmp1", "tabs"], writes=["dtmp1"])
                tt("dve", DT[:, h], dtmp[:, 0], dtmp[:, 1], ALU.add, reads=["dtmp0", "dtmp1"], writes=["DT"])
            for d in range(2):
                for jt in range(2):
                    act(KD[:, d, jt, :], LG[:, d, :], AF.Exp, reads=["LG", "idxt"], writes=["KD"], scale=idxt[:, d, jt:jt + 1])
                for it in range(2):
                    act(KQ[:, d, it, :], LG[:, d, :], AF.Exp, reads=["LG", "idxq"], writes=["KQ"], scale=idxq[:, d, it:it + 1])
            for d in range(2):
                tt("dve", xcf[:, d], bcast(LG[:, d, :].unsqueeze(1), [128, 5, 8]), bcast(xco[:, 0, d, :].unsqueeze(2), [128, 5, 8]),
                   ALU.mult, reads=["LG", "xco"], writes=["xcf"])
            act(xcf[:].rearrange("p d s h -> p (d s h)"), xcf[:].rearrange("p d s h -> p (d s h)"), AF.Exp, reads=["xcf"], writes=["xcf"])
            for d in range(2):
                tt("dve", xcf[:, d], xcf[:, d], bcast(xco[:, 1, d, :].unsqueeze(2), [128, 5, 8]), ALU.mult,
                   reads=["xcf", "xco"], writes=["xcf"])

        def rope(out, src, tl, tab, H, nf, reads, writes, scale=None):
            sv = src.rearrange("p (h f x n) -> p h f x n", h=H, f=2, x=2)
            ov = out.rearrange("p (h f x n) -> p h f x n", h=H, f=2, x=2)
            C = bcast(tab[:, tl, 0].unsqueeze(1), [128, H, 2, nf])
            Sn = bcast(tab[:, tl, 1].unsqueeze(1), [128, H, 2, nf])
            a = fA[:, 0:H * 2 * nf].rearrange("p (h f n) -> p h f n", h=H, f=2)
            b = fB[:, 0:H * 2 * nf].rearrange("p (h f n) -> p h f n", h=H, f=2)
            x1, x2 = sv[:, :, :, 0, :], sv[:, :, :, 1, :]
            tt("dve", a, x1, C, ALU.mult, reads=reads, writes=["fA"])
            tt("dve", b, x2, Sn, ALU.mult, reads=reads, writes=["fB"])
            tt("dve", ov[:, :, :, 0, :], a, b, ALU.subtract, reads=["fA", "fB"], writes=writes)
            tt("dve", a, x1, Sn, ALU.mult, reads=reads, writes=["fA"])
            tt("dve", b, x2, C, ALU.mult, reads=reads, writes=["fB"])
            tt("dve", ov[:, :, :, 1, :], a, b, ALU.add, reads=["fA", "fB"], writes=writes)

        def seg_project(l, seg, emit_out=True):
            is_s = seg == 4
            S.alias(["q_tm", "k_tm"], ["mix_tm"])
            S.alias(["KcT", "VX"], ["Sacc", "Sin"])
            gi = 2 if is_s else seg // 2
            for tl in range(2):
                tile = seg * 2 + tl
                t0 = tile * 128
                for cg in range(5):
                    c0 = cg * 512
                    ncol = min(512, IN_COLS - c0)
                    for c in range(8):
                        mm(cg, PS[cg][:, 0:ncol], hm[:, c, t0:t0 + 128], w_in[:, c, c0:c0 + ncol], c == 0, c == 7,
                           reads=["hm:s%d" % seg, "w_in"])
                if is_s:
                    rope(fqk0, PS[0][:, :], tl, rope64, 8, 16, reads=[PSN[0], "rope64"], writes=["fqk"])
                    S.op("act", lambda e: e.copy(q_tm[:, tl, :], fqk0), reads=["fqk"], writes=["q_tm"])
                else:
                    S.op("act", lambda e: e.copy(q_tm[:, tl, :], PS[0][:, :]), reads=[PSN[0]], writes=["q_tm"])
                if is_s:
                    rope(fqk0, PS[1][:, :], tl, rope64, 8, 16, reads=[PSN[1], "rope64"], writes=["fqk"])
                    S.op("act", lambda e: e.mul(k_tm[:, tl, :], fqk0, 0.125), reads=["fqk"], writes=["k_tm"])
                else:
                    S.op("act", lambda e: e.mul(k_tm[:, tl, :], PS[1][:, :], 0.125), reads=[PSN[1]], writes=["k_tm"])
                S.op("dve", lambda e: e.tensor_copy(v_tm[:, tl, :], PS[2][:, :]), reads=[PSN[2]], writes=["v_tm"])
                act(g_tm[:, tl, :], PS[3][:, :], AF.Silu, reads=[PSN[3]], writes=["g_tm"])
                zq = PS[4]
                act(sqt[:, 0:416], zq[:, 0:416], AF.Square, reads=[PSN[4]], writes=["sqt"])
                red(st8[:, 0:1], sqt[:, 0:256], reads=["sqt"], writes=["st8"])
                red(st8[:, 1:2], sqt[:, 256:384], reads=["sqt"], writes=["st8"])
                red(st8[:, 2:3], sqt[:, 384:416], reads=["sqt"], writes=["st8"])
                act(st8[:, 0:1], st8[:, 0:1], AF.Sqrt, reads=["st8", "epsc"], writes=["st8"], scale=1.0 / 256, bias=epsc[:])
                act(st8[:, 1:2], st8[:, 1:2], AF.Sqrt, reads=["st8", "epsc"], writes=["st8"], scale=1.0 / 128, bias=epsc[:])
                act(st8[:, 2:3], st8[:, 2:3], AF.Sqrt, reads=["st8", "epsc"], writes=["st8"], scale=1.0 / 32, bias=epsc[:])
                S.op("dve", lambda e: e.reciprocal(out=st8[:, 0:3], in_=st8[:, 0:3]), reads=["st8"], writes=["st8"])
                ts("dve", qa_n[:, tl, :], zq[:, 0:256], st8[:, 0:1], None, ALU.mult, None, reads=[PSN[4], "st8"], writes=["qa_n"])
                stt("dve", ckv[:, tl, :], zq[:, 256:384], st8[:, 1:2], gsm[:, G_KVA:G_KVA + 128], ALU.mult, ALU.mult,
                    reads=[PSN[4], "st8", "gsm"], writes=["ckv"])
                if is_s:
                    stt("dve", fB[:, 256:288], zq[:, 384:416], st8[:, 2:3], gsm[:, G_KR:G_KR + 32], ALU.mult, ALU.mult,
                        reads=[PSN[4], "st8", "gsm"], writes=["fkr"])
                    rope(kr[:, tl, :], fB[:, 256:288], tl, rope32, 1, 8, reads=["fkr", "rope32"], writes=["kr"])
                else:
                    stt("dve", kr[:, tl, :], zq[:, 384:416], st8[:, 2:3], gsm[:, G_KR:G_KR + 32], ALU.mult, ALU.mult,
                        reads=[PSN[4], "st8", "gsm"], writes=["kr"])
            if not emit_out:
                return
            if is_s:
                S.dma("sp", kvb_d[l][:, 0:128].rearrange("(t p) n -> p t n", p=128), ckv[:], reads=["ckv"], writes=["kvb_d%d" % l])
                S.dma("sp", kvb_d[l][:, 128:160].rearrange("(t p) n -> p t n", p=128), kr[:], reads=["kr"], writes=["kvb_d%d" % l])
            else:
                S.dma("sp", ockv_d[l, seg * 256:(seg + 1) * 256, :].rearrange("(t p) n -> p t n", p=128), ckv[:], reads=["ckv"], writes=["ockv"])
                S.dma("sp", okr_d[l, seg * 256:(seg + 1) * 256, :].rearrange("(t p) n -> p t n", p=128), kr[:], reads=["kr"], writes=["okr"])

        fqk0 = sqt[:, 0:512]

        def seg_q(l, seg):
            is_s = seg == 4
            for tl in range(2):
                pt = PS[5][:, 0:128].bitcast(BF16).rearrange("p (c i) -> p c i", c=2)
                for c in range(2):
                    tr(5, pt[:, c, :], qa_n[:, tl, c * 128:(c + 1) * 128], reads=["qa_n"], last=(c == 1))
                for c in range(2):
                    act(qaT[:, c, tl * 128:(tl + 1) * 128], pt[:, c, :], AF.Copy, reads=[PSN[5], "gqa"], writes=["qaT"],
                        scale=gqa[:, l, c:c + 1])
                for half in range(2):
                    for c in range(2):
                        mm(6 + half, PS[6 + half][:, 0:384], qaT[:, c, tl * 128:(tl + 1) * 128],
                           w_qb[:, c, half * 384:(half + 1) * 384], c == 0, c == 1, reads=["qaT", "w_qb"])
                for half in range(2):
                    qv = PS[6 + half][:, 0:384].rearrange("p (h e) -> p h e", h=4)
                    sq = sqt[:, half * 384:(half + 1) * 384].rearrange("p (h e) -> p h e", h=4)
                    act(sq, qv, AF.Square, reads=[PSN[6 + half]], writes=["sqt"])
                    red(st8[:, 8 + half * 4:12 + half * 4], sq[:, :, 0:64], reads=["sqt"], writes=["st8q"])
                    red(st8[:, 16 + half * 4:20 + half * 4], sq[:, :, 64:96], reads=["sqt"], writes=["st8q"])
                act(st8[:, 8:16], st8[:, 8:16], AF.Sqrt, reads=["st8q", "epsc"], writes=["st8q"], scale=1.0 / 64, bias=epsc[:])
                act(st8[:, 16:24], st8[:, 16:24], AF.Sqrt, reads=["st8q", "epsc"], writes=["st8q"], scale=1.0 / 32, bias=epsc[:])
                S.op("dve", lambda e: e.reciprocal(out=st8[:, 8:24], in_=st8[:, 8:24]), reads=["st8q"], writes=["st8q"])
                for half in range(2):
                    qv = PS[6 + half][:, 0:384].rearrange("p (h e) -> p h e", h=4)
                    dst = fA[:, 0:384].rearrange("p (h e) -> p h e", h=4) if half == 0 else fB[:, 0:384].rearrange("p (h e) -> p h e", h=4)
                    nm = "fA" if half == 0 else "fB"
                    tt("dve", dst[:, :, 0:64], qv[:, :, 0:64], bcast(st8[:, 8 + half * 4:12 + half * 4].unsqueeze(2), [128, 4, 64]),
                       ALU.mult, reads=[PSN[6 + half], "st8q"], writes=[nm])
                    tt("dve", dst[:, :, 64:96], qv[:, :, 64:96], bcast(st8[:, 16 + half * 4:20 + half * 4].unsqueeze(2), [128, 4, 32]),
                       ALU.mult, reads=[PSN[6 + half], "st8q"], writes=[nm])
                    if not is_s:
                        tt("dve", Qc_tm[:, tl, half * 4:(half + 1) * 4, :], dst, bcast(gq96[:].unsqueeze(1), [128, 4, 96]), ALU.mult,
                           reads=[nm, "gq96"], writes=["Qc_tm"])
                    else:
                        tt("dve", dst, dst, bcast(gq96[:].unsqueeze(1), [128, 4, 96]), ALU.mult, reads=[nm, "gq96"], writes=[nm])
                        S.op("act", lambda e: e.copy(Qc_tm[:, tl, half * 4:(half + 1) * 4, 0:64], dst[:, :, 0:64]), reads=[nm], writes=["Qc_tm"])
                        S.op("act", lambda e: e.copy(sqt[:, 0:128].rearrange("p (h e) -> p h e", h=4), dst[:, :, 64:96]), reads=[nm], writes=["sqtr"])
                        rope(sqt[:, 128:256], sqt[:, 0:128], tl, rope32, 4, 8, reads=["sqtr", "rope32"], writes=["sqtr2"])
                        S.op("act", lambda e: e.copy(Qc_tm[:, tl, half * 4:(half + 1) * 4, 64:96],
                                                     sqt[:, 128:256].rearrange("p (h e) -> p h e", h=4)), reads=["sqtr2"], writes=["Qc_tm"])
                pq = PS[5][:, :].bitcast(BF16).rearrange("p (h i) -> p h i", h=8)
                for h in range(8):
                    tr(5, pq[0:96, h, :], Qc_tm[:, tl, h, :], reads=["Qc_tm"], last=(h == 7))
                S.op("act", lambda e: e.copy(QcT[0:96, :, tl * 128:(tl + 1) * 128], pq[0:96, :, :]), reads=[PSN[5]], writes=["QcT"])

        def mla_up(src_ckv, src_kr, KT_dst, VX_dst, reads, kt_name, vx_name):
            S.op("dve", lambda e: e.tensor_copy(ckvb, src_ckv), reads=reads, writes=["ckvb"])
            ck("m0a")
            pt = PS[5][:, 0:64].bitcast(BF16)
            tr(5, pt, ckvb, reads=["ckvb"])
            ck("m0b")
            S.op("act", lambda e: e.copy(ckvT, pt), reads=[PSN[5]], writes=["ckvT"])
            ck("m1")
            for half in range(2):
                mm(6 + half, PS[6 + half][:, :], ckvT, w_kvb[:, half * 512:(half + 1) * 512], True, True, reads=["ckvT", "w_kvb"])
            for half in range(2):
                kv = PS[6 + half][:, :].rearrange("p (h e) -> p h e", h=4)
                sq = sqt[:, half * 256:(half + 1) * 256].rearrange("p (h e) -> p h e", h=4)
                act(sq, kv[:, :, 0:64], AF.Square, reads=[PSN[6 + half]], writes=["sqt"])
                red(st8[:, 24 + half * 4:28 + half * 4], sq, reads=["sqt"], writes=["st8k"])
                S.op("act", lambda e: e.copy(VX_dst[:, half * 4:(half + 1) * 4, 0:64], kv[:, :, 64:128]), reads=[PSN[6 + half]], writes=[vx_name])
            ck("m2")
            act(st8[:, 24:32], st8[:, 24:32], AF.Sqrt, reads=["st8k", "epsc"], writes=["st8k"], scale=1.0 / 64, bias=epsc[:])
            S.op("dve", lambda e: e.reciprocal(out=st8[:, 24:32], in_=st8[:, 24:32]), reads=["st8k"], writes=["st8k"])
            for half in range(2):
                kv = PS[6 + half][:, :].rearrange("p (h e) -> p h e", h=4)
                tmp = fA[:, 0:256].rearrange("p (h e) -> p h e", h=4) if half == 0 else fB[:, 0:256].rearrange("p (h e) -> p h e", h=4)
                nm = "fA" if half == 0 else "fB"
                tt("dve", tmp, kv[:, :, 0:64], bcast(st8[:, 24 + half * 4:28 + half * 4].unsqueeze(2), [128, 4, 64]), ALU.mult,
                   reads=[PSN[6 + half], "st8k"], writes=[nm])
                tt("dve", Kc_tm[:, half * 4:(half + 1) * 4, 0:64], tmp, bcast(gsm[:, G_KN:G_KN + 64].unsqueeze(1), [128, 4, 64]), ALU.mult,
                   reads=[nm, "gsm"], writes=["Kc_tm"])
            ck("m3")
            S.op("act", lambda e: e.copy(Kc_tm[:, :, 64:96], bcast(src_kr.unsqueeze(1), [128, 8, 32])), reads=reads, writes=["Kc_tm"])
            ck("m4")
            S.op("dve", lambda e: e.memset(VX_dst[:, :, 64:65], 1.0), writes=[vx_name])
            ck("m5")
            pk = PS[5][:, :].bitcast(BF16).rearrange("p (h i) -> p h i", h=8)
            for h in range(8):
                tr(5, pk[0:96, h, :], Kc_tm[:, h, :], reads=["Kc_tm"], last=(h == 7))
            S.op("act", lambda e: e.copy(KT_dst, pk[0:96, :, :]), reads=[PSN[5]], writes=[kt_name])

        def seg_retention(l, seg, emit_out=True):
            is_s = seg == 4
            for w, (src, nm) in enumerate(((q_tm, "q_tm"), (k_tm, "k_tm"))):
                for tl in range(2):
                    pt = PS[5][:, 0:256].bitcast(BF16).rearrange("p (c i) -> p c i", c=4)
                    for c in range(4):
                        tr(5, pt[:, c, :], src[:, tl, c * 128:(c + 1) * 128], reads=[nm], last=(c == 3))
                    S.op("act", lambda e: e.copy(qkT[:, w, :, tl * 128:(tl + 1) * 128], pt), reads=[PSN[5]], writes=["qkT"])
            for d in range(2):
                for tl in range(2):
                    tt("dve", kdec[:, d, tl, :].rearrange("p (h e) -> p h e", h=8), k_tm[:, tl, :].rearrange("p (h e) -> p h e", h=8),
                       bcast(KD[:, d, tl, :].unsqueeze(2), [128, 8, 64]), ALU.mult, reads=["k_tm", "KD"], writes=["kdec"])
            for d, bank in ((0, 4), (1, 6)):
                for pr in range(4):
                    for tl in range(2):
                        mm(bank, PS[bank][:, pr * 128:(pr + 1) * 128], kdec[:, d, tl, pr * 128:(pr + 1) * 128],
                           v_tm[:, tl, pr * 128:(pr + 1) * 128], tl == 0, tl == 1, reads=["kdec", "v_tm"])
                S.op("act", lambda e: e.copy(Ust[:, d, :], PS[bank][:, :]), reads=[PSN[bank]], writes=["Ust"])
            Uv = Ust[:].rearrange("p d (r e) -> p d r e", r=4)
            if not emit_out:
                return
            if is_s:
                dst = ub_d[l].rearrange("(d r x k) v -> k d r x v", d=2, r=4, x=2)
                for d in range(2):
                    S.dma("sp", dst[:, d, :, 0, :], Uv[0:64, d, :, 0:64], reads=["Ust"], writes=["ub_d%d" % l])
                    S.dma("sp", dst[:, d, :, 1, :], Uv[64:128, d, :, 64:128], reads=["Ust"], writes=["ub_d%d" % l])
            else:
                for d, od in ((0, osf_d), (1, osb_d)):
                    dst = od[l, seg].rearrange("(r x) k v -> k r x v", x=2)
                    S.dma("sp", dst[:, :, 0, :], Uv[0:64, d, :, 0:64], reads=["Ust"], writes=["ost"])
                    S.dma("sp", dst[:, :, 1, :], Uv[64:128, d, :, 64:128], reads=["Ust"], writes=["ost"])

        def seg_retention_out(l, seg):
            is_s = seg == 4
            S.alias(["mix_tm"], ["q_tm", "k_tm"])
            for h in range(8):
                c, po = h // 2, (h % 2) * 64
                bank = h % 2
                pst = PS[bank][:, :].rearrange("p (j i) -> p j i", j=2)
                for jt in range(2):
                    mm(bank, pst[:, jt, :], qkT[po:po + 64, 1, c, jt * 128:(jt + 1) * 128], qkT[po:po + 64, 0, c, :], True, True,
                       reads=["qkT"])
                at = AT[h % 2]
                tt("dve", at, pst, DT[:, h], ALU.mult, reads=[PSN[bank], "DT"], writes=["AT%d" % (h % 2)])
                for it in range(2):
                    ob = 2 + it
                    o = PS[ob][:, h * 64:(h + 1) * 64]
                    for jt in range(2):
                        mm(ob, o, at[:, jt, it * 128:(it + 1) * 128], v_tm[:, jt, h * 64:(h + 1) * 64], jt == 0, jt == 1,
                           reads=["AT%d" % (h % 2), "v_tm"])
                    if is_s:
                        for d in range(2):
                            cb = 4 + it * 2 + d
                            mm(cb, PS[cb][:, h * 64:(h + 1) * 64], qkT[po:po + 64, 0, c, it * 128:(it + 1) * 128], Sin[po:po + 64, d, h, :],
                               True, True, reads=["qkT", "Sin"])
            for it in range(2):
                ob = 2 + it
                ov = PS[ob][:, :].rearrange("p (h e) -> p h e", h=8)
                ovn = [PSN[ob]]
                if is_s:
                    osb = sqt[:, 0:512].rearrange("p (h e) -> p h e", h=8)
                    for d in range(2):
                        cb = 4 + it * 2 + d
                        tmp = (fA if d == 0 else fB)[:, 0:512].rearrange("p (h e) -> p h e", h=8)
                        tt("dve", tmp, PS[cb][:, :].rearrange("p (h e) -> p h e", h=8), bcast(KQ[:, d, it, :].unsqueeze(2), [128, 8, 64]),
                           ALU.mult, reads=[PSN[cb], "KQ"], writes=["fA" if d == 0 else "fB"])
                    tt("dve", osb, ov, fA[:, 0:512].rearrange("p (h e) -> p h e", h=8), ALU.add, reads=[PSN[ob], "fA"], writes=["sqt"])
                    tt("dve", osb, osb, fB[:, 0:512].rearrange("p (h e) -> p h e", h=8), ALU.add, reads=["sqt", "fB"], writes=["sqt"])
                    ov = osb
                    ovn = ["sqt"]
                red(st8[:, 32:40], ov, reads=ovn, writes=["st8g"])
                ts("dve", st8[:, 32:40], st8[:, 32:40], 1.0 / 64, None, ALU.mult, None, reads=["st8g"], writes=["st8g"])
                dv_ = fA[:, 0:512].rearrange("p (h e) -> p h e", h=8)
                tt("dve", dv_, ov, bcast(st8[:, 32:40].unsqueeze(2), [128, 8, 64]), ALU.subtract, reads=ovn + ["st8g"], writes=["fA"])
                sq = fB[:, 0:512].rearrange("p (h e) -> p h e", h=8)
                act(sq, dv_, AF.Square, reads=["fA"], writes=["fB"])
                red(st8[:, 40:48], sq, reads=["fB"], writes=["st8h"])
                act(st8[:, 40:48], st8[:, 40:48], AF.Sqrt, reads=["st8h", "epsc"], writes=["st8h"], scale=1.0 / 64, bias=epsc[:])
                S.op("dve", lambda e: e.reciprocal(out=st8[:, 40:48], in_=st8[:, 40:48]), reads=["st8h"], writes=["st8h"])
                tt("dve", dv_, dv_, bcast(st8[:, 40:48].unsqueeze(2), [128, 8, 64]), ALU.mult, reads=["fA", "st8h"], writes=["fA"])
                tt("dve", fA[:, 0:512], fA[:, 0:512], gsm[:, G_GN:G_GN + 512], ALU.mult, reads=["fA", "gsm"], writes=["fA"])
                tt("dve", fA[:, 0:512], fA[:, 0:512], gsm[:, B_GN:B_GN + 512], ALU.add, reads=["fA", "gsm"], writes=["fA"])
                tt("dve", mix_tm[:, it, 0:512], fA[:, 0:512], g_tm[:, it, :], ALU.mult, reads=["fA", "g_tm"], writes=["mix_tm"])

        def seg_attention(l, seg, nkt, KT, VXs, kt_name, vx_name):
            for hg in range(2):
                for hh in range(4):
                    h = hg * 4 + hh
                    for kt in range(nkt):
                        sbk = (h * nkt + kt) % 2
                        ps_ = PS[sbk][:, 0:256]
                        mm(sbk, ps_, KT[0:96, h, kt * 128:(kt + 1) * 128], QcT[0:96, h, :], True, True, reads=[kt_name, "QcT"])
                        p = PT[(h * nkt + kt) % 2]
                        pn = "PT%d" % ((h * nkt + kt) % 2)
                        act(p, ps_, AF.Exp, reads=[PSN[sbk]], writes=[pn])
                        for it in range(2):
                            ob = 2 + it if hg == 0 else 4 + 2 * it
                            mm(ob, PS[ob][:, hh * 65:(hh + 1) * 65], p[:, it * 128:(it + 1) * 128], VXs(kt)[:, h, :], kt == 0, kt == nkt - 1,
                               reads=[pn, vx_name])
            for hg in range(2):
                for it in range(2):
                    ob = 2 + it if hg == 0 else 4 + 2 * it
                    ov = PS[ob][:, 0:260].rearrange("p (h e) -> p h e", h=4)
                    S.op("dve", lambda e: e.reciprocal(out=st8[:, 48:52], in_=ov[:, :, 64]), reads=[PSN[ob]], writes=["st8a"])
                    tt("dve", mix_tm[:, it, 512 + hg * 256:512 + (hg + 1) * 256].rearrange("p (h e) -> p h e", h=4), ov[:, :, 0:64],
                       bcast(st8[:, 48:52].unsqueeze(2), [128, 4, 64]), ALU.mult, reads=[PSN[ob], "st8a"], writes=["mix_tm"])

        def seg_mix_out(seg):
            gi = 2 if seg == 4 else seg // 2
            for tl in range(2):
                t0 = (seg * 2 + tl) * 128
                pt = PS[7][:, :].bitcast(BF16).rearrange("p (c i) -> p c i", c=8)
                for c in range(8):
                    tr(7, pt[:, c, :], mix_tm[:, tl, c * 128:(c + 1) * 128], reads=["mix_tm"], last=(c == 7))
                S.op("act", lambda e: e.copy(hm[:, :, t0:t0 + 128], pt), reads=[PSN[7]], writes=["hm:s%d" % seg])

        def sample_exchange(l):
            S.collective([kvb_c[l]], [kvg_c[l]], [[0, 1, 2, 3], [4, 5, 6, 7]], reads=["kvb_d%d" % l], writes=["kvg_d%d" % l])
            S.collective([ub_c[l]], [ug_c[l]], [[0, 1, 2, 3], [4, 5, 6, 7]], reads=["ub_d%d" % l], writes=["ug_d%d" % l])

        def sample_states(l):
            ugv = ug_d[l].rearrange("(s d h k) v -> k s d h v", s=4, d=2, h=8)
            for d in range(2):
                for s_ in range(5):
                    if s_ < 4:
                        S.dma("sp", Ug[0:64], ugv[:, s_, d], reads=["ug_d%d" % l], writes=["Ug"])
                    else:
                        S.dma("sp", Ug[0:64], s0_d[l, d].rearrange("h k v -> k h v"), writes=["Ug"])
                    cf = bcast(xcf[0:64, d, s_, :].unsqueeze(2), [64, 8, 64])
                    if s_ == 0:
                        tt("dve", Sacc[0:64, d], Ug[0:64], cf, ALU.mult, reads=["Ug", "xcf"], writes=["Sacc"])
                    else:
                        tt("dve", fA[0:64, 0:512].rearrange("p (h e) -> p h e", h=8), Ug[0:64], cf, ALU.mult,
                           reads=["Ug", "xcf"], writes=["fA"])
                        tt("dve", Sacc[0:64, d], Sacc[0:64, d], fA[0:64, 0:512].rearrange("p (h e) -> p h e", h=8), ALU.add,
                           reads=["Sacc", "fA"], writes=["Sacc"])
            S.op("act", lambda e: e.copy(Sin[0:64], Sacc[0:64]), reads=["Sacc"], writes=["Sin"])
            S.dma("sp", Sin[64:128], Sin[0:64], reads=["Sin"], writes=["Sin"])

        def sample_keys(l):
            S.alias(["sKT", "sVX"], ["w_in"])
            kg = kvg_d[l].rearrange("(t p) n -> t p n", p=128)
            for kt in range(10):
                if kt < 8:
                    S.dma("sp", kvin[:, :], kg[kt], reads=["kvg_d%d" % l], writes=["kvin"])
                else:
                    S.dma("sp", kvin[:, 0:128], cckv_d[l, (kt - 8) * 128:(kt - 7) * 128, :], writes=["kvin"])
                    S.dma("sp", kvin[:, 128:160], ckr_d[l, (kt - 8) * 128:(kt - 7) * 128, :], writes=["kvin"])
                mla_up(kvin[:, 0:128], kvin[:, 128:160], sKT[0:96, :, kt * 128:(kt + 1) * 128], sVX[:, kt], ["kvin"], "sKT", "sVX")

        def phase_wo(l):
            for gi, (t0, t1, grp) in enumerate(GROUPS):
                n = t1 - t0
                for j in range(8):
                    bank = j % 4
                    for c in range(8):
                        mm(bank, PS[bank][:, 0:n], w_o[:, c, j * 128:(j + 1) * 128], hm[:, c, t0:t1], c == 0, c == 7,
                           reads=["w_o"] + hmg(gi))
                    stt("dve", xT[:, j, t0:t1], PS[bank][:, 0:n], modcol(2, j, l, grp), xT[:, j, t0:t1], ALU.mult, ALU.add,
                        reads=[PSN[bank], "MS0", "MS1", "xT:%d" % gi], writes=["xT:%d" % gi])

        def phase_ffn(l):
            S.alias(["actT"], ["DT", "wm"] + SEGN + NRMN)
            S.alias(["wfi0", "wfi1"], ["w_o"])
            S.alias(["w_fo"], ["w_in", "w_qb", "w_kvb", "sKT", "sVX"])
            nblk = 22
            for j in range(nblk):
                buf = j % 2
                wv = w_fi[buf]
                S.dma("pool", wv[:, :, 0:128], w_fi_d[l, :, j * 128:(j + 1) * 128].rearrange("(c p) n -> p c n", p=128), writes=["wfi%d" % buf])
                S.dma("pool", wv[:, :, 128:256], w_fi_d[l, :, DFF + j * 128:DFF + (j + 1) * 128].rearrange("(c p) n -> p c n", p=128),
                      writes=["wfi%d" % buf])
                if j == 2:
                    S.dma("pool", w_fo, w_fo_d[l].rearrange("(c p) n -> p c n", p=128), writes=["w_fo"])
                for gi, (t0, t1, grp) in enumerate(GROUPS):
                    n = t1 - t0
                    pb = ((j * 3 + gi) % 2) * 2
                    for half in range(2):
                        for c in range(8):
                            mm(pb + half, PS[pb + half][:, 0:n], wv[:, c, half * 128:(half + 1) * 128], hm[:, c, t0:t1], c == 0, c == 7,
                               reads=["wfi%d" % buf] + hmg(gi))
                    sgb = sg[(j * 3 + gi) % 2]
                    sgn = "sg%d" % ((j * 3 + gi) % 2)
                    act(sgb[:, 0:n], PS[pb][:, 0:n], AF.Silu, reads=[PSN[pb]], writes=[sgn])
                    tt("dve", actT[:, j, t0:t1], sgb[:, 0:n], PS[pb + 1][:, 0:n], ALU.mult, reads=[sgn, PSN[pb + 1]], writes=["actT"])
            for gi, (t0, t1, grp) in enumerate(GROUPS):
                n = t1 - t0
                for j in range(8):
                    bank = 4 + (j % 4)
                    for c in range(22):
                        mm(bank, PS[bank][:, 0:n], w_fo[:, c, j * 128:(j + 1) * 128], actT[:, c, t0:t1], c == 0, c == 21,
                           reads=["w_fo", "actT"])
                    stt("dve", xT[:, j, t0:t1], PS[bank][:, 0:n], modcol(5, j, l, grp), xT[:, j, t0:t1], ALU.mult, ALU.add,
                        reads=[PSN[bank], "MS0", "MS1", "xT:%d" % gi], writes=["xT:%d" % gi])

        def main_program():
            modulation_setup()
            S.alias(["DT"] + NRMN, ["wm"])
            ck("s0")
            for l in range(2):
                if l > 0:
                    S.alias(["w_in", "w_qb", "w_kvb"], ["w_fo"])
                    S.alias(["w_o"], ["wfi0", "wfi1"])
                    S.alias(["DT"] + NRMN, ["actT"])
                S.dma("pool", w_in, w_in_d[l].rearrange("(c p) n -> p c n", p=128), writes=["w_in"])
                S.dma("pool", w_qb, w_qb_d[l].rearrange("(c p) n -> p c n", p=128), writes=["w_qb"])
                S.dma("pool", w_kvb, w_kvb_d[l], writes=["w_kvb"])
                S.dma("pool", w_o, w_o_d[l].rearrange("(c p) n -> p c n", p=128), writes=["w_o"])
                ck("s1")
                layer_tables(l)
                ck("s2")
                norm_phase(l, 0)
                ck("a0")
                if stop == 1:
                    dbg_dump(S, "hm", hm[:], [128, 8, TOK], BF16, ["hm:s%d" % i for i in range(5)])
                    break
                S.alias(SEGN, NRMN)
                seg_project(l, 4)
                ck("a1")
                seg_retention(l, 4)
                S.dma("sp", sp_qk[l], qkT[:].rearrange("p w c i -> p (w c i)"), reads=["qkT"], writes=["sp_qk%d" % l])
                S.dma("sp", sp_v[l], v_tm[:].rearrange("p t n -> p (t n)"), reads=["v_tm"], writes=["sp_v%d" % l])
                S.dma("sp", sp_g[l], g_tm[:].rearrange("p t n -> p (t n)"), reads=["g_tm"], writes=["sp_g%d" % l])
                S.dma("sp", sp_qa[l], qa_n[:].rearrange("p t n -> p (t n)"), reads=["qa_n"], writes=["sp_qa%d" % l])
                ck("a2")
                sample_exchange(l)
                ck("a3")
                for seg in range(4):
                    seg_project(l, seg)
                    seg_retention(l, seg)
                    ck("b1")
                    seg_retention_out(l, seg)
                    ck("b2")
                    seg_q(l, seg)
                    ck("b3")
                    for tl in range(2):
                        mla_up(ckv[:, tl, :], kr[:, tl, :], KcT[0:96, :, tl * 128:(tl + 1) * 128], VX[:, tl], ["ckv", "kr"], "KcT", "VX")
                    ck("b4")
                    seg_attention(l, seg, 2, KcT, lambda kt: VX[:, kt], "KcT", "VX")
                    ck("b5")
                    seg_mix_out(seg)
                    ck("b6")
                S.alias(["q_tm", "k_tm"], ["mix_tm"])
                S.alias(["KcT", "VX"], ["Sacc", "Sin"])
                S.dma("sp", qkT[:].rearrange("p w c i -> p (w c i)"), sp_qk[l], reads=["sp_qk%d" % l], writes=["qkT"])
                S.dma("sp", v_tm[:].rearrange("p t n -> p (t n)"), sp_v[l], reads=["sp_v%d" % l], writes=["v_tm"])
                S.dma("sp", g_tm[:].rearrange("p t n -> p (t n)"), sp_g[l], reads=["sp_g%d" % l], writes=["g_tm"])
                S.dma("sp", qa_n[:].rearrange("p t n -> p (t n)"), sp_qa[l], reads=["sp_qa%d" % l], writes=["qa_n"])
                S.alias(["Sacc", "Sin"], ["KcT", "VX"])
                ck("c0")
                sample_states(l)
                ck("c1")
                seg_retention_out(l, 4)
                seg_q(l, 4)
                ck("c2")
                sample_keys(l)
                ck("c3")
                seg_attention(l, 4, 10, sKT, lambda kt: sVX[:, kt], "sKT", "sVX")
                seg_mix_out(4)
                if stop == 2:
                    dbg_dump(S, "mixT", hm[:], [128, 8, TOK], BF16, ["hm:s%d" % i for i in range(5)])
                    break
                phase_wo(l)
                S.alias(NRMN, SEGN)
                norm_phase(l, 1)
                phase_ffn(l)
                if stop == 3:
                    break


        try:
            main_program()
        except StopBuild:
            pass

        for gi, (t0, t1, grp) in enumerate(GROUPS):
            S.dma("sp", yT_d.rearrange("(c p) t -> p c t", p=128)[:, :, t0:t1], xT[:, :, t0:t1], reads=["xT:%d" % gi], writes=["yT_d%d" % gi])
        S.wait_all("sp")
        print("program built: waits=%d counts=%s dmas=%s r3=%d" % (S.n_wait, S.cnt, S.dcnt, r3o[0]))
    return nc, dbg_out


def _rope_tab(pos_row, pos_col, nf):
    inv = (10000.0 ** (-np.arange(nf, dtype=np.float64) / nf))
    ar = pos_row[:, None].astype(np.float64) * inv
    ac = pos_col[:, None].astype(np.float64) * inv
    C = np.stack([np.cos(ar), np.cos(ac)], 1)
    Sn = np.stack([np.sin(ar), np.sin(ac)], 1)
    return np.stack([C, Sn], 1).astype(np.float32)


def make_inputs(inp):
    f = lambda a: np.ascontiguousarray(np.asarray(a, dtype=np.float32))
    x_prompt, x_sample = f(inp["x_prompt"]), f(inp["x_sample"])
    shared = {k: f(inp[k]) for k in ("w_in", "w_q_b", "w_kv_b", "w_o", "w_ffn_in", "w_ffn_out")}
    w_mod, b_mod = f(inp["w_mod"]), f(inp["b_mod"])
    conds = np.stack([f(inp["c_ctx"]), f(inp["c"])[0], f(inp["c"])[1]], 0)
    condT = f(conds.reshape(3, 8, 128).transpose(2, 1, 0))
    gnT = f(np.stack([f(inp["g_norm_mix"]), f(inp["g_norm_ffn"])], 1).reshape(2, 2, 8, 128).transpose(3, 0, 1, 2))
    gqaT = f(f(inp["g_q_a"]).reshape(2, 2, 128).transpose(2, 0, 1))
    gsm = f(np.concatenate([f(inp[k]) for k in ("g_kv_a", "g_qn", "g_qr", "g_kn", "g_kr", "g_ret_gn", "b_ret_gn")], 1))
    assert gsm.shape == (2, GSM_N)
    retp = f(np.concatenate([f(inp["ret_p_fwd"]), f(inp["ret_p_bwd"])], 1))
    j = np.arange(128)[:, None, None] + 128 * np.arange(2)[None, :, None]
    i = np.arange(256)[None, None, :]
    diff = (i - j).astype(np.float32)
    tabs = f(np.stack([np.maximum(diff, 0), np.maximum(-diff, 0), (diff >= 0).astype(np.float32),
                       (diff <= 0).astype(np.float32)], 1))
    jj = np.arange(128)[:, None] + 128 * np.arange(2)[None, :]
    idxt = f(np.stack([255.0 - jj, jj.astype(np.float64)], 1))
    idxq = f(np.stack([jj + 1.0, 256.0 - jj], 1))
    maps = []
    for core in range(NCORES):
        r, seq = core % 4, core // 4
        xs = np.concatenate([x_prompt[4 * core + b] for b in range(4)] + [x_sample[seq, 256 * r:256 * (r + 1)]], 0)
        n = 256 * r + np.arange(256)
        sel = np.zeros((128, 2), np.float32)
        sel[:, seq] = 1.0
        xco = np.zeros((2, 2, 5), np.float32)
        for s_ in range(4):
            if s_ < r:
                xco[0, 0, s_] = 256.0 * (r - 1 - s_); xco[1, 0, s_] = 1.0
            if s_ > r:
                xco[0, 1, s_] = 256.0 * (s_ - r - 1); xco[1, 1, s_] = 1.0
        xco[0, 0, 4] = 256.0 * r; xco[1, 0, 4] = 1.0
        xco[0, 1, 4] = 256.0 * (3 - r); xco[1, 1, 4] = 1.0
        m = dict(shared)
        m.update({
            "xT": f(xs.T),
            "w_mod_sh": f(w_mod[:, :, 1536 * r:1536 * (r + 1)]),
            "b_mod_sh": f(b_mod[:, 1536 * r:1536 * (r + 1)].reshape(2, 12, 128).transpose(2, 1, 0)),
            "condT": condT, "sel": sel, "gnT": gnT, "gqaT": gqaT, "gsm": gsm, "retp": retp,
            "c_ckv": f(inp["cache_ckv"])[seq], "c_kr": f(inp["cache_krope"])[seq],
            "s0": f(np.stack([f(inp["state_ret_fwd"])[seq], f(inp["state_ret_bwd"])[seq]], 1)),
            "tabs": tabs, "idxt": idxt, "idxq": idxq,
            "rope64": f(_rope_tab(n // 64, n % 64, 16).reshape(2, 128, 2, 2, 16).transpose(1, 0, 2, 3, 4)),
            "rope32": f(_rope_tab(n // 64, n % 64, 8).reshape(2, 128, 2, 2, 8).transpose(1, 0, 2, 3, 4)),
            "xcoef": f(np.broadcast_to(xco[None], (128, 2, 2, 5))),
        })
        maps.append(m)
    return maps


_CACHE = {}


def run(inp, stop=99, dbg=()):
    key = (stop, tuple(dbg))
    if key not in _CACHE:
        _CACHE[key] = build_program(stop, dbg)
    nc, dbg_out = _CACHE[key]
    maps = make_inputs(inp)
    res = run_bass_kernel_spmd(nc, maps, core_ids=list(range(NCORES)))
    return res.results


def kernel(**inp):
    R = run(inp)
    y_prompt = np.zeros((32, 256, D), np.float32)
    y_sample = np.zeros((2, 1024, D), np.float32)
    new_ckv = np.zeros((32, 2, 256, 128), np.float32)
    new_kr = np.zeros((32, 2, 256, 32), np.float32)
    new_sf = np.zeros((32, 2, 8, 64, 64), np.float32)
    new_sb = np.zeros((32, 2, 8, 64, 64), np.float32)
    for core in range(NCORES):
        r, seq = core % 4, core // 4
        o = R[core]
        y = np.asarray(o["yT"]).T
        y_prompt[4 * core:4 * core + 4] = y[0:1024].reshape(4, 256, D)
        y_sample[seq, 256 * r:256 * (r + 1)] = y[1024:1280]
        new_ckv[4 * core:4 * core + 4] = np.asarray(o["o_ckv"]).reshape(2, 4, 256, 128).transpose(1, 0, 2, 3)
        new_kr[4 * core:4 * core + 4] = np.asarray(o["o_kr"]).reshape(2, 4, 256, 32).transpose(1, 0, 2, 3)
        new_sf[4 * core:4 * core + 4] = np.asarray(o["o_sf"]).transpose(1, 0, 2, 3, 4)
        new_sb[4 * core:4 * core + 4] = np.asarray(o["o_sb"]).transpose(1, 0, 2, 3, 4)
    return (y_prompt, y_sample, new_ckv, new_kr, new_sf, new_sb)
```

```python
import os
import numpy as np
from contextlib import ExitStack
import concourse.bass as bass
import concourse.mybir as mybir
from concourse.bass_utils import run_bass_kernel_spmd

F32 = mybir.dt.float32
BF16 = mybir.dt.bfloat16
ALU = mybir.AluOpType
AF = mybir.ActivationFunctionType
AX = mybir.AxisListType

N_DMA_SEMS = 12
NCORES = 8
D = 1024
TOK = 1280
NTILE = 10
IN_COLS = 2464
DFF = 2816
EPS = 1e-6
LN2 = float(np.log(2.0))
ATT_SCALE = float(96 ** -0.5)

G_KVA, G_QN, G_QR, G_KN, G_KR, G_GN, B_GN = 0, 128, 192, 224, 288, 320, 832
GSM_N = 1344


class StopBuild(Exception):
    pass


class Sync:
    def __init__(self, nc, stack):
        self.nc = nc
        self.stack = stack
        self.engs = {"pe": nc.tensor, "act": nc.scalar, "dve": nc.vector,
                     "pool": nc.gpsimd, "sp": nc.sync}
        self.sem = {k: stack.enter_context(nc.semaphore("sem_" + k)) for k in self.engs}
        self.cnt = {k: 0 for k in self.engs}
        self.seen = {k: {} for k in self.engs}
        self.dsem = {q: [stack.enter_context(nc.semaphore("dsem_%s%d" % (q, i)))
                         for i in range(N_DMA_SEMS)] for q in ("sp", "pool")}
        self.dcnt = {q: 0 for q in self.dsem}
        self.ccsem = []
        self.res = {}
        self.n_wait = 0

    def _semof(self, kind, a):
        if kind == "eng":
            return self.sem[a]
        if kind == "dma":
            return self.dsem[a[0]][a[1]]
        return self.ccsem[a]

    def _wait(self, e, tok):
        if tok is None:
            return
        kind, a, v = tok
        key = (kind, a)
        if self.seen[e].get(key, 0) >= v:
            return
        self.seen[e][key] = v
        self.engs[e].wait_ge(self._semof(kind, a), v)
        self.n_wait += 1

    def _deps(self, e, reads, writes, is_dma):
        deps = []
        for r in reads:
            st = self.res.get(r)
            if st and st["w"] is not None:
                deps.append(st["w"])
        for w in writes:
            st = self.res.get(w)
            if st:
                for t in [st["w"]] + list(st["r"].values()):
                    if t is None:
                        continue
                    if (not is_dma) and t[0] == "eng" and t[1] == e:
                        continue
                    deps.append(t)
        for t in deps:
            self._wait(e, t)

    def _update(self, tok, reads, writes):
        for r in reads:
            st = self.res.setdefault(r, {"w": None, "r": {}})
            st["r"][(tok[0], tok[1])] = tok
        for w in writes:
            self.res[w] = {"w": tok, "r": {}}

    @staticmethod
    def _split(reads, writes):
        writes = list(writes) + [r for r in reads if r.startswith("ps:")]
        reads = [r for r in reads if not r.startswith("ps:")]
        return reads, writes

    def op(self, e, fn, reads=(), writes=(), inc=True):
        reads, writes = self._split(reads, writes)
        self._deps(e, reads, writes, False)
        inst = fn(self.engs[e])
        if inc:
            self.cnt[e] += 1
            inst.then_inc(self.sem[e], 1)
            tok = ("eng", e, self.cnt[e])
        else:
            tok = ("eng", e, self.cnt[e] + 1)
        self._update(tok, reads, writes)
        return tok

    def dma(self, q, out, in_, reads=(), writes=(), **kw):
        self._deps(q, reads, writes, True)
        n = self.dcnt[q]
        self.dcnt[q] += 1
        slot = n % N_DMA_SEMS
        prev = 16 * (n // N_DMA_SEMS)
        if prev > 0:
            self._wait(q, ("dma", (q, slot), prev))
        inst = self.engs[q].dma_start(out=out, in_=in_, **kw)
        inst.then_inc(self.dsem[q][slot], 16)
        tok = ("dma", (q, slot), prev + 16)
        self._update(tok, reads, writes)
        return tok

    def collective(self, ins, outs, groups, reads, writes):
        self._deps("pool", reads, writes, True)
        if self.ccsem:
            self._wait("pool", ("cc", len(self.ccsem) - 1, 1))
        sem = self.stack.enter_context(self.nc.semaphore("ccsem%d" % len(self.ccsem)))
        self.ccsem.append(sem)
        inst = self.nc.gpsimd.collective_compute("AllGather", ALU.bypass, replica_groups=groups,
                                                 ins=ins, outs=outs)
        inst.then_inc(sem)
        tok = ("cc", len(self.ccsem) - 1, 1)
        self._update(tok, reads, writes)
        return tok

    def alias(self, new_names, old_names):
        merged = {}
        for o in old_names:
            st = self.res.get(o)
            if not st:
                continue
            toks = list(st["r"].values()) + ([st["w"]] if st["w"] is not None else [])
            for t in toks:
                k = (t[0], t[1])
                if k not in merged or merged[k][2] < t[2]:
                    merged[k] = t
        for n in new_names:
            st = self.res.setdefault(n, {"w": None, "r": {}})
            for k, t in merged.items():
                if k not in st["r"] or st["r"][k][2] < t[2]:
                    st["r"][k] = t

    def wait_all(self, e):
        for r, st in self.res.items():
            if st["w"] is not None:
                self._wait(e, st["w"])


def bcast(ap, shape):
    return ap.to_broadcast(shape)


def build_program(stop=99, dbg=()):
    nc = bass.Bass("TRN2", target_bir_lowering=False)
    dbg_out = {}

    def din(name, shape, dt=F32):
        return nc.dram_tensor(name, list(shape), dt, kind="ExternalInput").ap()

    def dout(name, shape, dt=F32):
        return nc.dram_tensor(name, list(shape), dt, kind="ExternalOutput").ap()

    xT_d = din("xT", [D, TOK])
    w_in_d = din("w_in", [2, D, IN_COLS])
    w_qb_d = din("w_q_b", [2, 256, 768])
    w_kvb_d = din("w_kv_b", [2, 128, 1024])
    w_o_d = din("w_o", [2, D, D])
    w_fi_d = din("w_ffn_in", [2, D, 2 * DFF])
    w_fo_d = din("w_ffn_out", [2, DFF, D])
    w_mod_d = din("w_mod_sh", [2, D, 1536])
    b_mod_d = din("b_mod_sh", [128, 12, 2])
    cond_d = din("condT", [128, 8, 3])
    sel_d = din("sel", [128, 2])
    gn_d = din("gnT", [128, 2, 2, 8])
    gqa_d = din("gqaT", [128, 2, 2])
    gsm_d = din("gsm", [2, GSM_N])
    retp_d = din("retp", [2, 16])
    cckv_d = din("c_ckv", [2, 256, 128])
    ckr_d = din("c_kr", [2, 256, 32])
    s0_d = din("s0", [2, 2, 8, 64, 64])
    tabs_d = din("tabs", [128, 4, 2, 256])
    idx_d = din("idxt", [128, 2, 2])
    idxq_d = din("idxq", [128, 2, 2])
    rope64_d = din("rope64", [128, 2, 2, 2, 16])
    rope32_d = din("rope32", [128, 2, 2, 2, 8])
    xco_d = din("xcoef", [128, 2, 2, 5])

    yT_d = dout("yT", [D, TOK])
    ockv_d = dout("o_ckv", [2, 1024, 128])
    okr_d = dout("o_kr", [2, 1024, 32])
    osf_d = dout("o_sf", [2, 4, 8, 64, 64])
    osb_d = dout("o_sb", [2, 4, 8, 64, 64])

    mb_c = nc.dram_tensor("mod_bounce", [32, 512], F32).ap()
    mg_c = nc.dram_tensor("mod_gath", [128, 512], F32).ap()
    mb_d = mb_c.rearrange("a b -> (a b)").rearrange("(p f) -> p f", f=128)
    mg_d = mg_c.rearrange("a b -> (a b)").rearrange("(p f) -> p f", f=128)
    kvb_c = [nc.dram_tensor("kv_bounce%d" % l, [80, 512], F32).ap() for l in range(2)]
    kvg_c = [nc.dram_tensor("kv_gath%d" % l, [320, 512], F32).ap() for l in range(2)]
    kvb_d = [a.rearrange("a b -> (a b)").rearrange("(t n) -> t n", n=160) for a in kvb_c]
    kvg_d = [a.rearrange("a b -> (a b)").rearrange("(t n) -> t n", n=160) for a in kvg_c]
    ub_c = [nc.dram_tensor("u_bounce%d" % l, [128, 512], F32).ap() for l in range(2)]
    ug_c = [nc.dram_tensor("u_gath%d" % l, [512, 512], F32).ap() for l in range(2)]
    ub_d = [a.rearrange("a b -> (a b)").rearrange("(t n) -> t n", n=64) for a in ub_c]
    ug_d = [a.rearrange("a b -> (a b)").rearrange("(t n) -> t n", n=64) for a in ug_c]

    sp_qk = [nc.dram_tensor("spill_qk%d" % l, [128, 2048], BF16).ap() for l in range(2)]
    sp_v = [nc.dram_tensor("spill_v%d" % l, [128, 1024], BF16).ap() for l in range(2)]
    sp_g = [nc.dram_tensor("spill_g%d" % l, [128, 1024], BF16).ap() for l in range(2)]
    sp_qa = [nc.dram_tensor("spill_qa%d" % l, [128, 512], BF16).ap() for l in range(2)]

    def dbg_dump(S, name, ap, shape, dt, reads):
        if name in dbg:
            o = dout("dbg_" + name, shape, dt)
            S.dma("sp", o, ap, reads=reads, writes=["dbgo_" + name])
            dbg_out[name] = (shape, dt)

    with ExitStack() as st:
        S = Sync(nc, st)
        ncd = nc.allow_non_contiguous_dma(reason="small strided parameter loads")
        st.enter_context(ncd)

        def sb(name, shape, dt):
            return st.enter_context(nc.sbuf_tensor(name, list(shape), dt))

        xT = sb("xT_sb", [128, 8, TOK], F32)
        hm = sb("hm_sb", [128, 8, TOK], BF16)
        R1 = sb("R1", [128, 22 * 1024], BF16)
        R2 = R1[:, 0:8192] if os.environ.get("SHRINK") else sb("R2", [128, 8 * 1024], BF16)
        R3 = sb("R3", [128, 35 * 1024], BF16)
        w_in = R1[:, 0:8 * IN_COLS].rearrange("p (c n) -> p c n", c=8)
        w_qb = R1[:, 8 * IN_COLS:8 * IN_COLS + 1536].rearrange("p (c n) -> p c n", c=2)
        w_kvb = R1[:, 8 * IN_COLS + 1536:8 * IN_COLS + 2560]
        w_fo = R1[:, :].rearrange("p (c n) -> p c n", c=22)
        sKT = R1[:, 0:8 * TOK].rearrange("p (h t) -> p h t", h=8)
        sVX = R1[:, 8 * TOK:8 * TOK + 10 * 520].rearrange("p (k h e) -> p k h e", k=10, h=8)
        w_o = R2[:, :].rearrange("p (c n) -> p c n", c=8)
        w_fi = [R2[:, i * 2048:(i + 1) * 2048].rearrange("p (c n) -> p c n", c=8) for i in range(4)]
        actT = R3[:, 0:22 * TOK].rearrange("p (c t) -> p c t", c=22)

        R3N = 35 * 1024
        r3o = [0]

        def r3(nelem_bf16, dt=BF16):
            o = r3o[0]
            r3o[0] += (nelem_bf16 + 15) // 16 * 16
            assert r3o[0] <= R3N, r3o[0]
            v = R3[:, o:o + nelem_bf16]
            return v.bitcast(F32) if dt == F32 else v

        DT = r3(2 * 8 * 512, F32).rearrange("p (h j i) -> p h j i", h=8, j=2)
        seg_base = r3o[0]
        qk_off = r3o[0]
        q_tm = r3(2 * 512).rearrange("p (t n) -> p t n", t=2)
        k_tm = r3(2 * 512).rearrange("p (t n) -> p t n", t=2)
        mix_tm = R3[:, qk_off:qk_off + 2048].rearrange("p (t n) -> p t n", t=2)
        v_tm = r3(2 * 512).rearrange("p (t n) -> p t n", t=2)
        g_tm = r3(2 * 512).rearrange("p (t n) -> p t n", t=2)
        kdec = r3(2 * 2 * 512).rearrange("p (d t n) -> p d t n", d=2, t=2)
        qkT = r3(2 * 4 * 256).rearrange("p (w c i) -> p w c i", w=2, c=4)
        qa_n = r3(2 * 256).rearrange("p (t n) -> p t n", t=2)
        qaT = r3(2 * 256).rearrange("p (c i) -> p c i", c=2)
        ckv = r3(2 * 2 * 128, F32).rearrange("p (t n) -> p t n", t=2)
        kr = r3(2 * 2 * 32, F32).rearrange("p (t n) -> p t n", t=2)
        ckvb = r3(128)
        ckvT = r3(128)
        Qc_tm = r3(2 * 768).rearrange("p (t h e) -> p t h e", t=2, h=8)
        QcT = r3(8 * 256).rearrange("p (h i) -> p h i", h=8)
        Kc_tm = r3(768).rearrange("p (h e) -> p h e", h=8)
        kc_off = r3o[0]
        KcT = r3(8 * 256).rearrange("p (h i) -> p h i", h=8)
        vx_off = r3o[0]
        VX = r3(2 * 520).rearrange("p (t h e) -> p t h e", t=2, h=8)
        Sacc = R3[:, kc_off:kc_off + 2048].bitcast(F32).rearrange("p (d h v) -> p d h v", d=2, h=8)
        Sin = R3[:, vx_off:vx_off + 1024].rearrange("p (d h v) -> p d h v", d=2, h=8)
        sqt = r3(2 * 768, F32)
        fA = r3(2 * 512, F32)
        fB = r3(2 * 512, F32)
        AT = [r3(512).rearrange("p (j i) -> p j i", j=2) for _ in range(2)]
        PT = [r3(256) for _ in range(2)]
        Ust = r3(2 * 2 * 512, F32).rearrange("p (d n) -> p d n", d=2)
        Ug = r3(2 * 512, F32).rearrange("p (h v) -> p h v", h=8)
        kvin = r3(2 * 160, F32)
        kvin2 = r3(2 * 160, F32)
        ckvb2 = r3(128)
        ckvT2 = r3(128)
        Kc_tm2 = r3(768).rearrange("p (h e) -> p h e", h=8)
        SEGN = ["q_tm", "k_tm", "v_tm", "g_tm", "kdec", "qkT", "qa_n", "qaT", "ckv", "kr", "ckvb", "ckvT", "Qc_tm", "QcT",
                "Kc_tm", "KcT", "VX", "sqt", "sqtr", "sqtr2", "fqk", "fkr", "fA", "fB", "AT0", "AT1", "PT0", "PT1", "mix_tm",
                "Ust", "Sin", "Ug", "Sacc", "kvin", "kvin1", "ckvb1", "ckvT1", "Kc_tm1"]
        seg_end = r3o[0]
        r3o[0] = seg_base
        nsq = r3(2 * 8 * 512, F32).rearrange("p (c t) -> p c t", c=8)
        s8 = r3(2 * 512, F32)
        rstd = r3(2 * 512, F32)
        ntmp = [r3(2 * 512, F32) for i in range(3)]
        tabs = r3(2 * 4 * 512, F32).rearrange("p (k j i) -> p k j i", k=4, j=2)
        dtmp = r3(2 * 2 * 512, F32).rearrange("p (k j i) -> p k j i", k=2, j=2)
        NRMN = ["nsq", "s8", "rstd", "ntmp0", "ntmp1", "ntmp2", "tabs", "dtmp0", "dtmp1"]
        print("R3 usage: seg_end=%d norm_end=%d of %d" % (seg_end, r3o[0], R3N))
        wm = R3[:, 8192:8192 + 2 * 8 * 1536].rearrange("p (l c n) -> p l c n", l=2, c=8)

        ones = sb("ones", [128, 128], F32)
        ident = sb("ident", [128, 128], BF16)
        identf = sb("identf", [128, 128], F32)
        epsc = sb("epsc", [128, 1], F32)
        onec = sb("onec", [128, 1], F32)
        condT = sb("condT_sb", [128, 8, 3], F32)
        sT = sb("sT", [128, 8, 3], BF16)
        sel = sb("sel_sb", [128, 2], F32)
        bmT = sb("bmT", [128, 12, 2], F32)
        mlp = sb("mlp", [128, 128], F32)
        ml = mlp[:, 0:72].rearrange("p (j l k) -> p j l k", j=12, l=2)
        MODT = sb("MODT", [128, 48, 2, 3], F32)
        MS = sb("MS", [128, 48, 2, 2], F32)
        gn = sb("gn_sb", [128, 2, 2, 8], F32)
        GG = sb("GG", [128, 2, 2, 2, 8], F32)
        gqa = sb("gqa_sb", [128, 2, 2], F32)
        gsm = sb("gsm_sb", [128, GSM_N], F32)
        gq96 = sb("gq96", [128, 96], F32)
        RP = sb("RP", [128, 16], F32)
        LG = sb("LG", [128, 2, 8], F32)
        KD = sb("KD", [128, 2, 2, 8], F32)
        KQ = sb("KQ", [128, 2, 2, 8], F32)
        idxq = sb("idxq_sb", [128, 2, 2], F32)
        idxt = sb("idxt_sb", [128, 2, 2], F32)
        rope64 = sb("rope64_sb", [128, 2, 2, 2, 16], F32)
        rope32 = sb("rope32_sb", [128, 2, 2, 2, 8], F32)
        xco = sb("xco_sb", [128, 2, 2, 5], F32)
        xcf = sb("xcf", [128, 2, 5, 8], F32)
        st8 = sb("st8", [128, 64], F32)
        sg = [sb("sg%d" % i, [128, 512], F32) for i in range(2)]

        print("SBUF bytes remaining after allocation:", nc.sbuf_bytes_remaining)
        PS = [st.enter_context(nc.psum_tensor("psb%d" % i, [128, 512], F32)) for i in range(8)]
        PSN = ["ps:%d" % i for i in range(8)]

        def mm(bank, out, lhsT, rhs, start, stop, reads):
            S.op("pe", lambda e: e.matmul(out, lhsT=lhsT, rhs=rhs, start=start, stop=stop),
                 reads=reads, writes=[PSN[bank]], inc=bool(stop))

        def tr(bank, out, in_, reads, last=True):
            S.op("pe", lambda e: e.transpose(out, in_, ident[:in_.shape[0], :in_.shape[0]]),
                 reads=list(reads) + ["ident"], writes=[PSN[bank]], inc=last)

        def act(out, in_, func, reads, writes, **kw):
            S.op("act", lambda e: e.activation(out=out, in_=in_, func=func, **kw), reads=reads, writes=writes)

        def tt(eng, out, in0, in1, op, reads, writes):
            S.op(eng, lambda e: e.tensor_tensor(out=out, in0=in0, in1=in1, op=op), reads=reads, writes=writes)

        def ts(eng, out, in0, s1, s2, op0, op1, reads, writes):
            if s2 is None:
                S.op(eng, lambda e: e.tensor_scalar(out=out, in0=in0, scalar1=s1, scalar2=None, op0=op0),
                     reads=reads, writes=writes)
            else:
                S.op(eng, lambda e: e.tensor_scalar(out=out, in0=in0, scalar1=s1, scalar2=s2, op0=op0, op1=op1),
                     reads=reads, writes=writes)

        def stt(eng, out, in0, scalar, in1, op0, op1, reads, writes):
            S.op(eng, lambda e: e.scalar_tensor_tensor(out=out, in0=in0, scalar=scalar, in1=in1, op0=op0, op1=op1),
                 reads=reads, writes=writes)

        def red(out, in_, reads, writes):
            S.op("dve", lambda e: e.tensor_reduce(out=out, in_=in_, axis=AX.X, op=ALU.add), reads=reads, writes=writes)

        def rsqrt(buf, scale, reads_writes):
            act(buf, buf, AF.Sqrt, reads=[reads_writes, "epsc"], writes=[reads_writes], scale=scale, bias=epsc[:buf.shape[0], :])
            S.op("dve", lambda e: e.reciprocal(out=buf, in_=buf), reads=[reads_writes], writes=[reads_writes])

        def ck(name):
            if stop == name:
                raise StopBuild()

        S.dma("sp", xT[:], xT_d.rearrange("(c p) t -> p c t", p=128), writes=["xT"])
        S.dma("sp", condT[:], cond_d, writes=["condT"])
        S.dma("sp", sel[:], sel_d, writes=["sel"])
        S.dma("sp", bmT[:], b_mod_d, writes=["bmT"])
        S.dma("sp", gn[:], gn_d, writes=["gn"])
        S.dma("sp", gqa[:], gqa_d, writes=["gqa"])
        S.dma("sp", idxt[:], idx_d, writes=["idxt"])
        S.dma("sp", idxq[:], idxq_d, writes=["idxq"])
        S.dma("sp", rope64[:], rope64_d, writes=["rope64"])
        S.dma("sp", rope32[:], rope32_d, writes=["rope32"])
        S.dma("sp", xco[:], xco_d, writes=["xco"])
        S.dma("pool", wm, w_mod_d.rearrange("l (c p) n -> p l c n", p=128), writes=["wm"])

        S.op("pool", lambda e: e.memset(ones[:], 1.0), writes=["ones"])
        S.op("pool", lambda e: e.memset(mlp[:], 0.0), writes=["ml"])
        S.op("pool", lambda e: e.memset(epsc[:], EPS), writes=["epsc"])
        S.op("pool", lambda e: e.memset(onec[:], 1.0), writes=["onec"])
        S.op("pool", lambda e: e.memset(identf[:], 0.0), writes=["identf"])
        S.op("pool", lambda e: e.affine_select(out=identf[:], in_=identf[:], pattern=[[-1, 128]],
                                               compare_op=ALU.not_equal, fill=1.0, base=0, channel_multiplier=1),
             reads=["identf"], writes=["identf"])
        S.op("dve", lambda e: e.tensor_copy(ident[:], identf[:]), reads=["identf"], writes=["ident"])

        def modulation_setup():
            ck("t0")
            act(sT[:], condT[:], AF.Silu, reads=["condT"], writes=["sT"])
            pm = PS[0][:, 0:72].rearrange("p (j l k) -> p j l k", j=12, l=2)
            for l in range(2):
                for cj in range(12):
                    for c in range(8):
                        mm(0, pm[:, cj, l, :], wm[:, l, c, cj * 128:(cj + 1) * 128], sT[:, c, :], c == 0, c == 7,
                           reads=["wm", "sT"])
            tt("dve", ml, pm, bcast(bmT[:].unsqueeze(3), [128, 12, 2, 3]), ALU.add, reads=[PSN[0], "bmT", "ml"], writes=["ml"])
            ck("t1")
            S.dma("sp", mb_d, mlp[:], reads=["ml"], writes=["mb_d"])
            S.collective([mb_c], [mg_c], [[0, 1, 2, 3], [4, 5, 6, 7]], reads=["mb_d"], writes=["mg_d"])
            S.dma("sp", MODT[:].rearrange("p (r j) l k -> p r (j l k)", r=4), mg_d.rearrange("(r p) f -> p r f", p=128)[:, :, 0:72],
                  reads=["mg_d"], writes=["MODT"])
            ck("t2")
            S.op("dve", lambda e: e.tensor_copy(MS[:, :, :, 0], MODT[:, :, :, 0]), reads=["MODT"], writes=["MS0"])
            ts("dve", MS[:, :, :, 1], MODT[:, :, :, 1], sel[:, 0:1], None, ALU.mult, None, reads=["MODT", "sel"], writes=["MS1"])
            stt("dve", MS[:, :, :, 1], MODT[:, :, :, 2], sel[:, 1:2], MS[:, :, :, 1], ALU.mult, ALU.add,
                reads=["MODT", "sel", "MS1"], writes=["MS1"])
            for l in range(2):
                for grp in range(2):
                    for which, k in ((0, 1), (1, 4)):
                        stt("dve", GG[:, l, grp, which, :], MS[:, k * 8:(k + 1) * 8, l, grp], 1.0, gn[:, l, which, :],
                            ALU.add, ALU.mult, reads=["MS0", "MS1", "gn"], writes=["GG"])

        def modcol(k, c, l, grp):
            return MS[:, k * 8 + c, l, grp:grp + 1]

        GROUPS = [(0, 512, 0), (512, 1024, 0), (1024, 1280, 1)]

        def norm_phase(l, which):
            kk_sh = 0 if which == 0 else 3
            for gi, (t0, t1, grp) in enumerate(GROUPS):
                n = t1 - t0
                act(nsq[:, :, 0:n], xT[:, :, t0:t1], AF.Square, reads=["xT:%d" % gi], writes=["nsq"])
                red(s8[:, 0:n], nsq[:, :, 0:n].rearrange("p c t -> p t c"), reads=["nsq"], writes=["s8"])
                mm(7, PS[7][:, 0:n], ones[:], s8[:, 0:n], True, True, reads=["ones", "s8"])
                act(rstd[:, 0:n], PS[7][:, 0:n], AF.Sqrt, reads=[PSN[7], "epsc"], writes=["rstd"], scale=1.0 / D, bias=epsc[:])
                S.op("dve", lambda e: e.reciprocal(out=rstd[:, 0:n], in_=rstd[:, 0:n]), reads=["rstd"], writes=["rstd"])
                for c in range(8):
                    tb = ntmp[c % 3]
                    stt("dve", tb[:, 0:n], xT[:, c, t0:t1], GG[:, l, grp, which, c:c + 1], rstd[:, 0:n], ALU.mult, ALU.mult,
                        reads=["xT:%d" % gi, "GG", "rstd"], writes=["ntmp%d" % (c % 3)])
                    act(hm[:, c, t0:t1], tb[:, 0:n], AF.Identity, reads=["ntmp%d" % (c % 3), "MS0", "MS1"],
                        writes=hmg(gi), bias=modcol(kk_sh, c, l, grp), scale=1.0)


        S.alias(["xT:0", "xT:1", "xT:2"], ["xT"])

        GSEG = [[0, 1], [2, 3], [4]]

        def hmg(gi):
            return ["hm:s%d" % sg_ for sg_ in GSEG[gi]]

        def layer_tables(l):
            S.dma("sp", gsm[:], gsm_d[l].partition_broadcast(128), writes=["gsm"])
            S.dma("sp", RP[:], retp_d[l].partition_broadcast(128), writes=["RP"])
            S.dma("sp", tabs, tabs_d, writes=["tabs"])
            ts("dve", gq96[:], gsm[:, G_QN:G_QN + 96], ATT_SCALE, None, ALU.mult, None, reads=["gsm"], writes=["gq96"])
            act(LG[:].rearrange("p d h -> p (d h)"), RP[:], AF.Exp, reads=["RP"], writes=["LG"], scale=-LN2)
            act(LG[:].rearrange("p d h -> p (d h)"), LG[:].rearrange("p d h -> p (d h)"), AF.Ln, reads=["LG", "onec"], writes=["LG"],
                scale=-1.0, bias=onec[:])
            for h in range(8):
                act(dtmp[:, 0], tabs[:, 0], AF.Exp, reads=["tabs", "LG"], writes=["dtmp0"], scale=LG[:, 0, h:h + 1])
                act(dtmp[:, 1], tabs[:, 1], AF.Exp, reads=["tabs", "LG"], writes=["dtmp1"], scale=LG[:, 1, h:h + 1])
                tt("dve", dtmp[:, 0], dtmp[:, 0], tabs[:, 2], ALU.mult, reads=["dtmp0", "tabs"], writes=["dtmp0"])
                tt("dve", dtmp[:, 1], dtmp[:, 1], tabs[:, 3], ALU.mult, reads=["dtmp1", "tabs"], writes=["dtmp1"])
                tt("dve", DT[:, h], dtmp[:, 0], dtmp[:, 1], ALU.add, reads=["dtmp0", "dtmp1"], writes=["DT"])
            for d in range(2):
                for jt in range(2):
                    act(KD[:, d, jt, :], LG[:, d, :], AF.Exp, reads=["LG", "idxt"], writes=["KD"], scale=idxt[:, d, jt:jt + 1])
                for it in range(2):
                    act(KQ[:, d, it, :], LG[:, d, :], AF.Exp, reads=["LG", "idxq"], writes=["KQ"], scale=idxq[:, d, it:it + 1])
            for d in range(2):
                tt("dve", xcf[:, d], bcast(LG[:, d, :].unsqueeze(1), [128, 5, 8]), bcast(xco[:, 0, d, :].unsqueeze(2), [128, 5, 8]),
                   ALU.mult, reads=["LG", "xco"], writes=["xcf"])
            act(xcf[:].rearrange("p d s h -> p (d s h)"), xcf[:].rearrange("p d s h -> p (d s h)"), AF.Exp, reads=["xcf"], writes=["xcf"])
            for d in range(2):
                tt("dve", xcf[:, d], xcf[:, d], bcast(xco[:, 1, d, :].unsqueeze(2), [128, 5, 8]), ALU.mult,
                   reads=["xcf", "xco"], writes=["xcf"])

        def rope(out, src, tl, tab, H, nf, reads, writes, scale=None):
            sv = src.rearrange("p (h f x n) -> p h f x n", h=H, f=2, x=2)
            ov = out.rearrange("p (h f x n) -> p h f x n", h=H, f=2, x=2)
            C = bcast(tab[:, tl, 0].unsqueeze(1), [128, H, 2, nf])
            Sn = bcast(tab[:, tl, 1].unsqueeze(1), [128, H, 2, nf])
            a = fA[:, 0:H * 2 * nf].rearrange("p (h f n) -> p h f n", h=H, f=2)
            b = fB[:, 0:H * 2 * nf].rearrange("p (h f n) -> p h f n", h=H, f=2)
            x1, x2 = sv[:, :, :, 0, :], sv[:, :, :, 1, :]
            tt("dve", a, x1, C, ALU.mult, reads=reads, writes=["fA"])
            tt("dve", b, x2, Sn, ALU.mult, reads=reads, writes=["fB"])
            tt("dve", ov[:, :, :, 0, :], a, b, ALU.subtract, reads=["fA", "fB"], writes=writes)
            tt("dve", a, x1, Sn, ALU.mult, reads=reads, writes=["fA"])
            tt("dve", b, x2, C, ALU.mult, reads=reads, writes=["fB"])
            tt("dve", ov[:, :, :, 1, :], a, b, ALU.add, reads=["fA", "fB"], writes=writes)

        def seg_project(l, seg, emit_out=True):
            is_s = seg == 4
            S.alias(["q_tm", "k_tm"], ["mix_tm"])
            S.alias(["KcT", "VX"], ["Sacc", "Sin"])
            gi = 2 if is_s else seg // 2
            for tl in range(2):
                tile = seg * 2 + tl
                t0 = tile * 128
                for cg in range(5):
                    c0 = cg * 512
                    ncol = min(512, IN_COLS - c0)
                    for c in range(8):
                        mm(cg, PS[cg][:, 0:ncol], hm[:, c, t0:t0 + 128], w_in[:, c, c0:c0 + ncol], c == 0, c == 7,
                           reads=["hm:s%d" % seg, "w_in"])
                if is_s:
                    rope(fqk0, PS[0][:, :], tl, rope64, 8, 16, reads=[PSN[0], "rope64"], writes=["fqk"])
                    S.op("act", lambda e: e.copy(q_tm[:, tl, :], fqk0), reads=["fqk"], writes=["q_tm"])
                else:
                    S.op("act", lambda e: e.copy(q_tm[:, tl, :], PS[0][:, :]), reads=[PSN[0]], writes=["q_tm"])
                if is_s:
                    rope(fqk0, PS[1][:, :], tl, rope64, 8, 16, reads=[PSN[1], "rope64"], writes=["fqk"])
                    S.op("act", lambda e: e.mul(k_tm[:, tl, :], fqk0, 0.125), reads=["fqk"], writes=["k_tm"])
                else:
                    S.op("act", lambda e: e.mul(k_tm[:, tl, :], PS[1][:, :], 0.125), reads=[PSN[1]], writes=["k_tm"])
                S.op("dve", lambda e: e.tensor_copy(v_tm[:, tl, :], PS[2][:, :]), reads=[PSN[2]], writes=["v_tm"])
                act(g_tm[:, tl, :], PS[3][:, :], AF.Silu, reads=[PSN[3]], writes=["g_tm"])
                zq = PS[4]
                act(sqt[:, 0:416], zq[:, 0:416], AF.Square, reads=[PSN[4]], writes=["sqt"])
                red(st8[:, 0:1], sqt[:, 0:256], reads=["sqt"], writes=["st8"])
                red(st8[:, 1:2], sqt[:, 256:384], reads=["sqt"], writes=["st8"])
                red(st8[:, 2:3], sqt[:, 384:416], reads=["sqt"], writes=["st8"])
                act(st8[:, 0:1], st8[:, 0:1], AF.Sqrt, reads=["st8", "epsc"], writes=["st8"], scale=1.0 / 256, bias=epsc[:])
                act(st8[:, 1:2], st8[:, 1:2], AF.Sqrt, reads=["st8", "epsc"], writes=["st8"], scale=1.0 / 128, bias=epsc[:])
                act(st8[:, 2:3], st8[:, 2:3], AF.Sqrt, reads=["st8", "epsc"], writes=["st8"], scale=1.0 / 32, bias=epsc[:])
                S.op("dve", lambda e: e.reciprocal(out=st8[:, 0:3], in_=st8[:, 0:3]), reads=["st8"], writes=["st8"])
                ts("dve", qa_n[:, tl, :], zq[:, 0:256], st8[:, 0:1], None, ALU.mult, None, reads=[PSN[4], "st8"], writes=["qa_n"])
                stt("dve", ckv[:, tl, :], zq[:, 256:384], st8[:, 1:2], gsm[:, G_KVA:G_KVA + 128], ALU.mult, ALU.mult,
                    reads=[PSN[4], "st8", "gsm"], writes=["ckv"])
                if is_s:
                    stt("dve", fB[:, 256:288], zq[:, 384:416], st8[:, 2:3], gsm[:, G_KR:G_KR + 32], ALU.mult, ALU.mult,
                        reads=[PSN[4], "st8", "gsm"], writes=["fkr"])
                    rope(kr[:, tl, :], fB[:, 256:288], tl, rope32, 1, 8, reads=["fkr", "rope32"], writes=["kr"])
                else:
                    stt("dve", kr[:, tl, :], zq[:, 384:416], st8[:, 2:3], gsm[:, G_KR:G_KR + 32], ALU.mult, ALU.mult,
                        reads=[PSN[4], "st8", "gsm"], writes=["kr"])
            if not emit_out:
                return
            if is_s:
                S.dma("sp", kvb_d[l][:, 0:128].rearrange("(t p) n -> p t n", p=128), ckv[:], reads=["ckv"], writes=["kvb_d%d" % l])
                S.dma("sp", kvb_d[l][:, 128:160].rearrange("(t p) n -> p t n", p=128), kr[:], reads=["kr"], writes=["kvb_d%d" % l])
            else:
                S.dma("sp", ockv_d[l, seg * 256:(seg + 1) * 256, :].rearrange("(t p) n -> p t n", p=128), ckv[:], reads=["ckv"], writes=["ockv"])
                S.dma("sp", okr_d[l, seg * 256:(seg + 1) * 256, :].rearrange("(t p) n -> p t n", p=128), kr[:], reads=["kr"], writes=["okr"])

        fqk0 = sqt[:, 0:512]

        def seg_q(l, seg):
            is_s = seg == 4
            for tl in range(2):
                pt = PS[5][:, 0:128].bitcast(BF16).rearrange("p (c i) -> p c i", c=2)
                for c in range(2):
                    tr(5, pt[:, c, :], qa_n[:, tl, c * 128:(c + 1) * 128], reads=["qa_n"], last=(c == 1))
                for c in range(2):
                    act(qaT[:, c, tl * 128:(tl + 1) * 128], pt[:, c, :], AF.Copy, reads=[PSN[5], "gqa"], writes=["qaT"],
                        scale=gqa[:, l, c:c + 1])
                for half in range(2):
                    for c in range(2):
                        mm(6 + half, PS[6 + half][:, 0:384], qaT[:, c, tl * 128:(tl + 1) * 128],
                           w_qb[:, c, half * 384:(half + 1) * 384], c == 0, c == 1, reads=["qaT", "w_qb"])
                for half in range(2):
                    qv = PS[6 + half][:, 0:384].rearrange("p (h e) -> p h e", h=4)
                    sq = sqt[:, half * 384:(half + 1) * 384].rearrange("p (h e) -> p h e", h=4)
                    act(sq, qv, AF.Square, reads=[PSN[6 + half]], writes=["sqt"])
                    red(st8[:, 8 + half * 4:12 + half * 4], sq[:, :, 0:64], reads=["sqt"], writes=["st8q"])
                    red(st8[:, 16 + half * 4:20 + half * 4], sq[:, :, 64:96], reads=["sqt"], writes=["st8q"])
                act(st8[:, 8:16], st8[:, 8:16], AF.Sqrt, reads=["st8q", "epsc"], writes=["st8q"], scale=1.0 / 64, bias=epsc[:])
                act(st8[:, 16:24], st8[:, 16:24], AF.Sqrt, reads=["st8q", "epsc"], writes=["st8q"], scale=1.0 / 32, bias=epsc[:])
                S.op("dve", lambda e: e.reciprocal(out=st8[:, 8:24], in_=st8[:, 8:24]), reads=["st8q"], writes=["st8q"])
                for half in range(2):
                    qv = PS[6 + half][:, 0:384].rearrange("p (h e) -> p h e", h=4)
                    dst = fA[:, 0:384].rearrange("p (h e) -> p h e", h=4) if half == 0 else fB[:, 0:384].rearrange("p (h e) -> p h e", h=4)
                    nm = "fA" if half == 0 else "fB"
                    tt("dve", dst[:, :, 0:64], qv[:, :, 0:64], bcast(st8[:, 8 + half * 4:12 + half * 4].unsqueeze(2), [128, 4, 64]),
                       ALU.mult, reads=[PSN[6 + half], "st8q"], writes=[nm])
                    tt("dve", dst[:, :, 64:96], qv[:, :, 64:96], bcast(st8[:, 16 + half * 4:20 + half * 4].unsqueeze(2), [128, 4, 32]),
                       ALU.mult, reads=[PSN[6 + half], "st8q"], writes=[nm])
                    if not is_s:
                        tt("dve", Qc_tm[:, tl, half * 4:(half + 1) * 4, :], dst, bcast(gq96[:].unsqueeze(1), [128, 4, 96]), ALU.mult,
                           reads=[nm, "gq96"], writes=["Qc_tm"])
                    else:
                        tt("dve", dst, dst, bcast(gq96[:].unsqueeze(1), [128, 4, 96]), ALU.mult, reads=[nm, "gq96"], writes=[nm])
                        S.op("act", lambda e: e.copy(Qc_tm[:, tl, half * 4:(half + 1) * 4, 0:64], dst[:, :, 0:64]), reads=[nm], writes=["Qc_tm"])
                        S.op("act", lambda e: e.copy(sqt[:, 0:128].rearrange("p (h e) -> p h e", h=4), dst[:, :, 64:96]), reads=[nm], writes=["sqtr"])
                        rope(sqt[:, 128:256], sqt[:, 0:128], tl, rope32, 4, 8, reads=["sqtr", "rope32"], writes=["sqtr2"])
                        S.op("act", lambda e: e.copy(Qc_tm[:, tl, half * 4:(half + 1) * 4, 64:96],
                                                     sqt[:, 128:256].rearrange("p (h e) -> p h e", h=4)), reads=["sqtr2"], writes=["Qc_tm"])
                pq = PS[5][:, :].bitcast(BF16).rearrange("p (h i) -> p h i", h=8)
                for h in range(8):
                    tr(5, pq[0:96, h, :], Qc_tm[:, tl, h, :], reads=["Qc_tm"], last=(h == 7))
                S.op("act", lambda e: e.copy(QcT[0:96, :, tl * 128:(tl + 1) * 128], pq[0:96, :, :]), reads=[PSN[5]], writes=["QcT"])

        def mla_up(src_ckv, src_kr, KT_dst, VX_dst, reads, kt_name, vx_name, par=0):
            cb_, cT_, Kc_ = (ckvb, ckvT, Kc_tm) if par == 0 else (ckvb2, ckvT2, Kc_tm2)
            n_cb, n_cT, n_Kc = ("ckvb", "ckvT", "Kc_tm") if par == 0 else ("ckvb1", "ckvT1", "Kc_tm1")
            tb = 5 if par == 0 else 4
            kb = (6, 7) if par == 0 else (0, 1)
            so = 24 if par == 0 else 52
            n_st = "st8k%d" % par
            S.op("dve", lambda e: e.tensor_copy(cb_, src_ckv), reads=reads, writes=[n_cb])
            pt = PS[tb][:, 0:64].bitcast(BF16)
            tr(tb, pt, cb_, reads=[n_cb])
            S.op("act", lambda e: e.copy(cT_, pt), reads=[PSN[tb]], writes=[n_cT])
            for half in range(2):
                mm(kb[half], PS[kb[half]][:, :], cT_, w_kvb[:, half * 512:(half + 1) * 512], True, True, reads=[n_cT, "w_kvb"])
            for half in range(2):
                kv = PS[kb[half]][:, :].rearrange("p (h e) -> p h e", h=4)
                sq = sqt[:, half * 256:(half + 1) * 256].rearrange("p (h e) -> p h e", h=4)
                act(sq, kv[:, :, 0:64], AF.Square, reads=[PSN[kb[half]]], writes=["sqt"])
                red(st8[:, so + half * 4:so + 4 + half * 4], sq, reads=["sqt"], writes=[n_st])
                S.op("act", lambda e: e.copy(VX_dst[:, half * 4:(half + 1) * 4, 0:64], kv[:, :, 64:128]), reads=[PSN[kb[half]]], writes=[vx_name])
            act(st8[:, so:so + 8], st8[:, so:so + 8], AF.Sqrt, reads=[n_st, "epsc"], writes=[n_st], scale=1.0 / 64, bias=epsc[:])
            S.op("dve", lambda e: e.reciprocal(out=st8[:, so:so + 8], in_=st8[:, so:so + 8]), reads=[n_st], writes=[n_st])
            for half in range(2):
                kv = PS[kb[half]][:, :].rearrange("p (h e) -> p h e", h=4)
                tmp = fA[:, 0:256].rearrange("p (h e) -> p h e", h=4) if half == 0 else fB[:, 0:256].rearrange("p (h e) -> p h e", h=4)
                nm = "fA" if half == 0 else "fB"
                tt("dve", tmp, kv[:, :, 0:64], bcast(st8[:, so + half * 4:so + 4 + half * 4].unsqueeze(2), [128, 4, 64]), ALU.mult,
                   reads=[PSN[kb[half]], n_st], writes=[nm])
                tt("dve", Kc_[:, half * 4:(half + 1) * 4, 0:64], tmp, bcast(gsm[:, G_KN:G_KN + 64].unsqueeze(1), [128, 4, 64]), ALU.mult,
                   reads=[nm, "gsm"], writes=[n_Kc])
            S.op("act", lambda e: e.copy(Kc_[:, :, 64:96], bcast(src_kr.unsqueeze(1), [128, 8, 32])), reads=reads, writes=[n_Kc])
            S.op("dve", lambda e: e.memset(VX_dst[:, :, 64:65], 1.0), writes=[vx_name])
            pk = PS[tb][:, :].bitcast(BF16).rearrange("p (h i) -> p h i", h=8)
            for h in range(8):
                tr(tb, pk[0:96, h, :], Kc_[:, h, :], reads=[n_Kc], last=(h == 7))
            S.op("act", lambda e: e.copy(KT_dst, pk[0:96, :, :]), reads=[PSN[tb]], writes=[kt_name])

        def seg_retention(l, seg, emit_out=True):
            is_s = seg == 4
            for w, (src, nm) in enumerate(((q_tm, "q_tm"), (k_tm, "k_tm"))):
                for tl in range(2):
                    pt = PS[5][:, 0:256].bitcast(BF16).rearrange("p (c i) -> p c i", c=4)
                    for c in range(4):
                        tr(5, pt[:, c, :], src[:, tl, c * 128:(c + 1) * 128], reads=[nm], last=(c == 3))
                    S.op("act", lambda e: e.copy(qkT[:, w, :, tl * 128:(tl + 1) * 128], pt), reads=[PSN[5]], writes=["qkT"])
            for d in range(2):
                for tl in range(2):
                    tt("dve", kdec[:, d, tl, :].rearrange("p (h e) -> p h e", h=8), k_tm[:, tl, :].rearrange("p (h e) -> p h e", h=8),
                       bcast(KD[:, d, tl, :].unsqueeze(2), [128, 8, 64]), ALU.mult, reads=["k_tm", "KD"], writes=["kdec"])
            for d, bank in ((0, 4), (1, 6)):
                for pr in range(4):
                    for tl in range(2):
                        mm(bank, PS[bank][:, pr * 128:(pr + 1) * 128], kdec[:, d, tl, pr * 128:(pr + 1) * 128],
                           v_tm[:, tl, pr * 128:(pr + 1) * 128], tl == 0, tl == 1, reads=["kdec", "v_tm"])
                S.op("act", lambda e: e.copy(Ust[:, d, :], PS[bank][:, :]), reads=[PSN[bank]], writes=["Ust"])
            Uv = Ust[:].rearrange("p d (r e) -> p d r e", r=4)
            if not emit_out:
                return
            if is_s:
                dst = ub_d[l].rearrange("(d r x k) v -> k d r x v", d=2, r=4, x=2)
                for d in range(2):
                    S.dma("sp", dst[:, d, :, 0, :], Uv[0:64, d, :, 0:64], reads=["Ust"], writes=["ub_d%d" % l])
                    S.dma("sp", dst[:, d, :, 1, :], Uv[64:128, d, :, 64:128], reads=["Ust"], writes=["ub_d%d" % l])
            else:
                for d, od in ((0, osf_d), (1, osb_d)):
                    dst = od[l, seg].rearrange("(r x) k v -> k r x v", x=2)
                    S.dma("sp", dst[:, :, 0, :], Uv[0:64, d, :, 0:64], reads=["Ust"], writes=["ost"])
                    S.dma("sp", dst[:, :, 1, :], Uv[64:128, d, :, 64:128], reads=["Ust"], writes=["ost"])

        def seg_retention_out(l, seg):
            is_s = seg == 4
            S.alias(["mix_tm"], ["q_tm", "k_tm"])
            for h in range(8):
                c, po = h // 2, (h % 2) * 64
                bank = h % 2
                pst = PS[bank][:, :].rearrange("p (j i) -> p j i", j=2)
                for jt in range(2):
                    mm(bank, pst[:, jt, :], qkT[po:po + 64, 1, c, jt * 128:(jt + 1) * 128], qkT[po:po + 64, 0, c, :], True, True,
                       reads=["qkT"])
                at = AT[h % 2]
                tt("dve", at, pst, DT[:, h], ALU.mult, reads=[PSN[bank], "DT"], writes=["AT%d" % (h % 2)])
                for it in range(2):
                    ob = 2 + it
                    o = PS[ob][:, h * 64:(h + 1) * 64]
                    for jt in range(2):
                        mm(ob, o, at[:, jt, it * 128:(it + 1) * 128], v_tm[:, jt, h * 64:(h + 1) * 64], jt == 0, jt == 1,
                           reads=["AT%d" % (h % 2), "v_tm"])
                    if is_s:
                        for d in range(2):
                            cb = 4 + it * 2 + d
                            mm(cb, PS[cb][:, h * 64:(h + 1) * 64], qkT[po:po + 64, 0, c, it * 128:(it + 1) * 128], Sin[po:po + 64, d, h, :],
                               True, True, reads=["qkT", "Sin"])
            for it in range(2):
                ob = 2 + it
                ov = PS[ob][:, :].rearrange("p (h e) -> p h e", h=8)
                ovn = [PSN[ob]]
                if is_s:
                    osb = sqt[:, 0:512].rearrange("p (h e) -> p h e", h=8)
                    for d in range(2):
                        cb = 4 + it * 2 + d
                        tmp = (fA if d == 0 else fB)[:, 0:512].rearrange("p (h e) -> p h e", h=8)
                        tt("dve", tmp, PS[cb][:, :].rearrange("p (h e) -> p h e", h=8), bcast(KQ[:, d, it, :].unsqueeze(2), [128, 8, 64]),
                           ALU.mult, reads=[PSN[cb], "KQ"], writes=["fA" if d == 0 else "fB"])
                    tt("dve", osb, ov, fA[:, 0:512].rearrange("p (h e) -> p h e", h=8), ALU.add, reads=[PSN[ob], "fA"], writes=["sqt"])
                    tt("dve", osb, osb, fB[:, 0:512].rearrange("p (h e) -> p h e", h=8), ALU.add, reads=["sqt", "fB"], writes=["sqt"])
                    ov = osb
                    ovn = ["sqt"]
                red(st8[:, 32:40], ov, reads=ovn, writes=["st8g"])
                ts("dve", st8[:, 32:40], st8[:, 32:40], 1.0 / 64, None, ALU.mult, None, reads=["st8g"], writes=["st8g"])
                dv_ = fA[:, 0:512].rearrange("p (h e) -> p h e", h=8)
                tt("dve", dv_, ov, bcast(st8[:, 32:40].unsqueeze(2), [128, 8, 64]), ALU.subtract, reads=ovn + ["st8g"], writes=["fA"])
                sq = fB[:, 0:512].rearrange("p (h e) -> p h e", h=8)
                act(sq, dv_, AF.Square, reads=["fA"], writes=["fB"])
                red(st8[:, 40:48], sq, reads=["fB"], writes=["st8h"])
                act(st8[:, 40:48], st8[:, 40:48], AF.Sqrt, reads=["st8h", "epsc"], writes=["st8h"], scale=1.0 / 64, bias=epsc[:])
                S.op("dve", lambda e: e.reciprocal(out=st8[:, 40:48], in_=st8[:, 40:48]), reads=["st8h"], writes=["st8h"])
                tt("dve", dv_, dv_, bcast(st8[:, 40:48].unsqueeze(2), [128, 8, 64]), ALU.mult, reads=["fA", "st8h"], writes=["fA"])
                tt("dve", fA[:, 0:512], fA[:, 0:512], gsm[:, G_GN:G_GN + 512], ALU.mult, reads=["fA", "gsm"], writes=["fA"])
                tt("dve", fA[:, 0:512], fA[:, 0:512], gsm[:, B_GN:B_GN + 512], ALU.add, reads=["fA", "gsm"], writes=["fA"])
                tt("dve", mix_tm[:, it, 0:512], fA[:, 0:512], g_tm[:, it, :], ALU.mult, reads=["fA", "g_tm"], writes=["mix_tm"])

        def seg_attention(l, seg, nkt, KT, VXs, kt_name, vx_name):
            for hg in range(2):
                for hh in range(4):
                    h = hg * 4 + hh
                    for kt in range(nkt):
                        sbk = (h * nkt + kt) % 2
                        ps_ = PS[sbk][:, 0:256]
                        mm(sbk, ps_, KT[0:96, h, kt * 128:(kt + 1) * 128], QcT[0:96, h, :], True, True, reads=list(kt_name) + ["QcT"])
                        p = PT[(h * nkt + kt) % 2]
                        pn = "PT%d" % ((h * nkt + kt) % 2)
                        act(p, ps_, AF.Exp, reads=[PSN[sbk]], writes=[pn])
                        for it in range(2):
                            ob = 2 + it if hg == 0 else 4 + 2 * it
                            mm(ob, PS[ob][:, hh * 65:(hh + 1) * 65], p[:, it * 128:(it + 1) * 128], VXs(kt)[:, h, :], kt == 0, kt == nkt - 1,
                               reads=[pn] + list(vx_name))
            for hg in range(2):
                for it in range(2):
                    ob = 2 + it if hg == 0 else 4 + 2 * it
                    ov = PS[ob][:, 0:260].rearrange("p (h e) -> p h e", h=4)
                    S.op("dve", lambda e: e.reciprocal(out=st8[:, 48:52], in_=ov[:, :, 64]), reads=[PSN[ob]], writes=["st8a"])
                    tt("dve", mix_tm[:, it, 512 + hg * 256:512 + (hg + 1) * 256].rearrange("p (h e) -> p h e", h=4), ov[:, :, 0:64],
                       bcast(st8[:, 48:52].unsqueeze(2), [128, 4, 64]), ALU.mult, reads=[PSN[ob], "st8a"], writes=["mix_tm"])

        def seg_mix_out(seg):
            gi = 2 if seg == 4 else seg // 2
            for tl in range(2):
                t0 = (seg * 2 + tl) * 128
                pt = PS[7][:, :].bitcast(BF16).rearrange("p (c i) -> p c i", c=8)
                for c in range(8):
                    tr(7, pt[:, c, :], mix_tm[:, tl, c * 128:(c + 1) * 128], reads=["mix_tm"], last=(c == 7))
                S.op("act", lambda e: e.copy(hm[:, :, t0:t0 + 128], pt), reads=[PSN[7]], writes=["hm:s%d" % seg])

        def sample_exchange(l):
            S.collective([kvb_c[l]], [kvg_c[l]], [[0, 1, 2, 3], [4, 5, 6, 7]], reads=["kvb_d%d" % l], writes=["kvg_d%d" % l])
            S.collective([ub_c[l]], [ug_c[l]], [[0, 1, 2, 3], [4, 5, 6, 7]], reads=["ub_d%d" % l], writes=["ug_d%d" % l])

        def sample_states(l):
            ugv = ug_d[l].rearrange("(s d h k) v -> k s d h v", s=4, d=2, h=8)
            for d in range(2):
                for s_ in range(5):
                    if s_ < 4:
                        S.dma("sp", Ug[0:64], ugv[:, s_, d], reads=["ug_d%d" % l], writes=["Ug"])
                    else:
                        S.dma("sp", Ug[0:64], s0_d[l, d].rearrange("h k v -> k h v"), writes=["Ug"])
                    cf = bcast(xcf[0:64, d, s_, :].unsqueeze(2), [64, 8, 64])
                    if s_ == 0:
                        tt("dve", Sacc[0:64, d], Ug[0:64], cf, ALU.mult, reads=["Ug", "xcf"], writes=["Sacc"])
                    else:
                        tt("dve", fA[0:64, 0:512].rearrange("p (h e) -> p h e", h=8), Ug[0:64], cf, ALU.mult,
                           reads=["Ug", "xcf"], writes=["fA"])
                        tt("dve", Sacc[0:64, d], Sacc[0:64, d], fA[0:64, 0:512].rearrange("p (h e) -> p h e", h=8), ALU.add,
                           reads=["Sacc", "fA"], writes=["Sacc"])
            S.op("act", lambda e: e.copy(Sin[0:64], Sacc[0:64]), reads=["Sacc"], writes=["Sin"])
            S.dma("sp", Sin[64:128], Sin[0:64], reads=["Sin"], writes=["Sin"])

        def sample_keys(l):
            S.alias(["sKT0", "sKT1", "sVX0", "sVX1"], ["w_in"])
            kg = kvg_d[l].rearrange("(t p) n -> t p n", p=128)
            for kt in range(10):
                kb_, kn_ = (kvin, "kvin") if kt % 2 == 0 else (kvin2, "kvin1")
                if kt < 8:
                    S.dma("sp", kb_[:, :], kg[kt], reads=["kvg_d%d" % l], writes=[kn_])
                else:
                    S.dma("sp", kb_[:, 0:128], cckv_d[l, (kt - 8) * 128:(kt - 7) * 128, :], writes=[kn_])
                    S.dma("sp", kb_[:, 128:160], ckr_d[l, (kt - 8) * 128:(kt - 7) * 128, :], writes=[kn_])
                mla_up(kb_[:, 0:128], kb_[:, 128:160], sKT[0:96, :, kt * 128:(kt + 1) * 128], sVX[:, kt], [kn_], "sKT%d" % (kt % 2), "sVX%d" % (kt % 2),
                       par=kt % 2)

        def phase_wo(l):
            for gi, (t0, t1, grp) in enumerate(GROUPS):
                n = t1 - t0
                for j in range(8):
                    bank = j % 4
                    for c in range(8):
                        mm(bank, PS[bank][:, 0:n], w_o[:, c, j * 128:(j + 1) * 128], hm[:, c, t0:t1], c == 0, c == 7,
                           reads=["w_o"] + hmg(gi))
                    stt("dve", xT[:, j, t0:t1], PS[bank][:, 0:n], modcol(2, j, l, grp), xT[:, j, t0:t1], ALU.mult, ALU.add,
                        reads=[PSN[bank], "MS0", "MS1", "xT:%d" % gi], writes=["xT:%d" % gi])

        def phase_ffn(l):
            S.alias(["actT"], ["DT", "wm"] + SEGN + NRMN)
            S.alias(["wfi0", "wfi1", "wfi2", "wfi3"], ["w_o"])
            S.alias(["w_fo"], ["w_in", "w_qb", "w_kvb", "sKT0", "sKT1", "sVX0", "sVX1"])
            nblk = 22
            for j in range(nblk):
                buf = j % 4
                wv = w_fi[buf]
                S.dma("pool", wv[:, :, 0:128], w_fi_d[l, :, j * 128:(j + 1) * 128].rearrange("(c p) n -> p c n", p=128), writes=["wfi%d" % buf])
                S.dma("pool", wv[:, :, 128:256], w_fi_d[l, :, DFF + j * 128:DFF + (j + 1) * 128].rearrange("(c p) n -> p c n", p=128),
                      writes=["wfi%d" % buf])
                if j == 2:
                    S.dma("pool", w_fo, w_fo_d[l].rearrange("(c p) n -> p c n", p=128), writes=["w_fo"])
                for gi, (t0, t1, grp) in enumerate(GROUPS):
                    n = t1 - t0
                    pb = ((j * 3 + gi) % 2) * 2
                    for half in range(2):
                        for c in range(8):
                            mm(pb + half, PS[pb + half][:, 0:n], wv[:, c, half * 128:(half + 1) * 128], hm[:, c, t0:t1], c == 0, c == 7,
                               reads=["wfi%d" % buf] + hmg(gi))
                    sgb = sg[(j * 3 + gi) % 2]
                    sgn = "sg%d" % ((j * 3 + gi) % 2)
                    act(sgb[:, 0:n], PS[pb][:, 0:n], AF.Silu, reads=[PSN[pb]], writes=[sgn])
                    tt("dve", actT[:, j, t0:t1], sgb[:, 0:n], PS[pb + 1][:, 0:n], ALU.mult, reads=[sgn, PSN[pb + 1]], writes=["actT"])
            for gi, (t0, t1, grp) in enumerate(GROUPS):
                n = t1 - t0
                for j in range(8):
                    bank = 4 + (j % 4)
                    for c in range(22):
                        mm(bank, PS[bank][:, 0:n], w_fo[:, c, j * 128:(j + 1) * 128], actT[:, c, t0:t1], c == 0, c == 21,
                           reads=["w_fo", "actT"])
                    stt("dve", xT[:, j, t0:t1], PS[bank][:, 0:n], modcol(5, j, l, grp), xT[:, j, t0:t1], ALU.mult, ALU.add,
                        reads=[PSN[bank], "MS0", "MS1", "xT:%d" % gi], writes=["xT:%d" % gi])

        def main_program():
            modulation_setup()
            S.alias(["DT"] + NRMN, ["wm"])
            ck("s0")
            for l in range(2):
                if l > 0:
                    S.alias(["w_in", "w_qb", "w_kvb"], ["w_fo"])
                    S.alias(["w_o"], ["wfi0", "wfi1", "wfi2", "wfi3"])
                    S.alias(["DT"] + NRMN, ["actT"])
                S.dma("pool", w_in, w_in_d[l].rearrange("(c p) n -> p c n", p=128), writes=["w_in"])
                S.dma("pool", w_qb, w_qb_d[l].rearrange("(c p) n -> p c n", p=128), writes=["w_qb"])
                S.dma("pool", w_kvb, w_kvb_d[l], writes=["w_kvb"])
                S.dma("pool", w_o, w_o_d[l].rearrange("(c p) n -> p c n", p=128), writes=["w_o"])
                ck("s1")
                layer_tables(l)
                ck("s2")
                norm_phase(l, 0)
                ck("a0")
                if stop == 1:
                    dbg_dump(S, "hm", hm[:], [128, 8, TOK], BF16, ["hm:s%d" % i for i in range(5)])
                    break
                S.alias(SEGN, NRMN)
                seg_project(l, 4)
                ck("a1")
                seg_retention(l, 4)
                S.dma("sp", sp_qk[l], qkT[:].rearrange("p w c i -> p (w c i)"), reads=["qkT"], writes=["sp_qk%d" % l])
                S.dma("sp", sp_v[l], v_tm[:].rearrange("p t n -> p (t n)"), reads=["v_tm"], writes=["sp_v%d" % l])
                S.dma("sp", sp_g[l], g_tm[:].rearrange("p t n -> p (t n)"), reads=["g_tm"], writes=["sp_g%d" % l])
                S.dma("sp", sp_qa[l], qa_n[:].rearrange("p t n -> p (t n)"), reads=["qa_n"], writes=["sp_qa%d" % l])
                ck("a2")
                sample_exchange(l)
                ck("a3")
                for seg in range(4):
                    seg_project(l, seg)
                    seg_retention(l, seg)
                    ck("b1")
                    seg_retention_out(l, seg)
                    ck("b2")
                    seg_q(l, seg)
                    ck("b3")
                    for tl in range(2):
                        mla_up(ckv[:, tl, :], kr[:, tl, :], KcT[0:96, :, tl * 128:(tl + 1) * 128], VX[:, tl], ["ckv", "kr"], "KcT", "VX", par=tl)
                    ck("b4")
                    seg_attention(l, seg, 2, KcT, lambda kt: VX[:, kt], ["KcT"], ["VX"])
                    ck("b5")
                    seg_mix_out(seg)
                    ck("b6")
                S.alias(["q_tm", "k_tm"], ["mix_tm"])
                S.alias(["KcT", "VX"], ["Sacc", "Sin"])
                S.dma("sp", qkT[:].rearrange("p w c i -> p (w c i)"), sp_qk[l], reads=["sp_qk%d" % l], writes=["qkT"])
                S.dma("sp", v_tm[:].rearrange("p t n -> p (t n)"), sp_v[l], reads=["sp_v%d" % l], writes=["v_tm"])
                S.dma("sp", g_tm[:].rearrange("p t n -> p (t n)"), sp_g[l], reads=["sp_g%d" % l], writes=["g_tm"])
                S.dma("sp", qa_n[:].rearrange("p t n -> p (t n)"), sp_qa[l], reads=["sp_qa%d" % l], writes=["qa_n"])
                S.alias(["Sacc", "Sin"], ["KcT", "VX"])
                ck("c0")
                sample_states(l)
                ck("c1")
                seg_retention_out(l, 4)
                seg_q(l, 4)
                ck("c2")
                sample_keys(l)
                ck("c3")
                seg_attention(l, 4, 10, sKT, lambda kt: sVX[:, kt], ["sKT0", "sKT1"], ["sVX0", "sVX1"])
                seg_mix_out(4)
                if stop == 2:
                    dbg_dump(S, "mixT", hm[:], [128, 8, TOK], BF16, ["hm:s%d" % i for i in range(5)])
                    break
                phase_wo(l)
                S.alias(NRMN, SEGN)
                norm_phase(l, 1)
                phase_ffn(l)
                if stop == 3:
                    break


        try:
            main_program()
        except StopBuild:
            pass

        for gi, (t0, t1, grp) in enumerate(GROUPS):
            S.dma("sp", yT_d.rearrange("(c p) t -> p c t", p=128)[:, :, t0:t1], xT[:, :, t0:t1], reads=["xT:%d" % gi], writes=["yT_d%d" % gi])
        S.wait_all("sp")
        print("program built: waits=%d counts=%s dmas=%s r3=%d" % (S.n_wait, S.cnt, S.dcnt, r3o[0]))
    return nc, dbg_out


def _rope_tab(pos_row, pos_col, nf):
    inv = (10000.0 ** (-np.arange(nf, dtype=np.float64) / nf))
    ar = pos_row[:, None].astype(np.float64) * inv
    ac = pos_col[:, None].astype(np.float64) * inv
    C = np.stack([np.cos(ar), np.cos(ac)], 1)
    Sn = np.stack([np.sin(ar), np.sin(ac)], 1)
    return np.stack([C, Sn], 1).astype(np.float32)


def make_inputs(inp):
    f = lambda a: np.ascontiguousarray(np.asarray(a, dtype=np.float32))
    x_prompt, x_sample = f(inp["x_prompt"]), f(inp["x_sample"])
    shared = {k: f(inp[k]) for k in ("w_in", "w_q_b", "w_kv_b", "w_o", "w_ffn_in", "w_ffn_out")}
    w_mod, b_mod = f(inp["w_mod"]), f(inp["b_mod"])
    conds = np.stack([f(inp["c_ctx"]), f(inp["c"])[0], f(inp["c"])[1]], 0)
    condT = f(conds.reshape(3, 8, 128).transpose(2, 1, 0))
    gnT = f(np.stack([f(inp["g_norm_mix"]), f(inp["g_norm_ffn"])], 1).reshape(2, 2, 8, 128).transpose(3, 0, 1, 2))
    gqaT = f(f(inp["g_q_a"]).reshape(2, 2, 128).transpose(2, 0, 1))
    gsm = f(np.concatenate([f(inp[k]) for k in ("g_kv_a", "g_qn", "g_qr", "g_kn", "g_kr", "g_ret_gn", "b_ret_gn")], 1))
    assert gsm.shape == (2, GSM_N)
    retp = f(np.concatenate([f(inp["ret_p_fwd"]), f(inp["ret_p_bwd"])], 1))
    j = np.arange(128)[:, None, None] + 128 * np.arange(2)[None, :, None]
    i = np.arange(256)[None, None, :]
    diff = (i - j).astype(np.float32)
    tabs = f(np.stack([np.maximum(diff, 0), np.maximum(-diff, 0), (diff >= 0).astype(np.float32),
                       (diff <= 0).astype(np.float32)], 1))
    jj = np.arange(128)[:, None] + 128 * np.arange(2)[None, :]
    idxt = f(np.stack([255.0 - jj, jj.astype(np.float64)], 1))
    idxq = f(np.stack([jj + 1.0, 256.0 - jj], 1))
    maps = []
    for core in range(NCORES):
        r, seq = core % 4, core // 4
        xs = np.concatenate([x_prompt[4 * core + b] for b in range(4)] + [x_sample[seq, 256 * r:256 * (r + 1)]], 0)
        n = 256 * r + np.arange(256)
        sel = np.zeros((128, 2), np.float32)
        sel[:, seq] = 1.0
        xco = np.zeros((2, 2, 5), np.float32)
        for s_ in range(4):
            if s_ < r:
                xco[0, 0, s_] = 256.0 * (r - 1 - s_); xco[1, 0, s_] = 1.0
            if s_ > r:
                xco[0, 1, s_] = 256.0 * (s_ - r - 1); xco[1, 1, s_] = 1.0
        xco[0, 0, 4] = 256.0 * r; xco[1, 0, 4] = 1.0
        xco[0, 1, 4] = 256.0 * (3 - r); xco[1, 1, 4] = 1.0
        m = dict(shared)
        m.update({
            "xT": f(xs.T),
            "w_mod_sh": f(w_mod[:, :, 1536 * r:1536 * (r + 1)]),
            "b_mod_sh": f(b_mod[:, 1536 * r:1536 * (r + 1)].reshape(2, 12, 128).transpose(2, 1, 0)),
            "condT": condT, "sel": sel, "gnT": gnT, "gqaT": gqaT, "gsm": gsm, "retp": retp,
            "c_ckv": f(inp["cache_ckv"])[seq], "c_kr": f(inp["cache_krope"])[seq],
            "s0": f(np.stack([f(inp["state_ret_fwd"])[seq], f(inp["state_ret_bwd"])[seq]], 1)),
            "tabs": tabs, "idxt": idxt, "idxq": idxq,
            "rope64": f(_rope_tab(n // 64, n % 64, 16).reshape(2, 128, 2, 2, 16).transpose(1, 0, 2, 3, 4)),
            "rope32": f(_rope_tab(n // 64, n % 64, 8).reshape(2, 128, 2, 2, 8).transpose(1, 0, 2, 3, 4)),
            "xcoef": f(np.broadcast_to(xco[None], (128, 2, 2, 5))),
        })
        maps.append(m)
    return maps


_CACHE = {}


def run(inp, stop=99, dbg=()):
    key = (stop, tuple(dbg))
    if key not in _CACHE:
        _CACHE[key] = build_program(stop, dbg)
    nc, dbg_out = _CACHE[key]
    maps = make_inputs(inp)
    res = run_bass_kernel_spmd(nc, maps, core_ids=list(range(NCORES)))
    return res.results


def kernel(**inp):
    R = run(inp)
    y_prompt = np.zeros((32, 256, D), np.float32)
    y_sample = np.zeros((2, 1024, D), np.float32)
    new_ckv = np.zeros((32, 2, 256, 128), np.float32)
    new_kr = np.zeros((32, 2, 256, 32), np.float32)
    new_sf = np.zeros((32, 2, 8, 64, 64), np.float32)
    new_sb = np.zeros((32, 2, 8, 64, 64), np.float32)
    for core in range(NCORES):
        r, seq = core % 4, core // 4
        o = R[core]
        y = np.asarray(o["yT"]).T
        y_prompt[4 * core:4 * core + 4] = y[0:1024].reshape(4, 256, D)
        y_sample[seq, 256 * r:256 * (r + 1)] = y[1024:1280]
        new_ckv[4 * core:4 * core + 4] = np.asarray(o["o_ckv"]).reshape(2, 4, 256, 128).transpose(1, 0, 2, 3)
        new_kr[4 * core:4 * core + 4] = np.asarray(o["o_kr"]).reshape(2, 4, 256, 32).transpose(1, 0, 2, 3)
        new_sf[4 * core:4 * core + 4] = np.asarray(o["o_sf"]).transpose(1, 0, 2, 3, 4)
        new_sb[4 * core:4 * core + 4] = np.asarray(o["o_sb"]).transpose(1, 0, 2, 3, 4)
    return (y_prompt, y_sample, new_ckv, new_kr, new_sf, new_sb)
```

```python
import os
import numpy as np
from contextlib import ExitStack
import concourse.bass as bass
import concourse.mybir as mybir
from concourse.bass_utils import run_bass_kernel_spmd

F32 = mybir.dt.float32
BF16 = mybir.dt.bfloat16
ALU = mybir.AluOpType
AF = mybir.ActivationFunctionType
AX = mybir.AxisListType

N_DMA_SEMS = 12
NCORES = 8
D = 1024
TOK = 1280
NTILE = 10
IN_COLS = 2464
DFF = 2816
EPS = 1e-6
LN2 = float(np.log(2.0))
ATT_SCALE = float(96 ** -0.5)

G_KVA, G_QN, G_QR, G_KN, G_KR, G_GN, B_GN = 0, 128, 192, 224, 288, 320, 832
GSM_N = 1344


class StopBuild(Exception):
    pass


class Sync:
    def __init__(self, nc, stack):
        self.nc = nc
        self.stack = stack
        self.engs = {"pe": nc.tensor, "act": nc.scalar, "dve": nc.vector,
                     "pool": nc.gpsimd, "sp": nc.sync}
        self.sem = {k: stack.enter_context(nc.semaphore("sem_" + k)) for k in self.engs}
        self.cnt = {k: 0 for k in self.engs}
        self.seen = {k: {} for k in self.engs}
        self.dsem = {q: [stack.enter_context(nc.semaphore("dsem_%s%d" % (q, i)))
                         for i in range(N_DMA_SEMS)] for q in ("sp", "pool")}
        self.dcnt = {q: 0 for q in self.dsem}
        self.ccsem = []
        self.res = {}
        self.n_wait = 0

    def _semof(self, kind, a):
        if kind == "eng":
            return self.sem[a]
        if kind == "dma":
            return self.dsem[a[0]][a[1]]
        return self.ccsem[a]

    def _wait(self, e, tok):
        if tok is None:
            return
        kind, a, v = tok
        key = (kind, a)
        if self.seen[e].get(key, 0) >= v:
            return
        self.seen[e][key] = v
        self.engs[e].wait_ge(self._semof(kind, a), v)
        self.n_wait += 1

    def _deps(self, e, reads, writes, is_dma):
        deps = []
        for r in reads:
            st = self.res.get(r)
            if st and st["w"] is not None:
                deps.append(st["w"])
        for w in writes:
            st = self.res.get(w)
            if st:
                for t in [st["w"]] + list(st["r"].values()):
                    if t is None:
                        continue
                    if (not is_dma) and t[0] == "eng" and t[1] == e:
                        continue
                    deps.append(t)
        for t in deps:
            self._wait(e, t)

    def _update(self, tok, reads, writes):
        for r in reads:
            st = self.res.setdefault(r, {"w": None, "r": {}})
            st["r"][(tok[0], tok[1])] = tok
        for w in writes:
            self.res[w] = {"w": tok, "r": {}}

    @staticmethod
    def _split(reads, writes):
        writes = list(writes) + [r for r in reads if r.startswith("ps:")]
        reads = [r for r in reads if not r.startswith("ps:")]
        return reads, writes

    def op(self, e, fn, reads=(), writes=(), inc=True):
        reads, writes = self._split(reads, writes)
        self._deps(e, reads, writes, False)
        inst = fn(self.engs[e])
        if inc:
            self.cnt[e] += 1
            inst.then_inc(self.sem[e], 1)
            tok = ("eng", e, self.cnt[e])
        else:
            tok = ("eng", e, self.cnt[e] + 1)
        self._update(tok, reads, writes)
        return tok

    def dma(self, q, out, in_, reads=(), writes=(), **kw):
        self._deps(q, reads, writes, True)
        n = self.dcnt[q]
        self.dcnt[q] += 1
        slot = n % N_DMA_SEMS
        prev = 16 * (n // N_DMA_SEMS)
        if prev > 0:
            self._wait(q, ("dma", (q, slot), prev))
        inst = self.engs[q].dma_start(out=out, in_=in_, **kw)
        inst.then_inc(self.dsem[q][slot], 16)
        tok = ("dma", (q, slot), prev + 16)
        self._update(tok, reads, writes)
        return tok

    def collective(self, ins, outs, groups, reads, writes):
        self._deps("pool", reads, writes, True)
        if self.ccsem:
            self._wait("pool", ("cc", len(self.ccsem) - 1, 1))
        sem = self.stack.enter_context(self.nc.semaphore("ccsem%d" % len(self.ccsem)))
        self.ccsem.append(sem)
        inst = self.nc.gpsimd.collective_compute("AllGather", ALU.bypass, replica_groups=groups,
                                                 ins=ins, outs=outs)
        inst.then_inc(sem)
        tok = ("cc", len(self.ccsem) - 1, 1)
        self._update(tok, reads, writes)
        return tok

    def alias(self, new_names, old_names):
        merged = {}
        for o in old_names:
            st = self.res.get(o)
            if not st:
                continue
            toks = list(st["r"].values()) + ([st["w"]] if st["w"] is not None else [])
            for t in toks:
                k = (t[0], t[1])
                if k not in merged or merged[k][2] < t[2]:
                    merged[k] = t
        for n in new_names:
            st = self.res.setdefault(n, {"w": None, "r": {}})
            for k, t in merged.items():
                if k not in st["r"] or st["r"][k][2] < t[2]:
                    st["r"][k] = t

    def wait_all(self, e):
        for r, st in self.res.items():
            if st["w"] is not None:
                self._wait(e, st["w"])


def bcast(ap, shape):
    return ap.to_broadcast(shape)


def build_program(stop=99, dbg=()):
    nc = bass.Bass("TRN2", target_bir_lowering=False)
    dbg_out = {}

    def din(name, shape, dt=F32):
        return nc.dram_tensor(name, list(shape), dt, kind="ExternalInput").ap()

    def dout(name, shape, dt=F32):
        return nc.dram_tensor(name, list(shape), dt, kind="ExternalOutput").ap()

    xT_d = din("xT", [D, TOK])
    w_in_d = din("w_in", [2, D, IN_COLS])
    w_qb_d = din("w_q_b", [2, 256, 768])
    w_kvb_d = din("w_kv_b", [2, 128, 1024])
    w_o_d = din("w_o", [2, D, D])
    w_fi_d = din("w_ffn_in", [2, D, 2 * DFF])
    w_fo_d = din("w_ffn_out", [2, DFF, D])
    w_mod_d = din("w_mod_sh", [2, D, 1536])
    b_mod_d = din("b_mod_sh", [128, 12, 2])
    cond_d = din("condT", [128, 8, 3])
    sel_d = din("sel", [128, 2])
    gn_d = din("gnT", [128, 2, 2, 8])
    gqa_d = din("gqaT", [128, 2, 2])
    gsm_d = din("gsm", [2, GSM_N])
    retp_d = din("retp", [2, 16])
    cckv_d = din("c_ckv", [2, 256, 128])
    ckr_d = din("c_kr", [2, 256, 32])
    s0_d = din("s0", [2, 2, 8, 64, 64])
    tabs_d = din("tabs", [128, 4, 2, 256])
    idx_d = din("idxt", [128, 2, 2])
    idxq_d = din("idxq", [128, 2, 2])
    rope64_d = din("rope64", [128, 2, 2, 2, 16])
    rope32_d = din("rope32", [128, 2, 2, 2, 8])
    xco_d = din("xcoef", [128, 2, 2, 5])

    yT_d = dout("yT", [D, TOK])
    ockv_d = dout("o_ckv", [2, 1024, 128])
    okr_d = dout("o_kr", [2, 1024, 32])
    osf_d = dout("o_sf", [2, 4, 8, 64, 64])
    osb_d = dout("o_sb", [2, 4, 8, 64, 64])

    mb_c = nc.dram_tensor("mod_bounce", [32, 512], F32).ap()
    mg_c = nc.dram_tensor("mod_gath", [128, 512], F32).ap()
    mb_d = mb_c.rearrange("a b -> (a b)").rearrange("(p f) -> p f", f=128)
    mg_d = mg_c.rearrange("a b -> (a b)").rearrange("(p f) -> p f", f=128)
    kvb_c = [nc.dram_tensor("kv_bounce%d" % l, [80, 512], F32).ap() for l in range(2)]
    kvg_c = [nc.dram_tensor("kv_gath%d" % l, [320, 512], F32).ap() for l in range(2)]
    kvb_d = [a.rearrange("a b -> (a b)").rearrange("(t n) -> t n", n=160) for a in kvb_c]
    kvg_d = [a.rearrange("a b -> (a b)").rearrange("(t n) -> t n", n=160) for a in kvg_c]
    ub_c = [nc.dram_tensor("u_bounce%d" % l, [128, 512], F32).ap() for l in range(2)]
    ug_c = [nc.dram_tensor("u_gath%d" % l, [512, 512], F32).ap() for l in range(2)]
    ub_d = [a.rearrange("a b -> (a b)").rearrange("(t n) -> t n", n=64) for a in ub_c]
    ug_d = [a.rearrange("a b -> (a b)").rearrange("(t n) -> t n", n=64) for a in ug_c]

    sp_qk = [nc.dram_tensor("spill_qk%d" % l, [128, 2048], BF16).ap() for l in range(2)]
    sp_v = [nc.dram_tensor("spill_v%d" % l, [128, 1024], BF16).ap() for l in range(2)]
    sp_g = [nc.dram_tensor("spill_g%d" % l, [128, 1024], BF16).ap() for l in range(2)]
    sp_qa = [nc.dram_tensor("spill_qa%d" % l, [128, 512], BF16).ap() for l in range(2)]

    def dbg_dump(S, name, ap, shape, dt, reads):
        if name in dbg:
            o = dout("dbg_" + name, shape, dt)
            S.dma("sp", o, ap, reads=reads, writes=["dbgo_" + name])
            dbg_out[name] = (shape, dt)

    with ExitStack() as st:
        S = Sync(nc, st)
        ncd = nc.allow_non_contiguous_dma(reason="small strided parameter loads")
        st.enter_context(ncd)

        def sb(name, shape, dt):
            return st.enter_context(nc.sbuf_tensor(name, list(shape), dt))

        xT = sb("xT_sb", [128, 8, TOK], F32)
        hm = sb("hm_sb", [128, 8, TOK], BF16)
        R1 = sb("R1", [128, 22 * 1024], BF16)
        R2 = R1[:, 0:8192] if os.environ.get("SHRINK") else sb("R2", [128, 8 * 1024], BF16)
        R3 = sb("R3", [128, 35 * 1024], BF16)
        w_in = R1[:, 0:8 * IN_COLS].rearrange("p (c n) -> p c n", c=8)
        w_qb = R1[:, 8 * IN_COLS:8 * IN_COLS + 1536].rearrange("p (c n) -> p c n", c=2)
        w_kvb = R1[:, 8 * IN_COLS + 1536:8 * IN_COLS + 2560]
        w_fo = R1[:, :].rearrange("p (c n) -> p c n", c=22)
        sKT = R1[:, 0:8 * TOK].rearrange("p (h t) -> p h t", h=8)
        sVX = R1[:, 8 * TOK:8 * TOK + 10 * 520].rearrange("p (k h e) -> p k h e", k=10, h=8)
        w_o = R2[:, :].rearrange("p (c n) -> p c n", c=8)
        w_fi = [R2[:, i * 4096:(i + 1) * 4096].rearrange("p (c n) -> p c n", c=8) for i in range(2)]
        actT = R3[:, 0:22 * TOK].rearrange("p (c t) -> p c t", c=22)

        R3N = 35 * 1024
        r3o = [0]

        def r3(nelem_bf16, dt=BF16):
            o = r3o[0]
            r3o[0] += (nelem_bf16 + 15) // 16 * 16
            assert r3o[0] <= R3N, r3o[0]
            v = R3[:, o:o + nelem_bf16]
            return v.bitcast(F32) if dt == F32 else v

        DT = r3(2 * 8 * 512, F32).rearrange("p (h j i) -> p h j i", h=8, j=2)
        seg_base = r3o[0]
        qk_off = r3o[0]
        q_tm = r3(2 * 512).rearrange("p (t n) -> p t n", t=2)
        k_tm = r3(2 * 512).rearrange("p (t n) -> p t n", t=2)
        mix_tm = R3[:, qk_off:qk_off + 2048].rearrange("p (t n) -> p t n", t=2)
        v_tm = r3(2 * 512).rearrange("p (t n) -> p t n", t=2)
        g_tm = r3(2 * 512).rearrange("p (t n) -> p t n", t=2)
        kdec = r3(2 * 2 * 512).rearrange("p (d t n) -> p d t n", d=2, t=2)
        qkT = r3(2 * 4 * 256).rearrange("p (w c i) -> p w c i", w=2, c=4)
        qa_n = r3(2 * 256).rearrange("p (t n) -> p t n", t=2)
        qaT = r3(2 * 256).rearrange("p (c i) -> p c i", c=2)
        ckv = r3(2 * 2 * 128, F32).rearrange("p (t n) -> p t n", t=2)
        kr = r3(2 * 2 * 32, F32).rearrange("p (t n) -> p t n", t=2)
        ckvb = r3(128)
        ckvT = r3(128)
        Qc_tm = r3(2 * 768).rearrange("p (t h e) -> p t h e", t=2, h=8)
        QcT = r3(8 * 256).rearrange("p (h i) -> p h i", h=8)
        Kc_tm = r3(768).rearrange("p (h e) -> p h e", h=8)
        kc_off = r3o[0]
        KcT = r3(8 * 256).rearrange("p (h i) -> p h i", h=8)
        vx_off = r3o[0]
        VX = r3(2 * 520).rearrange("p (t h e) -> p t h e", t=2, h=8)
        Sacc = R3[:, kc_off:kc_off + 2048].bitcast(F32).rearrange("p (d h v) -> p d h v", d=2, h=8)
        Sin = R3[:, vx_off:vx_off + 1024].rearrange("p (d h v) -> p d h v", d=2, h=8)
        sqt = r3(2 * 768, F32)
        fA = r3(2 * 512, F32)
        fB = r3(2 * 512, F32)
        AT = [r3(512).rearrange("p (j i) -> p j i", j=2) for _ in range(2)]
        PT = [r3(256) for _ in range(2)]
        Ust = r3(2 * 2 * 512, F32).rearrange("p (d n) -> p d n", d=2)
        Ug = r3(2 * 512, F32).rearrange("p (h v) -> p h v", h=8)
        kvin = r3(2 * 160, F32)
        kvin2 = r3(2 * 160, F32)
        ckvb2 = r3(128)
        ckvT2 = r3(128)
        Kc_tm2 = r3(768).rearrange("p (h e) -> p h e", h=8)
        SEGN = ["q_tm", "k_tm", "v_tm", "g_tm", "kdec", "qkT", "qa_n", "qaT", "ckv", "kr", "ckvb", "ckvT", "Qc_tm", "QcT",
                "Kc_tm", "KcT", "VX", "sqt", "sqtr", "sqtr2", "fqk", "fkr", "fA", "fB", "AT0", "AT1", "PT0", "PT1", "mix_tm",
                "Ust", "Sin", "Ug", "Sacc", "kvin", "kvin1", "ckvb1", "ckvT1", "Kc_tm1"]
        seg_end = r3o[0]
        r3o[0] = seg_base
        nsq = r3(2 * 8 * 512, F32).rearrange("p (c t) -> p c t", c=8)
        s8 = r3(2 * 512, F32)
        rstd = r3(2 * 512, F32)
        ntmp = [r3(2 * 512, F32) for i in range(3)]
        tabs = r3(2 * 4 * 512, F32).rearrange("p (k j i) -> p k j i", k=4, j=2)
        dtmp = r3(2 * 2 * 512, F32).rearrange("p (k j i) -> p k j i", k=2, j=2)
        NRMN = ["nsq", "s8", "rstd", "ntmp0", "ntmp1", "ntmp2", "tabs", "dtmp0", "dtmp1"]
        print("R3 usage: seg_end=%d norm_end=%d of %d" % (seg_end, r3o[0], R3N))
        wm = R3[:, 8192:8192 + 2 * 8 * 1536].rearrange("p (l c n) -> p l c n", l=2, c=8)

        ones = sb("ones", [128, 128], F32)
        ident = sb("ident", [128, 128], BF16)
        identf = sb("identf", [128, 128], F32)
        epsc = sb("epsc", [128, 1], F32)
        onec = sb("onec", [128, 1], F32)
        condT = sb("condT_sb", [128, 8, 3], F32)
        sT = sb("sT", [128, 8, 3], BF16)
        sel = sb("sel_sb", [128, 2], F32)
        bmT = sb("bmT", [128, 12, 2], F32)
        mlp = sb("mlp", [128, 128], F32)
        ml = mlp[:, 0:72].rearrange("p (j l k) -> p j l k", j=12, l=2)
        MODT = sb("MODT", [128, 48, 2, 3], F32)
        MS = sb("MS", [128, 48, 2, 2], F32)
        gn = sb("gn_sb", [128, 2, 2, 8], F32)
        GG = sb("GG", [128, 2, 2, 2, 8], F32)
        gqa = sb("gqa_sb", [128, 2, 2], F32)
        gsm = sb("gsm_sb", [128, GSM_N], F32)
        gq96 = sb("gq96", [128, 96], F32)
        RP = sb("RP", [128, 16], F32)
        LG = sb("LG", [128, 2, 8], F32)
        KD = sb("KD", [128, 2, 2, 8], F32)
        KQ = sb("KQ", [128, 2, 2, 8], F32)
        idxq = sb("idxq_sb", [128, 2, 2], F32)
        idxt = sb("idxt_sb", [128, 2, 2], F32)
        rope64 = sb("rope64_sb", [128, 2, 2, 2, 16], F32)
        rope32 = sb("rope32_sb", [128, 2, 2, 2, 8], F32)
        xco = sb("xco_sb", [128, 2, 2, 5], F32)
        xcf = sb("xcf", [128, 2, 5, 8], F32)
        st8 = sb("st8", [128, 64], F32)
        sg = [sb("sg%d" % i, [128, 512], F32) for i in range(2)]

        print("SBUF bytes remaining after allocation:", nc.sbuf_bytes_remaining)
        PS = [st.enter_context(nc.psum_tensor("psb%d" % i, [128, 512], F32)) for i in range(8)]
        PSN = ["ps:%d" % i for i in range(8)]

        def mm(bank, out, lhsT, rhs, start, stop, reads):
            S.op("pe", lambda e: e.matmul(out, lhsT=lhsT, rhs=rhs, start=start, stop=stop),
                 reads=reads, writes=[PSN[bank]], inc=bool(stop))

        def tr(bank, out, in_, reads, last=True):
            S.op("pe", lambda e: e.transpose(out, in_, ident[:in_.shape[0], :in_.shape[0]]),
                 reads=list(reads) + ["ident"], writes=[PSN[bank]], inc=last)

        def act(out, in_, func, reads, writes, **kw):
            S.op("act", lambda e: e.activation(out=out, in_=in_, func=func, **kw), reads=reads, writes=writes)

        def tt(eng, out, in0, in1, op, reads, writes):
            S.op(eng, lambda e: e.tensor_tensor(out=out, in0=in0, in1=in1, op=op), reads=reads, writes=writes)

        def ts(eng, out, in0, s1, s2, op0, op1, reads, writes):
            if s2 is None:
                S.op(eng, lambda e: e.tensor_scalar(out=out, in0=in0, scalar1=s1, scalar2=None, op0=op0),
                     reads=reads, writes=writes)
            else:
                S.op(eng, lambda e: e.tensor_scalar(out=out, in0=in0, scalar1=s1, scalar2=s2, op0=op0, op1=op1),
                     reads=reads, writes=writes)

        def stt(eng, out, in0, scalar, in1, op0, op1, reads, writes):
            S.op(eng, lambda e: e.scalar_tensor_tensor(out=out, in0=in0, scalar=scalar, in1=in1, op0=op0, op1=op1),
                 reads=reads, writes=writes)

        def red(out, in_, reads, writes):
            S.op("dve", lambda e: e.tensor_reduce(out=out, in_=in_, axis=AX.X, op=ALU.add), reads=reads, writes=writes)

        def rsqrt(buf, scale, reads_writes):
            act(buf, buf, AF.Sqrt, reads=[reads_writes, "epsc"], writes=[reads_writes], scale=scale, bias=epsc[:buf.shape[0], :])
            S.op("dve", lambda e: e.reciprocal(out=buf, in_=buf), reads=[reads_writes], writes=[reads_writes])

        def ck(name):
            if stop == name:
                raise StopBuild()

        S.dma("sp", xT[:], xT_d.rearrange("(c p) t -> p c t", p=128), writes=["xT"])
        S.dma("sp", condT[:], cond_d, writes=["condT"])
        S.dma("sp", sel[:], sel_d, writes=["sel"])
        S.dma("sp", bmT[:], b_mod_d, writes=["bmT"])
        S.dma("sp", gn[:], gn_d, writes=["gn"])
        S.dma("sp", gqa[:], gqa_d, writes=["gqa"])
        S.dma("sp", idxt[:], idx_d, writes=["idxt"])
        S.dma("sp", idxq[:], idxq_d, writes=["idxq"])
        S.dma("sp", rope64[:], rope64_d, writes=["rope64"])
        S.dma("sp", rope32[:], rope32_d, writes=["rope32"])
        S.dma("sp", xco[:], xco_d, writes=["xco"])
        S.dma("pool", wm, w_mod_d.rearrange("l (c p) n -> p l c n", p=128), writes=["wm"])

        S.op("pool", lambda e: e.memset(ones[:], 1.0), writes=["ones"])
        S.op("pool", lambda e: e.memset(mlp[:], 0.0), writes=["ml"])
        S.op("pool", lambda e: e.memset(epsc[:], EPS), writes=["epsc"])
        S.op("pool", lambda e: e.memset(onec[:], 1.0), writes=["onec"])
        S.op("pool", lambda e: e.memset(identf[:], 0.0), writes=["identf"])
        S.op("pool", lambda e: e.affine_select(out=identf[:], in_=identf[:], pattern=[[-1, 128]],
                                               compare_op=ALU.not_equal, fill=1.0, base=0, channel_multiplier=1),
             reads=["identf"], writes=["identf"])
        S.op("dve", lambda e: e.tensor_copy(ident[:], identf[:]), reads=["identf"], writes=["ident"])

        def modulation_setup():
            ck("t0")
            act(sT[:], condT[:], AF.Silu, reads=["condT"], writes=["sT"])
            pm = PS[0][:, 0:72].rearrange("p (j l k) -> p j l k", j=12, l=2)
            for l in range(2):
                for cj in range(12):
                    for c in range(8):
                        mm(0, pm[:, cj, l, :], wm[:, l, c, cj * 128:(cj + 1) * 128], sT[:, c, :], c == 0, c == 7,
                           reads=["wm", "sT"])
            tt("dve", ml, pm, bcast(bmT[:].unsqueeze(3), [128, 12, 2, 3]), ALU.add, reads=[PSN[0], "bmT", "ml"], writes=["ml"])
            ck("t1")
            S.dma("sp", mb_d, mlp[:], reads=["ml"], writes=["mb_d"])
            S.collective([mb_c], [mg_c], [[0, 1, 2, 3], [4, 5, 6, 7]], reads=["mb_d"], writes=["mg_d"])
            S.dma("sp", MODT[:].rearrange("p (r j) l k -> p r (j l k)", r=4), mg_d.rearrange("(r p) f -> p r f", p=128)[:, :, 0:72],
                  reads=["mg_d"], writes=["MODT"])
            ck("t2")
            S.op("dve", lambda e: e.tensor_copy(MS[:, :, :, 0], MODT[:, :, :, 0]), reads=["MODT"], writes=["MS0"])
            ts("dve", MS[:, :, :, 1], MODT[:, :, :, 1], sel[:, 0:1], None, ALU.mult, None, reads=["MODT", "sel"], writes=["MS1"])
            stt("dve", MS[:, :, :, 1], MODT[:, :, :, 2], sel[:, 1:2], MS[:, :, :, 1], ALU.mult, ALU.add,
                reads=["MODT", "sel", "MS1"], writes=["MS1"])
            for l in range(2):
                for grp in range(2):
                    for which, k in ((0, 1), (1, 4)):
                        stt("dve", GG[:, l, grp, which, :], MS[:, k * 8:(k + 1) * 8, l, grp], 1.0, gn[:, l, which, :],
                            ALU.add, ALU.mult, reads=["MS0", "MS1", "gn"], writes=["GG"])

        def modcol(k, c, l, grp):
            return MS[:, k * 8 + c, l, grp:grp + 1]

        GROUPS = [(0, 512, 0), (512, 1024, 0), (1024, 1280, 1)]

        def norm_phase(l, which):
            kk_sh = 0 if which == 0 else 3
            for gi, (t0, t1, grp) in enumerate(GROUPS):
                n = t1 - t0
                act(nsq[:, :, 0:n], xT[:, :, t0:t1], AF.Square, reads=["xT:%d" % gi], writes=["nsq"])
                red(s8[:, 0:n], nsq[:, :, 0:n].rearrange("p c t -> p t c"), reads=["nsq"], writes=["s8"])
                mm(7, PS[7][:, 0:n], ones[:], s8[:, 0:n], True, True, reads=["ones", "s8"])
                act(rstd[:, 0:n], PS[7][:, 0:n], AF.Sqrt, reads=[PSN[7], "epsc"], writes=["rstd"], scale=1.0 / D, bias=epsc[:])
                S.op("dve", lambda e: e.reciprocal(out=rstd[:, 0:n], in_=rstd[:, 0:n]), reads=["rstd"], writes=["rstd"])
                for c in range(8):
                    tb = ntmp[c % 3]
                    stt("dve", tb[:, 0:n], xT[:, c, t0:t1], GG[:, l, grp, which, c:c + 1], rstd[:, 0:n], ALU.mult, ALU.mult,
                        reads=["xT:%d" % gi, "GG", "rstd"], writes=["ntmp%d" % (c % 3)])
                    act(hm[:, c, t0:t1], tb[:, 0:n], AF.Identity, reads=["ntmp%d" % (c % 3), "MS0", "MS1"],
                        writes=hmg(gi), bias=modcol(kk_sh, c, l, grp), scale=1.0)


        S.alias(["xT:0", "xT:1", "xT:2"], ["xT"])

        GSEG = [[0, 1], [2, 3], [4]]

        def hmg(gi):
            return ["hm:s%d" % sg_ for sg_ in GSEG[gi]]

        def layer_tables(l):
            S.dma("sp", gsm[:], gsm_d[l].partition_broadcast(128), writes=["gsm"])
            S.dma("sp", RP[:], retp_d[l].partition_broadcast(128), writes=["RP"])
            S.dma("sp", tabs, tabs_d, writes=["tabs"])
            ts("dve", gq96[:], gsm[:, G_QN:G_QN + 96], ATT_SCALE, None, ALU.mult, None, reads=["gsm"], writes=["gq96"])
            act(LG[:].rearrange("p d h -> p (d h)"), RP[:], AF.Exp, reads=["RP"], writes=["LG"], scale=-LN2)
            act(LG[:].rearrange("p d h -> p (d h)"), LG[:].rearrange("p d h -> p (d h)"), AF.Ln, reads=["LG", "onec"], writes=["LG"],
                scale=-1.0, bias=onec[:])
            for h in range(8):
                act(dtmp[:, 0], tabs[:, 0], AF.Exp, reads=["tabs", "LG"], writes=["dtmp0"], scale=LG[:, 0, h:h + 1])
                act(dtmp[:, 1], tabs[:, 1], AF.Exp, reads=["tabs", "LG"], writes=["dtmp1"], scale=LG[:, 1, h:h + 1])
                tt("dve", dtmp[:, 0], dtmp[:, 0], tabs[:, 2], ALU.mult, reads=["dtmp0", "tabs"], writes=["dtmp0"])
                tt("dve", dtmp[:, 1], dtmp[:, 1], tabs[:, 3], ALU.mult, reads=["dtmp1", "tabs"], writes=["dtmp1"])
                tt("dve", DT[:, h], dtmp[:, 0], dtmp[:, 1], ALU.add, reads=["dtmp0", "dtmp1"], writes=["DT"])
            for d in range(2):
                for jt in range(2):
                    act(KD[:, d, jt, :], LG[:, d, :], AF.Exp, reads=["LG", "idxt"], writes=["KD"], scale=idxt[:, d, jt:jt + 1])
                for it in range(2):
                    act(KQ[:, d, it, :], LG[:, d, :], AF.Exp, reads=["LG", "idxq"], writes=["KQ"], scale=idxq[:, d, it:it + 1])
            for d in range(2):
                tt("dve", xcf[:, d], bcast(LG[:, d, :].unsqueeze(1), [128, 5, 8]), bcast(xco[:, 0, d, :].unsqueeze(2), [128, 5, 8]),
                   ALU.mult, reads=["LG", "xco"], writes=["xcf"])
            act(xcf[:].rearrange("p d s h -> p (d s h)"), xcf[:].rearrange("p d s h -> p (d s h)"), AF.Exp, reads=["xcf"], writes=["xcf"])
            for d in range(2):
                tt("dve", xcf[:, d], xcf[:, d], bcast(xco[:, 1, d, :].unsqueeze(2), [128, 5, 8]), ALU.mult,
                   reads=["xcf", "xco"], writes=["xcf"])

        def rope(out, src, tl, tab, H, nf, reads, writes, scale=None):
            sv = src.rearrange("p (h f x n) -> p h f x n", h=H, f=2, x=2)
            ov = out.rearrange("p (h f x n) -> p h f x n", h=H, f=2, x=2)
            C = bcast(tab[:, tl, 0].unsqueeze(1), [128, H, 2, nf])
            Sn = bcast(tab[:, tl, 1].unsqueeze(1), [128, H, 2, nf])
            a = fA[:, 0:H * 2 * nf].rearrange("p (h f n) -> p h f n", h=H, f=2)
            b = fB[:, 0:H * 2 * nf].rearrange("p (h f n) -> p h f n", h=H, f=2)
            x1, x2 = sv[:, :, :, 0, :], sv[:, :, :, 1, :]
            tt("dve", a, x1, C, ALU.mult, reads=reads, writes=["fA"])
            tt("dve", b, x2, Sn, ALU.mult, reads=reads, writes=["fB"])
            tt("dve", ov[:, :, :, 0, :], a, b, ALU.subtract, reads=["fA", "fB"], writes=writes)
            tt("dve", a, x1, Sn, ALU.mult, reads=reads, writes=["fA"])
            tt("dve", b, x2, C, ALU.mult, reads=reads, writes=["fB"])
            tt("dve", ov[:, :, :, 1, :], a, b, ALU.add, reads=["fA", "fB"], writes=writes)

        def seg_project(l, seg, emit_out=True):
            is_s = seg == 4
            S.alias(["q_tm", "k_tm"], ["mix_tm"])
            S.alias(["KcT", "VX"], ["Sacc", "Sin"])
            gi = 2 if is_s else seg // 2
            for tl in range(2):
                tile = seg * 2 + tl
                t0 = tile * 128
                for cg in range(5):
                    c0 = cg * 512
                    ncol = min(512, IN_COLS - c0)
                    for c in range(8):
                        mm(cg, PS[cg][:, 0:ncol], hm[:, c, t0:t0 + 128], w_in[:, c, c0:c0 + ncol], c == 0, c == 7,
                           reads=["hm:s%d" % seg, "w_in"])
                if is_s:
                    rope(fqk0, PS[0][:, :], tl, rope64, 8, 16, reads=[PSN[0], "rope64"], writes=["fqk"])
                    S.op("act", lambda e: e.copy(q_tm[:, tl, :], fqk0), reads=["fqk"], writes=["q_tm"])
                else:
                    S.op("act", lambda e: e.copy(q_tm[:, tl, :], PS[0][:, :]), reads=[PSN[0]], writes=["q_tm"])
                if is_s:
                    rope(fqk0, PS[1][:, :], tl, rope64, 8, 16, reads=[PSN[1], "rope64"], writes=["fqk"])
                    S.op("act", lambda e: e.mul(k_tm[:, tl, :], fqk0, 0.125), reads=["fqk"], writes=["k_tm"])
                else:
                    S.op("act", lambda e: e.mul(k_tm[:, tl, :], PS[1][:, :], 0.125), reads=[PSN[1]], writes=["k_tm"])
                S.op("dve", lambda e: e.tensor_copy(v_tm[:, tl, :], PS[2][:, :]), reads=[PSN[2]], writes=["v_tm"])
                act(g_tm[:, tl, :], PS[3][:, :], AF.Silu, reads=[PSN[3]], writes=["g_tm"])
                zq = PS[4]
                act(sqt[:, 0:416], zq[:, 0:416], AF.Square, reads=[PSN[4]], writes=["sqt"])
                red(st8[:, 0:1], sqt[:, 0:256], reads=["sqt"], writes=["st8"])
                red(st8[:, 1:2], sqt[:, 256:384], reads=["sqt"], writes=["st8"])
                red(st8[:, 2:3], sqt[:, 384:416], reads=["sqt"], writes=["st8"])
                act(st8[:, 0:1], st8[:, 0:1], AF.Sqrt, reads=["st8", "epsc"], writes=["st8"], scale=1.0 / 256, bias=epsc[:])
                act(st8[:, 1:2], st8[:, 1:2], AF.Sqrt, reads=["st8", "epsc"], writes=["st8"], scale=1.0 / 128, bias=epsc[:])
                act(st8[:, 2:3], st8[:, 2:3], AF.Sqrt, reads=["st8", "epsc"], writes=["st8"], scale=1.0 / 32, bias=epsc[:])
                S.op("dve", lambda e: e.reciprocal(out=st8[:, 0:3], in_=st8[:, 0:3]), reads=["st8"], writes=["st8"])
                ts("dve", qa_n[:, tl, :], zq[:, 0:256], st8[:, 0:1], None, ALU.mult, None, reads=[PSN[4], "st8"], writes=["qa_n"])
                stt("dve", ckv[:, tl, :], zq[:, 256:384], st8[:, 1:2], gsm[:, G_KVA:G_KVA + 128], ALU.mult, ALU.mult,
                    reads=[PSN[4], "st8", "gsm"], writes=["ckv"])
                if is_s:
                    stt("dve", fB[:, 256:288], zq[:, 384:416], st8[:, 2:3], gsm[:, G_KR:G_KR + 32], ALU.mult, ALU.mult,
                        reads=[PSN[4], "st8", "gsm"], writes=["fkr"])
                    rope(kr[:, tl, :], fB[:, 256:288], tl, rope32, 1, 8, reads=["fkr", "rope32"], writes=["kr"])
                else:
                    stt("dve", kr[:, tl, :], zq[:, 384:416], st8[:, 2:3], gsm[:, G_KR:G_KR + 32], ALU.mult, ALU.mult,
                        reads=[PSN[4], "st8", "gsm"], writes=["kr"])
            if not emit_out:
                return
            if is_s:
                S.dma("sp", kvb_d[l][:, 0:128].rearrange("(t p) n -> p t n", p=128), ckv[:], reads=["ckv"], writes=["kvb_d%d" % l])
                S.dma("sp", kvb_d[l][:, 128:160].rearrange("(t p) n -> p t n", p=128), kr[:], reads=["kr"], writes=["kvb_d%d" % l])
            else:
                S.dma("sp", ockv_d[l, seg * 256:(seg + 1) * 256, :].rearrange("(t p) n -> p t n", p=128), ckv[:], reads=["ckv"], writes=["ockv"])
                S.dma("sp", okr_d[l, seg * 256:(seg + 1) * 256, :].rearrange("(t p) n -> p t n", p=128), kr[:], reads=["kr"], writes=["okr"])

        fqk0 = sqt[:, 0:512]

        def seg_q(l, seg):
            is_s = seg == 4
            for tl in range(2):
                pt = PS[5][:, 0:128].bitcast(BF16).rearrange("p (c i) -> p c i", c=2)
                for c in range(2):
                    tr(5, pt[:, c, :], qa_n[:, tl, c * 128:(c + 1) * 128], reads=["qa_n"], last=(c == 1))
                for c in range(2):
                    act(qaT[:, c, tl * 128:(tl + 1) * 128], pt[:, c, :], AF.Copy, reads=[PSN[5], "gqa"], writes=["qaT"],
                        scale=gqa[:, l, c:c + 1])
                for half in range(2):
                    for c in range(2):
                        mm(6 + half, PS[6 + half][:, 0:384], qaT[:, c, tl * 128:(tl + 1) * 128],
                           w_qb[:, c, half * 384:(half + 1) * 384], c == 0, c == 1, reads=["qaT", "w_qb"])
                for half in range(2):
                    qv = PS[6 + half][:, 0:384].rearrange("p (h e) -> p h e", h=4)
                    sq = sqt[:, half * 384:(half + 1) * 384].rearrange("p (h e) -> p h e", h=4)
                    act(sq, qv, AF.Square, reads=[PSN[6 + half]], writes=["sqt"])
                    red(st8[:, 8 + half * 4:12 + half * 4], sq[:, :, 0:64], reads=["sqt"], writes=["st8q"])
                    red(st8[:, 16 + half * 4:20 + half * 4], sq[:, :, 64:96], reads=["sqt"], writes=["st8q"])
                act(st8[:, 8:16], st8[:, 8:16], AF.Sqrt, reads=["st8q", "epsc"], writes=["st8q"], scale=1.0 / 64, bias=epsc[:])
                act(st8[:, 16:24], st8[:, 16:24], AF.Sqrt, reads=["st8q", "epsc"], writes=["st8q"], scale=1.0 / 32, bias=epsc[:])
                S.op("dve", lambda e: e.reciprocal(out=st8[:, 8:24], in_=st8[:, 8:24]), reads=["st8q"], writes=["st8q"])
                for half in range(2):
                    qv = PS[6 + half][:, 0:384].rearrange("p (h e) -> p h e", h=4)
                    dst = fA[:, 0:384].rearrange("p (h e) -> p h e", h=4) if half == 0 else fB[:, 0:384].rearrange("p (h e) -> p h e", h=4)
                    nm = "fA" if half == 0 else "fB"
                    tt("dve", dst[:, :, 0:64], qv[:, :, 0:64], bcast(st8[:, 8 + half * 4:12 + half * 4].unsqueeze(2), [128, 4, 64]),
                       ALU.mult, reads=[PSN[6 + half], "st8q"], writes=[nm])
                    tt("dve", dst[:, :, 64:96], qv[:, :, 64:96], bcast(st8[:, 16 + half * 4:20 + half * 4].unsqueeze(2), [128, 4, 32]),
                       ALU.mult, reads=[PSN[6 + half], "st8q"], writes=[nm])
                    if not is_s:
                        tt("dve", Qc_tm[:, tl, half * 4:(half + 1) * 4, :], dst, bcast(gq96[:].unsqueeze(1), [128, 4, 96]), ALU.mult,
                           reads=[nm, "gq96"], writes=["Qc_tm"])
                    else:
                        tt("dve", dst, dst, bcast(gq96[:].unsqueeze(1), [128, 4, 96]), ALU.mult, reads=[nm, "gq96"], writes=[nm])
                        S.op("act", lambda e: e.copy(Qc_tm[:, tl, half * 4:(half + 1) * 4, 0:64], dst[:, :, 0:64]), reads=[nm], writes=["Qc_tm"])
                        S.op("act", lambda e: e.copy(sqt[:, 0:128].rearrange("p (h e) -> p h e", h=4), dst[:, :, 64:96]), reads=[nm], writes=["sqtr"])
                        rope(sqt[:, 128:256], sqt[:, 0:128], tl, rope32, 4, 8, reads=["sqtr", "rope32"], writes=["sqtr2"])
                        S.op("act", lambda e: e.copy(Qc_tm[:, tl, half * 4:(half + 1) * 4, 64:96],
                                                     sqt[:, 128:256].rearrange("p (h e) -> p h e", h=4)), reads=["sqtr2"], writes=["Qc_tm"])
                pq = PS[5][:, :].bitcast(BF16).rearrange("p (h i) -> p h i", h=8)
                for h in range(8):
                    tr(5, pq[0:96, h, :], Qc_tm[:, tl, h, :], reads=["Qc_tm"], last=(h == 7))
                S.op("act", lambda e: e.copy(QcT[0:96, :, tl * 128:(tl + 1) * 128], pq[0:96, :, :]), reads=[PSN[5]], writes=["QcT"])

        def mla_up(src_ckv, src_kr, KT_dst, VX_dst, reads, kt_name, vx_name, par=0):
            cb_, cT_, Kc_ = (ckvb, ckvT, Kc_tm) if par == 0 else (ckvb2, ckvT2, Kc_tm2)
            n_cb, n_cT, n_Kc = ("ckvb", "ckvT", "Kc_tm") if par == 0 else ("ckvb1", "ckvT1", "Kc_tm1")
            tb = 5 if par == 0 else 4
            kb = (6, 7) if par == 0 else (0, 1)
            so = 24 if par == 0 else 52
            n_st = "st8k%d" % par
            S.op("dve", lambda e: e.tensor_copy(cb_, src_ckv), reads=reads, writes=[n_cb])
            pt = PS[tb][:, 0:64].bitcast(BF16)
            tr(tb, pt, cb_, reads=[n_cb])
            S.op("act", lambda e: e.copy(cT_, pt), reads=[PSN[tb]], writes=[n_cT])
            for half in range(2):
                mm(kb[half], PS[kb[half]][:, :], cT_, w_kvb[:, half * 512:(half + 1) * 512], True, True, reads=[n_cT, "w_kvb"])
            for half in range(2):
                kv = PS[kb[half]][:, :].rearrange("p (h e) -> p h e", h=4)
                sq = sqt[:, half * 256:(half + 1) * 256].rearrange("p (h e) -> p h e", h=4)
                act(sq, kv[:, :, 0:64], AF.Square, reads=[PSN[kb[half]]], writes=["sqt"])
                red(st8[:, so + half * 4:so + 4 + half * 4], sq, reads=["sqt"], writes=[n_st])
                S.op("act", lambda e: e.copy(VX_dst[:, half * 4:(half + 1) * 4, 0:64], kv[:, :, 64:128]), reads=[PSN[kb[half]]], writes=[vx_name])
            act(st8[:, so:so + 8], st8[:, so:so + 8], AF.Sqrt, reads=[n_st, "epsc"], writes=[n_st], scale=1.0 / 64, bias=epsc[:])
            S.op("dve", lambda e: e.reciprocal(out=st8[:, so:so + 8], in_=st8[:, so:so + 8]), reads=[n_st], writes=[n_st])
            for half in range(2):
                kv = PS[kb[half]][:, :].rearrange("p (h e) -> p h e", h=4)
                tmp = fA[:, 0:256].rearrange("p (h e) -> p h e", h=4) if half == 0 else fB[:, 0:256].rearrange("p (h e) -> p h e", h=4)
                nm = "fA" if half == 0 else "fB"
                tt("dve", tmp, kv[:, :, 0:64], bcast(st8[:, so + half * 4:so + 4 + half * 4].unsqueeze(2), [128, 4, 64]), ALU.mult,
                   reads=[PSN[kb[half]], n_st], writes=[nm])
                tt("dve", Kc_[:, half * 4:(half + 1) * 4, 0:64], tmp, bcast(gsm[:, G_KN:G_KN + 64].unsqueeze(1), [128, 4, 64]), ALU.mult,
                   reads=[nm, "gsm"], writes=[n_Kc])
            S.op("act", lambda e: e.copy(Kc_[:, :, 64:96], bcast(src_kr.unsqueeze(1), [128, 8, 32])), reads=reads, writes=[n_Kc])
            S.op("dve", lambda e: e.memset(VX_dst[:, :, 64:65], 1.0), writes=[vx_name])
            pk = PS[tb][:, :].bitcast(BF16).rearrange("p (h i) -> p h i", h=8)
            for h in range(8):
                tr(tb, pk[0:96, h, :], Kc_[:, h, :], reads=[n_Kc], last=(h == 7))
            S.op("act", lambda e: e.copy(KT_dst, pk[0:96, :, :]), reads=[PSN[tb]], writes=[kt_name])

        def seg_retention(l, seg, emit_out=True):
            is_s = seg == 4
            for w, (src, nm) in enumerate(((q_tm, "q_tm"), (k_tm, "k_tm"))):
                for tl in range(2):
                    pt = PS[5][:, 0:256].bitcast(BF16).rearrange("p (c i) -> p c i", c=4)
                    for c in range(4):
                        tr(5, pt[:, c, :], src[:, tl, c * 128:(c + 1) * 128], reads=[nm], last=(c == 3))
                    S.op("act", lambda e: e.copy(qkT[:, w, :, tl * 128:(tl + 1) * 128], pt), reads=[PSN[5]], writes=["qkT"])
            for d in range(2):
                for tl in range(2):
                    tt("dve", kdec[:, d, tl, :].rearrange("p (h e) -> p h e", h=8), k_tm[:, tl, :].rearrange("p (h e) -> p h e", h=8),
                       bcast(KD[:, d, tl, :].unsqueeze(2), [128, 8, 64]), ALU.mult, reads=["k_tm", "KD"], writes=["kdec"])
            for d, bank in ((0, 4), (1, 6)):
                for pr in range(4):
                    for tl in range(2):
                        mm(bank, PS[bank][:, pr * 128:(pr + 1) * 128], kdec[:, d, tl, pr * 128:(pr + 1) * 128],
                           v_tm[:, tl, pr * 128:(pr + 1) * 128], tl == 0, tl == 1, reads=["kdec", "v_tm"])
                S.op("act", lambda e: e.copy(Ust[:, d, :], PS[bank][:, :]), reads=[PSN[bank]], writes=["Ust"])
            Uv = Ust[:].rearrange("p d (r e) -> p d r e", r=4)
            if not emit_out:
                return
            if is_s:
                dst = ub_d[l].rearrange("(d r x k) v -> k d r x v", d=2, r=4, x=2)
                for d in range(2):
                    S.dma("sp", dst[:, d, :, 0, :], Uv[0:64, d, :, 0:64], reads=["Ust"], writes=["ub_d%d" % l])
                    S.dma("sp", dst[:, d, :, 1, :], Uv[64:128, d, :, 64:128], reads=["Ust"], writes=["ub_d%d" % l])
            else:
                for d, od in ((0, osf_d), (1, osb_d)):
                    dst = od[l, seg].rearrange("(r x) k v -> k r x v", x=2)
                    S.dma("sp", dst[:, :, 0, :], Uv[0:64, d, :, 0:64], reads=["Ust"], writes=["ost"])
                    S.dma("sp", dst[:, :, 1, :], Uv[64:128, d, :, 64:128], reads=["Ust"], writes=["ost"])

        def seg_retention_out(l, seg):
            is_s = seg == 4
            S.alias(["mix_tm"], ["q_tm", "k_tm"])
            for h in range(8):
                c, po = h // 2, (h % 2) * 64
                bank = h % 2
                pst = PS[bank][:, :].rearrange("p (j i) -> p j i", j=2)
                for jt in range(2):
                    mm(bank, pst[:, jt, :], qkT[po:po + 64, 1, c, jt * 128:(jt + 1) * 128], qkT[po:po + 64, 0, c, :], True, True,
                       reads=["qkT"])
                at = AT[h % 2]
                tt("dve", at, pst, DT[:, h], ALU.mult, reads=[PSN[bank], "DT"], writes=["AT%d" % (h % 2)])
                for it in range(2):
                    ob = 2 + it
                    o = PS[ob][:, h * 64:(h + 1) * 64]
                    for jt in range(2):
                        mm(ob, o, at[:, jt, it * 128:(it + 1) * 128], v_tm[:, jt, h * 64:(h + 1) * 64], jt == 0, jt == 1,
                           reads=["AT%d" % (h % 2), "v_tm"])
                    if is_s:
                        for d in range(2):
                            cb = 4 + it * 2 + d
                            mm(cb, PS[cb][:, h * 64:(h + 1) * 64], qkT[po:po + 64, 0, c, it * 128:(it + 1) * 128], Sin[po:po + 64, d, h, :],
                               True, True, reads=["qkT", "Sin"])
            for it in range(2):
                ob = 2 + it
                ov = PS[ob][:, :].rearrange("p (h e) -> p h e", h=8)
                ovn = [PSN[ob]]
                if is_s:
                    osb = sqt[:, 0:512].rearrange("p (h e) -> p h e", h=8)
                    for d in range(2):
                        cb = 4 + it * 2 + d
                        tmp = (fA if d == 0 else fB)[:, 0:512].rearrange("p (h e) -> p h e", h=8)
                        tt("dve", tmp, PS[cb][:, :].rearrange("p (h e) -> p h e", h=8), bcast(KQ[:, d, it, :].unsqueeze(2), [128, 8, 64]),
                           ALU.mult, reads=[PSN[cb], "KQ"], writes=["fA" if d == 0 else "fB"])
                    tt("dve", osb, ov, fA[:, 0:512].rearrange("p (h e) -> p h e", h=8), ALU.add, reads=[PSN[ob], "fA"], writes=["sqt"])
                    tt("dve", osb, osb, fB[:, 0:512].rearrange("p (h e) -> p h e", h=8), ALU.add, reads=["sqt", "fB"], writes=["sqt"])
                    ov = osb
                    ovn = ["sqt"]
                red(st8[:, 32:40], ov, reads=ovn, writes=["st8g"])
                ts("dve", st8[:, 32:40], st8[:, 32:40], 1.0 / 64, None, ALU.mult, None, reads=["st8g"], writes=["st8g"])
                dv_ = fA[:, 0:512].rearrange("p (h e) -> p h e", h=8)
                tt("dve", dv_, ov, bcast(st8[:, 32:40].unsqueeze(2), [128, 8, 64]), ALU.subtract, reads=ovn + ["st8g"], writes=["fA"])
                sq = fB[:, 0:512].rearrange("p (h e) -> p h e", h=8)
                act(sq, dv_, AF.Square, reads=["fA"], writes=["fB"])
                red(st8[:, 40:48], sq, reads=["fB"], writes=["st8h"])
                act(st8[:, 40:48], st8[:, 40:48], AF.Sqrt, reads=["st8h", "epsc"], writes=["st8h"], scale=1.0 / 64, bias=epsc[:])
                S.op("dve", lambda e: e.reciprocal(out=st8[:, 40:48], in_=st8[:, 40:48]), reads=["st8h"], writes=["st8h"])
                tt("dve", dv_, dv_, bcast(st8[:, 40:48].unsqueeze(2), [128, 8, 64]), ALU.mult, reads=["fA", "st8h"], writes=["fA"])
                tt("dve", fA[:, 0:512], fA[:, 0:512], gsm[:, G_GN:G_GN + 512], ALU.mult, reads=["fA", "gsm"], writes=["fA"])
                tt("dve", fA[:, 0:512], fA[:, 0:512], gsm[:, B_GN:B_GN + 512], ALU.add, reads=["fA", "gsm"], writes=["fA"])
                tt("dve", mix_tm[:, it, 0:512], fA[:, 0:512], g_tm[:, it, :], ALU.mult, reads=["fA", "g_tm"], writes=["mix_tm"])

        def seg_attention(l, seg, nkt, KT, VXs, kt_name, vx_name):
            for hg in range(2):
                for hh in range(4):
                    h = hg * 4 + hh
                    for kt in range(nkt):
                        sbk = (h * nkt + kt) % 2
                        ps_ = PS[sbk][:, 0:256]
                        mm(sbk, ps_, KT[0:96, h, kt * 128:(kt + 1) * 128], QcT[0:96, h, :], True, True, reads=list(kt_name) + ["QcT"])
                        p = PT[(h * nkt + kt) % 2]
                        pn = "PT%d" % ((h * nkt + kt) % 2)
                        act(p, ps_, AF.Exp, reads=[PSN[sbk]], writes=[pn])
                        for it in range(2):
                            ob = 2 + it if hg == 0 else 4 + 2 * it
                            mm(ob, PS[ob][:, hh * 65:(hh + 1) * 65], p[:, it * 128:(it + 1) * 128], VXs(kt)[:, h, :], kt == 0, kt == nkt - 1,
                               reads=[pn] + list(vx_name))
            for hg in range(2):
                for it in range(2):
                    ob = 2 + it if hg == 0 else 4 + 2 * it
                    ov = PS[ob][:, 0:260].rearrange("p (h e) -> p h e", h=4)
                    S.op("dve", lambda e: e.reciprocal(out=st8[:, 48:52], in_=ov[:, :, 64]), reads=[PSN[ob]], writes=["st8a"])
                    tt("dve", mix_tm[:, it, 512 + hg * 256:512 + (hg + 1) * 256].rearrange("p (h e) -> p h e", h=4), ov[:, :, 0:64],
                       bcast(st8[:, 48:52].unsqueeze(2), [128, 4, 64]), ALU.mult, reads=[PSN[ob], "st8a"], writes=["mix_tm"])

        def seg_mix_out(seg):
            gi = 2 if seg == 4 else seg // 2
            for tl in range(2):
                t0 = (seg * 2 + tl) * 128
                pt = PS[7][:, :].bitcast(BF16).rearrange("p (c i) -> p c i", c=8)
                for c in range(8):
                    tr(7, pt[:, c, :], mix_tm[:, tl, c * 128:(c + 1) * 128], reads=["mix_tm"], last=(c == 7))
                S.op("act", lambda e: e.copy(hm[:, :, t0:t0 + 128], pt), reads=[PSN[7]], writes=["hm:s%d" % seg])

        def sample_exchange(l):
            S.collective([kvb_c[l]], [kvg_c[l]], [[0, 1, 2, 3], [4, 5, 6, 7]], reads=["kvb_d%d" % l], writes=["kvg_d%d" % l])
            S.collective([ub_c[l]], [ug_c[l]], [[0, 1, 2, 3], [4, 5, 6, 7]], reads=["ub_d%d" % l], writes=["ug_d%d" % l])

        def sample_states(l):
            ugv = ug_d[l].rearrange("(s d h k) v -> k s d h v", s=4, d=2, h=8)
            for d in range(2):
                for s_ in range(5):
                    if s_ < 4:
                        S.dma("sp", Ug[0:64], ugv[:, s_, d], reads=["ug_d%d" % l], writes=["Ug"])
                    else:
                        S.dma("sp", Ug[0:64], s0_d[l, d].rearrange("h k v -> k h v"), writes=["Ug"])
                    cf = bcast(xcf[0:64, d, s_, :].unsqueeze(2), [64, 8, 64])
                    if s_ == 0:
                        tt("dve", Sacc[0:64, d], Ug[0:64], cf, ALU.mult, reads=["Ug", "xcf"], writes=["Sacc"])
                    else:
                        tt("dve", fA[0:64, 0:512].rearrange("p (h e) -> p h e", h=8), Ug[0:64], cf, ALU.mult,
                           reads=["Ug", "xcf"], writes=["fA"])
                        tt("dve", Sacc[0:64, d], Sacc[0:64, d], fA[0:64, 0:512].rearrange("p (h e) -> p h e", h=8), ALU.add,
                           reads=["Sacc", "fA"], writes=["Sacc"])
            S.op("act", lambda e: e.copy(Sin[0:64], Sacc[0:64]), reads=["Sacc"], writes=["Sin"])
            S.dma("sp", Sin[64:128], Sin[0:64], reads=["Sin"], writes=["Sin"])

        def sample_keys(l):
            S.alias(["sKT0", "sKT1", "sVX0", "sVX1"], ["w_in"])
            kg = kvg_d[l].rearrange("(t p) n -> t p n", p=128)
            for kt in range(10):
                kb_, kn_ = (kvin, "kvin") if kt % 2 == 0 else (kvin2, "kvin1")
                if kt < 8:
                    S.dma("sp", kb_[:, :], kg[kt], reads=["kvg_d%d" % l], writes=[kn_])
                else:
                    S.dma("sp", kb_[:, 0:128], cckv_d[l, (kt - 8) * 128:(kt - 7) * 128, :], writes=[kn_])
                    S.dma("sp", kb_[:, 128:160], ckr_d[l, (kt - 8) * 128:(kt - 7) * 128, :], writes=[kn_])
                mla_up(kb_[:, 0:128], kb_[:, 128:160], sKT[0:96, :, kt * 128:(kt + 1) * 128], sVX[:, kt], [kn_], "sKT%d" % (kt % 2), "sVX%d" % (kt % 2),
                       par=kt % 2)

        def phase_wo(l):
            for gi, (t0, t1, grp) in enumerate(GROUPS):
                n = t1 - t0
                for j in range(8):
                    bank = j % 4
                    for c in range(8):
                        mm(bank, PS[bank][:, 0:n], w_o[:, c, j * 128:(j + 1) * 128], hm[:, c, t0:t1], c == 0, c == 7,
                           reads=["w_o"] + hmg(gi))
                    stt("dve", xT[:, j, t0:t1], PS[bank][:, 0:n], modcol(2, j, l, grp), xT[:, j, t0:t1], ALU.mult, ALU.add,
                        reads=[PSN[bank], "MS0", "MS1", "xT:%d" % gi], writes=["xT:%d" % gi])

        def phase_ffn(l):
            S.alias(["actT"], ["DT", "wm"] + SEGN + NRMN)
            S.alias(["wfi0", "wfi1"], ["w_o"])
            S.alias(["w_fo"], ["w_in", "w_qb", "w_kvb", "sKT0", "sKT1", "sVX0", "sVX1"])
            for jb in range(11):
                buf = jb % 2
                wv = w_fi[buf]
                S.dma("pool", wv[:, :, 0:256], w_fi_d[l, :, jb * 256:(jb + 1) * 256].rearrange("(c p) n -> p c n", p=128), writes=["wfi%d" % buf])
                S.dma("pool", wv[:, :, 256:512], w_fi_d[l, :, DFF + jb * 256:DFF + (jb + 1) * 256].rearrange("(c p) n -> p c n", p=128),
                      writes=["wfi%d" % buf])
                if jb == 1:
                    S.dma("pool", w_fo, w_fo_d[l].rearrange("(c p) n -> p c n", p=128), writes=["w_fo"])
                for jj in range(2):
                    j = 2 * jb + jj
                    for gi, (t0, t1, grp) in enumerate(GROUPS):
                        n = t1 - t0
                        pb = ((j * 3 + gi) % 2) * 2
                        for half in range(2):
                            for c in range(8):
                                mm(pb + half, PS[pb + half][:, 0:n], wv[:, c, half * 256 + jj * 128:half * 256 + (jj + 1) * 128], hm[:, c, t0:t1],
                                   c == 0, c == 7, reads=["wfi%d" % buf] + hmg(gi))
                        sgb = sg[(j * 3 + gi) % 2]
                        sgn = "sg%d" % ((j * 3 + gi) % 2)
                        act(sgb[:, 0:n], PS[pb][:, 0:n], AF.Silu, reads=[PSN[pb]], writes=[sgn])
                        tt("dve", actT[:, j, t0:t1], sgb[:, 0:n], PS[pb + 1][:, 0:n], ALU.mult, reads=[sgn, PSN[pb + 1]], writes=["actT"])
            for gi, (t0, t1, grp) in enumerate(GROUPS):
                n = t1 - t0
                for j in range(8):
                    bank = 4 + (j % 4)
                    for c in range(22):
                        mm(bank, PS[bank][:, 0:n], w_fo[:, c, j * 128:(j + 1) * 128], actT[:, c, t0:t1], c == 0, c == 21,
                           reads=["w_fo", "actT"])
                    stt("dve", xT[:, j, t0:t1], PS[bank][:, 0:n], modcol(5, j, l, grp), xT[:, j, t0:t1], ALU.mult, ALU.add,
                        reads=[PSN[bank], "MS0", "MS1", "xT:%d" % gi], writes=["xT:%d" % gi])

        def main_program():
            modulation_setup()
            S.alias(["DT"] + NRMN, ["wm"])
            ck("s0")
            for l in range(2):
                if l > 0:
                    S.alias(["w_in", "w_qb", "w_kvb"], ["w_fo"])
                    S.alias(["w_o"], ["wfi0", "wfi1"])
                    S.alias(["DT"] + NRMN, ["actT"])
                S.dma("pool", w_in, w_in_d[l].rearrange("(c p) n -> p c n", p=128), writes=["w_in"])
                S.dma("pool", w_qb, w_qb_d[l].rearrange("(c p) n -> p c n", p=128), writes=["w_qb"])
                S.dma("pool", w_kvb, w_kvb_d[l], writes=["w_kvb"])
                S.dma("pool", w_o, w_o_d[l].rearrange("(c p) n -> p c n", p=128), writes=["w_o"])
                ck("s1")
                layer_tables(l)
                ck("s2")
                norm_phase(l, 0)
                ck("a0")
                if stop == 1:
                    dbg_dump(S, "hm", hm[:], [128, 8, TOK], BF16, ["hm:s%d" % i for i in range(5)])
                    break
                S.alias(SEGN, NRMN)
                seg_project(l, 4)
                ck("a1")
                seg_retention(l, 4)
                S.dma("sp", sp_qk[l], qkT[:].rearrange("p w c i -> p (w c i)"), reads=["qkT"], writes=["sp_qk%d" % l])
                S.dma("sp", sp_v[l], v_tm[:].rearrange("p t n -> p (t n)"), reads=["v_tm"], writes=["sp_v%d" % l])
                S.dma("sp", sp_g[l], g_tm[:].rearrange("p t n -> p (t n)"), reads=["g_tm"], writes=["sp_g%d" % l])
                S.dma("sp", sp_qa[l], qa_n[:].rearrange("p t n -> p (t n)"), reads=["qa_n"], writes=["sp_qa%d" % l])
                ck("a2")
                sample_exchange(l)
                ck("a3")
                for seg in range(4):
                    seg_project(l, seg)
                    seg_retention(l, seg)
                    ck("b1")
                    seg_retention_out(l, seg)
                    ck("b2")
                    seg_q(l, seg)
                    ck("b3")
                    for tl in range(2):
                        mla_up(ckv[:, tl, :], kr[:, tl, :], KcT[0:96, :, tl * 128:(tl + 1) * 128], VX[:, tl], ["ckv", "kr"], "KcT", "VX", par=tl)
                    ck("b4")
                    seg_attention(l, seg, 2, KcT, lambda kt: VX[:, kt], ["KcT"], ["VX"])
                    ck("b5")
                    seg_mix_out(seg)
                    ck("b6")
                S.alias(["q_tm", "k_tm"], ["mix_tm"])
                S.alias(["KcT", "VX"], ["Sacc", "Sin"])
                S.dma("sp", qkT[:].rearrange("p w c i -> p (w c i)"), sp_qk[l], reads=["sp_qk%d" % l], writes=["qkT"])
                S.dma("sp", v_tm[:].rearrange("p t n -> p (t n)"), sp_v[l], reads=["sp_v%d" % l], writes=["v_tm"])
                S.dma("sp", g_tm[:].rearrange("p t n -> p (t n)"), sp_g[l], reads=["sp_g%d" % l], writes=["g_tm"])
                S.dma("sp", qa_n[:].rearrange("p t n -> p (t n)"), sp_qa[l], reads=["sp_qa%d" % l], writes=["qa_n"])
                S.alias(["Sacc", "Sin"], ["KcT", "VX"])
                ck("c0")
                sample_states(l)
                ck("c1")
                seg_retention_out(l, 4)
                seg_q(l, 4)
                ck("c2")
                sample_keys(l)
                ck("c3")
                seg_attention(l, 4, 10, sKT, lambda kt: sVX[:, kt], ["sKT0", "sKT1"], ["sVX0", "sVX1"])
                seg_mix_out(4)
                if stop == 2:
                    dbg_dump(S, "mixT", hm[:], [128, 8, TOK], BF16, ["hm:s%d" % i for i in range(5)])
                    break
                phase_wo(l)
                S.alias(NRMN, SEGN)
                norm_phase(l, 1)
                phase_ffn(l)
                if stop == 3:
                    break


        try:
            main_program()
        except StopBuild:
            pass

        for gi, (t0, t1, grp) in enumerate(GROUPS):
            S.dma("sp", yT_d.rearrange("(c p) t -> p c t", p=128)[:, :, t0:t1], xT[:, :, t0:t1], reads=["xT:%d" % gi], writes=["yT_d%d" % gi])
        S.wait_all("sp")
        print("program built: waits=%d counts=%s dmas=%s r3=%d" % (S.n_wait, S.cnt, S.dcnt, r3o[0]))
    return nc, dbg_out


def _rope_tab(pos_row, pos_col, nf):
    inv = (10000.0 ** (-np.arange(nf, dtype=np.float64) / nf))
    ar = pos_row[:, None].astype(np.float64) * inv
    ac = pos_col[:, None].astype(np.float64) * inv
    C = np.stack([np.cos(ar), np.cos(ac)], 1)
    Sn = np.stack([np.sin(ar), np.sin(ac)], 1)
    return np.stack([C, Sn], 1).astype(np.float32)


def make_inputs(inp):
    f = lambda a: np.ascontiguousarray(np.asarray(a, dtype=np.float32))
    x_prompt, x_sample = f(inp["x_prompt"]), f(inp["x_sample"])
    shared = {k: f(inp[k]) for k in ("w_in", "w_q_b", "w_kv_b", "w_o", "w_ffn_in", "w_ffn_out")}
    w_mod, b_mod = f(inp["w_mod"]), f(inp["b_mod"])
    conds = np.stack([f(inp["c_ctx"]), f(inp["c"])[0], f(inp["c"])[1]], 0)
    condT = f(conds.reshape(3, 8, 128).transpose(2, 1, 0))
    gnT = f(np.stack([f(inp["g_norm_mix"]), f(inp["g_norm_ffn"])], 1).reshape(2, 2, 8, 128).transpose(3, 0, 1, 2))
    gqaT = f(f(inp["g_q_a"]).reshape(2, 2, 128).transpose(2, 0, 1))
    gsm = f(np.concatenate([f(inp[k]) for k in ("g_kv_a", "g_qn", "g_qr", "g_kn", "g_kr", "g_ret_gn", "b_ret_gn")], 1))
    assert gsm.shape == (2, GSM_N)
    retp = f(np.concatenate([f(inp["ret_p_fwd"]), f(inp["ret_p_bwd"])], 1))
    j = np.arange(128)[:, None, None] + 128 * np.arange(2)[None, :, None]
    i = np.arange(256)[None, None, :]
    diff = (i - j).astype(np.float32)
    tabs = f(np.stack([np.maximum(diff, 0), np.maximum(-diff, 0), (diff >= 0).astype(np.float32),
                       (diff <= 0).astype(np.float32)], 1))
    jj = np.arange(128)[:, None] + 128 * np.arange(2)[None, :]
    idxt = f(np.stack([255.0 - jj, jj.astype(np.float64)], 1))
    idxq = f(np.stack([jj + 1.0, 256.0 - jj], 1))
    maps = []
    for core in range(NCORES):
        r, seq = core % 4, core // 4
        xs = np.concatenate([x_prompt[4 * core + b] for b in range(4)] + [x_sample[seq, 256 * r:256 * (r + 1)]], 0)
        n = 256 * r + np.arange(256)
        sel = np.zeros((128, 2), np.float32)
        sel[:, seq] = 1.0
        xco = np.zeros((2, 2, 5), np.float32)
        for s_ in range(4):
            if s_ < r:
                xco[0, 0, s_] = 256.0 * (r - 1 - s_); xco[1, 0, s_] = 1.0
            if s_ > r:
                xco[0, 1, s_] = 256.0 * (s_ - r - 1); xco[1, 1, s_] = 1.0
        xco[0, 0, 4] = 256.0 * r; xco[1, 0, 4] = 1.0
        xco[0, 1, 4] = 256.0 * (3 - r); xco[1, 1, 4] = 1.0
        m = dict(shared)
        m.update({
            "xT": f(xs.T),
            "w_mod_sh": f(w_mod[:, :, 1536 * r:1536 * (r + 1)]),
            "b_mod_sh": f(b_mod[:, 1536 * r:1536 * (r + 1)].reshape(2, 12, 128).transpose(2, 1, 0)),
            "condT": condT, "sel": sel, "gnT": gnT, "gqaT": gqaT, "gsm": gsm, "retp": retp,
            "c_ckv": f(inp["cache_ckv"])[seq], "c_kr": f(inp["cache_krope"])[seq],
            "s0": f(np.stack([f(inp["state_ret_fwd"])[seq], f(inp["state_ret_bwd"])[seq]], 1)),
            "tabs": tabs, "idxt": idxt, "idxq": idxq,
            "rope64": f(_rope_tab(n // 64, n % 64, 16).reshape(2, 128, 2, 2, 16).transpose(1, 0, 2, 3, 4)),
            "rope32": f(_rope_tab(n // 64, n % 64, 8).reshape(2, 128, 2, 2, 8).transpose(1, 0, 2, 3, 4)),
            "xcoef": f(np.broadcast_to(xco[None], (128, 2, 2, 5))),
        })
        maps.append(m)
    return maps


_CACHE = {}


def run(inp, stop=99, dbg=()):
    key = (stop, tuple(dbg))
    if key not in _CACHE:
        _CACHE[key] = build_program(stop, dbg)
    nc, dbg_out = _CACHE[key]
    maps = make_inputs(inp)
    res = run_bass_kernel_spmd(nc, maps, core_ids=list(range(NCORES)))
    return res.results


def kernel(**inp):
    R = run(inp)
    y_prompt = np.zeros((32, 256, D), np.float32)
    y_sample = np.zeros((2, 1024, D), np.float32)
    new_ckv = np.zeros((32, 2, 256, 128), np.float32)
    new_kr = np.zeros((32, 2, 256, 32), np.float32)
    new_sf = np.zeros((32, 2, 8, 64, 64), np.float32)
    new_sb = np.zeros((32, 2, 8, 64, 64), np.float32)
    for core in range(NCORES):
        r, seq = core % 4, core // 4
        o = R[core]
        y = np.asarray(o["yT"]).T
        y_prompt[4 * core:4 * core + 4] = y[0:1024].reshape(4, 256, D)
        y_sample[seq, 256 * r:256 * (r + 1)] = y[1024:1280]
        new_ckv[4 * core:4 * core + 4] = np.asarray(o["o_ckv"]).reshape(2, 4, 256, 128).transpose(1, 0, 2, 3)
        new_kr[4 * core:4 * core + 4] = np.asarray(o["o_kr"]).reshape(2, 4, 256, 32).transpose(1, 0, 2, 3)
        new_sf[4 * core:4 * core + 4] = np.asarray(o["o_sf"]).transpose(1, 0, 2, 3, 4)
        new_sb[4 * core:4 * core + 4] = np.asarray(o["o_sb"]).transpose(1, 0, 2, 3, 4)
    return (y_prompt, y_sample, new_ckv, new_kr, new_sf, new_sb)
```
